# Optimizing a Trainium2 kernel written in Bass

```python
import math
import numpy as np
import jax, jax.numpy as jnp
from jax import lax

D_MODEL = 1024
BATCH = 32
SEQ = 2048
DEPTH = 2
DEC_BATCH = 2
DEC_SEQ = 8192
PAST_LEN = 128

GRID_W = 64
NA_HEADS = 8
NA_HEAD_DIM = 64
NA_WIDTH = NA_HEADS * NA_HEAD_DIM
NA_WIN_ROWS_MAX = 8
NA_WIN_COLS = 16
CONV_CH = 512
CONV_WIDTH = 31
DIFF_HEADS = 4
DIFF_HEAD_DIM = 64
DIFF_WIDTH = DIFF_HEADS * 2 * DIFF_HEAD_DIM
N_BRANCH = 3
BRANCH_W = 512
T5_BUCKETS = 32
T5_MAX_DIST = 128
Q_BLOCK = 128
FFN_HIDDEN = -(-8 * D_MODEL // (3 * 256)) * 256
EPS = 1e-6

NA_COLS = 3 * NA_WIDTH
CONV_COLS = 2 * CONV_CH
DIFF_COLS = 3 * DIFF_WIDTH
GATE_COLS = N_BRANCH * D_MODEL
OFF_CONV = NA_COLS
OFF_DIFF = OFF_CONV + CONV_COLS
OFF_GATE = OFF_DIFF + DIFF_COLS
IN_COLS = OFF_GATE + GATE_COLS

kernel_name = "hybrid_natten_conformer_diffattn_encoder"


def rms_norm(x, g):
    xf = x.astype(jnp.float32)
    y = xf * lax.rsqrt(jnp.mean(xf * xf, axis=-1, keepdims=True) + EPS)
    return (y * g.astype(jnp.float32)).astype(x.dtype)


def layer_norm(x, g, b):
    xf = x.astype(jnp.float32)
    mu = jnp.mean(xf, axis=-1, keepdims=True)
    xc = xf - mu
    y = xc * lax.rsqrt(jnp.mean(xc * xc, axis=-1, keepdims=True) + EPS)
    return (y * g.astype(jnp.float32) + b.astype(jnp.float32)).astype(x.dtype)


def t5_bucket(rel):
    nb = T5_BUCKETS // 2
    max_exact = nb // 2
    ret = jnp.where(rel > 0, nb, 0)
    n = jnp.abs(rel)
    nf = jnp.maximum(n, 1).astype(jnp.float32)
    large = max_exact + (jnp.log(nf / max_exact) / math.log(T5_MAX_DIST / max_exact)
                         * (nb - max_exact)).astype(jnp.int32)
    large = jnp.minimum(large, nb - 1)
    return ret + jnp.where(n < max_exact, n, large)


def neighbourhood_attention(q, k, v, rpb):
    B, S, H, dh = q.shape
    rows = S // GRID_W
    kr = min(NA_WIN_ROWS_MAX, rows)
    kw = NA_WIN_COLS
    qg = q.reshape(B, rows, GRID_W, H, dh)
    kg = k.reshape(B, rows, GRID_W, H, dh)
    vg = v.reshape(B, rows, GRID_W, H, dh)
    col = np.arange(GRID_W)
    col_start = np.clip(col - kw // 2, 0, GRID_W - kw)
    col_idx = col_start[:, None] + np.arange(kw)[None, :]
    col_off = col_idx - col[:, None] + (NA_WIN_COLS - 1)
    col_bias = rpb[:, :, col_off]
    scale = dh ** -0.5

    def one_row(args):
        r, q_row = args
        rs = jnp.clip(r - kr // 2, 0, rows - kr)
        kb = lax.dynamic_slice_in_dim(kg, rs, kr, axis=1)[:, :, col_idx]
        vb = lax.dynamic_slice_in_dim(vg, rs, kr, axis=1)[:, :, col_idx]
        row_off = rs + jnp.arange(kr) - r + (NA_WIN_ROWS_MAX - 1)
        bias = jnp.take(col_bias, row_off, axis=1).transpose(0, 2, 1, 3)
        s = (jnp.einsum('bwhd,bawkhd->bhwak', q_row, kb).astype(jnp.float32) * scale
             + bias[None].astype(jnp.float32))
        p = jax.nn.softmax(s.reshape(B, H, GRID_W, kr * kw), axis=-1)
        p = p.reshape(s.shape).astype(v.dtype)
        return jnp.einsum('bhwak,bawkhd->bwhd', p, vb)

    out = lax.map(one_row, (jnp.arange(rows), qg.transpose(1, 0, 2, 3, 4)))
    return out.transpose(1, 0, 2, 3, 4).reshape(B, S, H * dh)


def conformer_conv(u, dw_w, dw_b, ln_g, ln_b):
    a, g = jnp.split(u, 2, axis=-1)
    x = a * jax.nn.sigmoid(g)
    x = lax.conv_general_dilated(
        x, dw_w[:, None, :], window_strides=(1,),
        padding=[(CONV_WIDTH // 2, CONV_WIDTH // 2)],
        dimension_numbers=('NWC', 'WIO', 'NWC'),
        feature_group_count=CONV_CH) + dw_b
    x = layer_norm(x, ln_g, ln_b)
    return jax.nn.silu(x)


def diff_attention(q, k, v, t5_table, lam, subln_g, lam_init):
    B, S, H, _, dh = q.shape
    lf = lam.astype(jnp.float32)
    lam_full = jnp.exp(jnp.sum(lf[0] * lf[1])) - jnp.exp(jnp.sum(lf[2] * lf[3])) + lam_init
    nblk = S // Q_BLOCK
    qb = q.reshape(B, nblk, Q_BLOCK, H, 2, dh).transpose(1, 0, 2, 3, 4, 5)
    kpos = jnp.arange(S)
    scale = dh ** -0.5

    def one_block(args):
        i, q_blk = args
        qpos = i * Q_BLOCK + jnp.arange(Q_BLOCK)
        bias = t5_table[t5_bucket(kpos[None, :] - qpos[:, None])]
        bias = bias.transpose(2, 0, 1).astype(jnp.float32)
        s = jnp.einsum('bqhcd,bkhcd->cbhqk', q_blk, k).astype(jnp.float32) * scale + bias
        p = jax.nn.softmax(s, axis=-1)
        a = (p[0] - lam_full * p[1]).astype(v.dtype)
        return jnp.einsum('bhqk,bkhe->bqhe', a, v)

    o = lax.map(one_block, (jnp.arange(nblk), qb))
    o = o.transpose(1, 0, 2, 3, 4).reshape(B, S, H, 2 * dh)
    o = rms_norm(o, subln_g) * (1.0 - lam_init)
    return o.reshape(B, S, H * 2 * dh)


def trunk(x, w_in, b_gate, na_rpb, conv_dw_w, conv_dw_b, conv_ln_g, conv_ln_b,
          diff_lambda, diff_subln_g, t5_bias, w_branch, w_out,
          ln_mix_pre, ln_mix_post, ln_ffn_pre, ln_ffn_post, w_ffn_in, w_ffn_out):
    B, S, _ = x.shape
    for l in range(DEPTH):
        lam_init = 0.8 - 0.6 * math.exp(-0.3 * l)
        h = rms_norm(x, ln_mix_pre[l])
        u = h @ w_in[l]
        na_q, na_k, na_v = jnp.split(u[..., :NA_COLS].reshape(B, S, 3, NA_HEADS, NA_HEAD_DIM), 3, axis=2)
        o_na = neighbourhood_attention(na_q[:, :, 0], na_k[:, :, 0], na_v[:, :, 0], na_rpb[l])
        o_cv = conformer_conv(u[..., OFF_CONV:OFF_DIFF], conv_dw_w[l], conv_dw_b[l], conv_ln_g[l], conv_ln_b[l])
        du = u[..., OFF_DIFF:OFF_GATE]
        d_q = du[..., :DIFF_WIDTH].reshape(B, S, DIFF_HEADS, 2, DIFF_HEAD_DIM)
        d_k = du[..., DIFF_WIDTH:2 * DIFF_WIDTH].reshape(B, S, DIFF_HEADS, 2, DIFF_HEAD_DIM)
        d_v = du[..., 2 * DIFF_WIDTH:].reshape(B, S, DIFF_HEADS, 2 * DIFF_HEAD_DIM)
        o_df = diff_attention(d_q, d_k, d_v, t5_bias, diff_lambda[l], diff_subln_g[l], lam_init)
        gates = jax.nn.sigmoid((u[..., OFF_GATE:] + b_gate[l]).reshape(B, S, N_BRANCH, D_MODEL))
        br = jnp.stack([o_na, o_cv, o_df], axis=2)
        proj = jnp.einsum('bsnc,ncd->bsnd', br, w_branch[l])
        merged = jnp.sum(gates * proj, axis=2)
        x = x + rms_norm(merged @ w_out[l], ln_mix_post[l])
        hf = rms_norm(x, ln_ffn_pre[l])
        gu = hf @ w_ffn_in[l]
        g, up = jnp.split(gu, 2, axis=-1)
        x = x + rms_norm((jax.nn.silu(g) * up) @ w_ffn_out[l], ln_ffn_post[l])
    return x


def setup_inputs(seed: int = 0) -> dict:
    key = jax.random.key(seed)
    ks = jax.random.split(key, 20)
    nrm = lambda k, shape, s: jax.random.normal(k, shape, jnp.float32) * s
    gain = lambda k, shape: 1.0 + 0.05 * jax.random.normal(k, shape, jnp.float32)
    return {
        "x_prompt": nrm(ks[0], (BATCH, SEQ, D_MODEL), 1.0),
        "x_sample": nrm(ks[1], (DEC_BATCH, DEC_SEQ, D_MODEL), 1.0),
        "w_in": nrm(ks[2], (DEPTH, D_MODEL, IN_COLS), D_MODEL ** -0.5),
        "b_gate": nrm(ks[3], (DEPTH, GATE_COLS), 0.1),
        "na_rpb": nrm(ks[4], (DEPTH, NA_HEADS, 2 * NA_WIN_ROWS_MAX - 1, 2 * NA_WIN_COLS - 1), 0.2),
        "conv_dw_w": nrm(ks[5], (DEPTH, CONV_WIDTH, CONV_CH), CONV_WIDTH ** -0.5),
        "conv_dw_b": nrm(ks[6], (DEPTH, CONV_CH), 0.02),
        "conv_ln_g": gain(ks[7], (DEPTH, CONV_CH)),
        "conv_ln_b": nrm(ks[8], (DEPTH, CONV_CH), 0.02),
        "diff_lambda": nrm(ks[9], (DEPTH, 4, DIFF_HEAD_DIM), 0.1),
        "diff_subln_g": gain(ks[10], (DEPTH, 2 * DIFF_HEAD_DIM)),
        "t5_bias": nrm(ks[11], (T5_BUCKETS, DIFF_HEADS), 0.2),
        "w_branch": nrm(ks[12], (DEPTH, N_BRANCH, BRANCH_W, D_MODEL), BRANCH_W ** -0.5),
        "w_out": nrm(ks[13], (DEPTH, D_MODEL, D_MODEL), D_MODEL ** -0.5),
        "ln_mix_pre": gain(ks[14], (DEPTH, D_MODEL)),
        "ln_mix_post": gain(ks[15], (DEPTH, D_MODEL)),
        "ln_ffn_pre": gain(ks[16], (DEPTH, D_MODEL)),
        "ln_ffn_post": gain(ks[17], (DEPTH, D_MODEL)),
        "w_ffn_in": nrm(ks[18], (DEPTH, D_MODEL, 2 * FFN_HIDDEN), D_MODEL ** -0.5),
        "w_ffn_out": nrm(ks[19], (DEPTH, FFN_HIDDEN, D_MODEL), FFN_HIDDEN ** -0.5),
    }


def reference(x_prompt, x_sample, w_in, b_gate, na_rpb, conv_dw_w, conv_dw_b, conv_ln_g, conv_ln_b,
              diff_lambda, diff_subln_g, t5_bias, w_branch, w_out,
              ln_mix_pre, ln_mix_post, ln_ffn_pre, ln_ffn_post, w_ffn_in, w_ffn_out):
    y_prompt = trunk(x_prompt, w_in, b_gate, na_rpb, conv_dw_w, conv_dw_b, conv_ln_g, conv_ln_b,
                     diff_lambda, diff_subln_g, t5_bias, w_branch, w_out,
                     ln_mix_pre, ln_mix_post, ln_ffn_pre, ln_ffn_post, w_ffn_in, w_ffn_out)
    y_sample = trunk(x_sample, w_in, b_gate, na_rpb, conv_dw_w, conv_dw_b, conv_ln_g, conv_ln_b,
                     diff_lambda, diff_subln_g, t5_bias, w_branch, w_out,
                     ln_mix_pre, ln_mix_post, ln_ffn_pre, ln_ffn_post, w_ffn_in, w_ffn_out)
    return (y_prompt, y_sample)
```

```python
import math
from contextlib import ExitStack

import numpy as np
import concourse.bass as bass
import concourse.mybir as mybir
from concourse.bass_utils import run_bass_kernel_spmd

F32 = mybir.dt.float32
BF16 = mybir.dt.bfloat16
AF = mybir.ActivationFunctionType
ALU = mybir.AluOpType

D = 1024
DEPTH = 2
U = 2048
NT = 16
IN_COLS = 7168
OFF_CONV = 1536
OFF_DIFF = 2560
OFF_GATE = 4096
FFN_H = 2816
NHC = 22
EPS = 1e-6
NEG = -30000.0
SCALE = 0.125
NPC = 177
PC_GPRE, PC_GFFN, PC_BGATE, PC_CW, PC_CB, PC_LNG, PC_LNB, PC_SUB = 0, 8, 16, 40, 164, 168, 172, 176
NAB_GRP = {"b0": (0, list(range(-2, 4))), "b1": (6, list(range(-2, 3))), "int": (11, list(range(-2, 3))),
           "b14": (16, list(range(-2, 3))), "b15": (21, list(range(-3, 3)))}
CH = 8000
KQ = 8


def _grp_of_block(i):
    return {0: "b0", 1: "b1", 14: "b14", 15: "b15"}.get(i, "int")


class _Op:
    __slots__ = ("eng", "fn", "deps", "kind", "sig", "needs_sig")

    def __init__(self, eng, fn, deps, kind):
        self.eng, self.fn, self.deps, self.kind = eng, fn, deps, kind
        self.sig = None
        self.needs_sig = kind != "c"


class Prog:
    def __init__(self):
        self.marks = []
        self.ops = []
        self.last_w = {}
        self.readers = {}
        self.reg_cur = {}
        self.reg_prev = {}

    def mark(self, label):
        cnt = {}
        for op in self.ops:
            cnt[op.eng] = cnt.get(op.eng, 0) + 1
        self.marks.append((label, cnt))

    def new_epoch(self, R):
        cur = self.reg_cur.pop(R, ({}, []))
        self.reg_prev[R] = list(cur[0].values()) + cur[1]

    def add(self, eng, fn, reads=(), writes=(), kind="c", regions=()):
        i = len(self.ops)
        deps = set()
        for R in regions:
            deps.update(self.reg_prev.get(R, ()))
            cur = self.reg_cur.setdefault(R, ({}, []))
            if kind == "d":
                cur[1].append(i)
            else:
                cur[0][eng] = i
        lw, rd = self.last_w, self.readers
        for k in reads:
            w = lw.get(k)
            if w is not None:
                deps.add(w)
        for k in writes:
            w = lw.get(k)
            if w is not None:
                deps.add(w)
            r = rd.get(k)
            if r:
                deps.update(r[0].values())
                deps.update(r[1])
        for k in reads:
            r = rd.get(k)
            if r is None:
                r = rd[k] = ({}, [])
            if kind == "d":
                r[1].append(i)
            else:
                r[0][eng] = i
        for k in writes:
            lw[k] = i
            rd[k] = ({}, [])
        deps.discard(i)
        self.ops.append(_Op(eng, fn, deps, kind))
        return i

    def finalize(self):
        ops = self.ops
        for op in ops:
            nd = []
            for d in op.deps:
                a = ops[d]
                if a.eng == "tensor" and op.eng == "tensor" and a.kind == "c" and op.kind == "c":
                    continue
                a.needs_sig = True
                nd.append(d)
            op.deps = nd
        self.ccount = {}
        self.dcount = {}
        self.ncc = 0
        for op in ops:
            if op.kind == "d":
                n = self.dcount.get(op.eng, 0)
                self.dcount[op.eng] = n + 1
                op.sig = ("d", op.eng, n)
            elif op.kind == "cc":
                op.sig = ("cc", self.ncc)
                self.ncc += 1
            elif op.needs_sig:
                k = self.ccount.get(op.eng, 0) + 1
                self.ccount[op.eng] = k
                op.sig = ("c", op.eng, k)
        last = {}
        for i, op in enumerate(ops):
            if op.kind == "d":
                lst = last.setdefault(op.eng, [])
                if len(lst) >= KQ:
                    op.deps.append(lst[-KQ])
                lst.append(i)

    def alloc_sems(self, nc, es):
        self.csem = {e: [es.enter_context(nc.semaphore(f"c_{e}_{j}")) for j in range((k + CH - 1) // CH)]
                     for e, k in self.ccount.items()}
        self.dsem = {e: [es.enter_context(nc.semaphore(f"d_{e}_{j}")) for j in range(KQ)] for e in self.dcount}
        self.ccsem = [es.enter_context(nc.semaphore(f"cc_{j}")) for j in range(self.ncc)]

    def _resolve(self, sig):
        if sig[0] == "c":
            k = sig[2]
            return self.csem[sig[1]][(k - 1) // CH], (k - 1) % CH + 1, 1
        if sig[0] == "d":
            n = sig[2]
            return self.dsem[sig[1]][n % KQ], 16 * (n // KQ + 1), 16
        return self.ccsem[sig[1]], 1, None

    def emit(self, engname, e, final_wait=False):
        waited = {}
        ops = self.ops
        for op in ops:
            if op.eng != engname:
                continue
            for d in op.deps:
                sem, val, _ = self._resolve(ops[d].sig)
                key = id(sem)
                if waited.get(key, 0) < val:
                    e.wait_ge(sem, val)
                    waited[key] = val
            try:
                ins = op.fn(e)
            except Exception:
                import inspect
                cv = inspect.getclosurevars(op.fn)
                print("EMIT FAIL", engname, op.kind, {k: (str(v)[:200]) for k, v in cv.nonlocals.items()}, flush=True)
                raise
            if op.sig is not None:
                sem, val, inc = self._resolve(op.sig)
                if inc is None:
                    ins.then_inc(sem)
                else:
                    ins.then_inc(sem, inc)
        if final_wait:
            for q, n in self.dcount.items():
                for j in range(min(KQ, n)):
                    cnt = (n - 1 - j) // KQ + 1
                    e.wait_ge(self.dsem[q][j], 16 * cnt)


def _t5_bucket(rel):
    nb, max_exact = 16, 8
    ret = np.where(rel > 0, nb, 0)
    n = np.abs(rel)
    nf = np.maximum(n, 1).astype(np.float32)
    large = max_exact + (np.log(nf / np.float32(max_exact)) / np.float32(math.log(128 / max_exact))
                         * np.float32(nb - max_exact)).astype(np.int32)
    large = np.minimum(large, nb - 1)
    return ret + np.where(n < max_exact, n, large)


def _na_tile(rpb_h, R, gi, gj):
    if gj < 0 or gj >= R // 2 or gi < 0 or gi >= R // 2:
        return np.full((128, 128), NEG, np.float32)
    kp = np.arange(128)
    krow, kcol = 2 * gj + kp // 64, kp % 64
    qrow, qcol = 2 * gi + kp // 64, kp % 64
    rs = np.clip(qrow - 4, 0, R - 8)
    cs = np.clip(qcol - 8, 0, 48)
    valid = ((krow[:, None] >= rs[None, :]) & (krow[:, None] < rs[None, :] + 8)
             & (kcol[:, None] >= cs[None, :]) & (kcol[:, None] < cs[None, :] + 16))
    val = rpb_h[np.clip(krow[:, None] - qrow[None, :] + 7, 0, 14), np.clip(kcol[:, None] - qcol[None, :] + 15, 0, 30)]
    return np.where(valid, val, np.float32(NEG)).astype(np.float32)


def _nab_table(rpb, R, base_block):
    L = rpb.shape[0]
    out = np.empty((L, 8, 128, 27, 128), np.float32)
    loc = {"b0": 0, "b1": 1, "int": 5, "b14": 14, "b15": 15}
    for l in range(L):
        for h in range(8):
            for g, (off, jrels) in NAB_GRP.items():
                gi = base_block + loc[g]
                for s, jr in enumerate(jrels):
                    out[l, h, :, off + s, :] = _na_tile(rpb[l, h], R, gi, gi + jr)
    return out


def _pack_pcol(inp, l):
    pc = np.zeros((128, NPC), np.float32)
    pc[:, PC_GPRE:PC_GPRE + 8] = inp["ln_mix_pre"][l].reshape(8, 128).T
    pc[:, PC_GFFN:PC_GFFN + 8] = inp["ln_ffn_pre"][l].reshape(8, 128).T
    pc[:, PC_BGATE:PC_BGATE + 24] = inp["b_gate"][l].reshape(24, 128).T
    pc[:, PC_CW:PC_CW + 124] = inp["conv_dw_w"][l].reshape(31, 4, 128).transpose(2, 1, 0).reshape(128, 124)
    pc[:, PC_CB:PC_CB + 4] = inp["conv_dw_b"][l].reshape(4, 128).T
    pc[:, PC_LNG:PC_LNG + 4] = inp["conv_ln_g"][l].reshape(4, 128).T
    pc[:, PC_LNB:PC_LNB + 4] = inp["conv_ln_b"][l].reshape(4, 128).T
    pc[:, PC_SUB] = inp["diff_subln_g"][l]
    return pc


def build_program(n_prompt=4, depth=DEPTH, with_sample=True, taps=()):
    nc = bass.Bass("TRN2", target_bir_lowering=False)
    NU = 1 + n_prompt
    P = Prog()
    dt = nc.dram_tensor

    xin = dt("xin", [NU, U, D], F32, kind="ExternalInput").ap()
    w_in = dt("w_in", [DEPTH, D, IN_COLS], F32, kind="ExternalInput").ap()
    w_branch = dt("w_branch", [DEPTH, 3, 512, D], F32, kind="ExternalInput").ap()
    w_out = dt("w_out", [DEPTH, D, D], F32, kind="ExternalInput").ap()
    w_ffn_in = dt("w_ffn_in", [DEPTH, D, 2 * FFN_H], F32, kind="ExternalInput").ap()
    w_ffn_out = dt("w_ffn_out", [DEPTH, FFN_H, D], F32, kind="ExternalInput").ap()
    pcol_d = dt("pcol", [DEPTH, 128, NPC], F32, kind="ExternalInput").ap()
    prow_d = dt("prow", [DEPTH, 2, D], F32, kind="ExternalInput").ap()
    lam_d = dt("lam", [DEPTH, 256], F32, kind="ExternalInput").ap()
    t5t_d = dt("t5t", [4, 128, 6, 512], F32, kind="ExternalInput").ap()
    t5c_d = dt("t5c", [128, 4, 2], F32, kind="ExternalInput").ap()
    t5cs_d = dt("t5cs", [128, 4, 4], F32, kind="ExternalInput").ap()
    t5j_d = dt("t5j", [4, 128, 2, 512], F32, kind="ExternalInput").ap()
    nabp_d = dt("nabp", [DEPTH, 8, 128, 27, 128], F32, kind="ExternalInput").ap()
    nabs_d = dt("nabs", [DEPTH, 8, 128, 27, 128], F32, kind="ExternalInput").ap()
    cflag_d = dt("cflag", [128, 2], F32, kind="ExternalInput").ap()
    ident_d = dt("ident", [128, 128], F32, kind="ExternalInput").ap()
    yout = dt("yout", [NU, U, D], F32, kind="ExternalOutput").ap()
    tap_d = {}
    for name, shape in taps:
        tap_d[name] = dt("tap_" + name, list(shape), F32, kind="ExternalOutput").ap()

    xmid = dt("xmid", [NU, U, D], F32).ap()
    x1 = dt("x1", [NU, U, D], F32).ap()
    cf = [[dt(f"cf{l}_{s_}", [128, U], BF16).ap() for s_ in range(12)] for l in range(depth)]
    ct = [[dt(f"ct{l}_{j_}", [U, 128], BF16).ap() for j_ in range(8)] for l in range(depth)]
    tfm = [[dt(f"tfm{l}_{s_}", [4 * 128, U], BF16).ap() for s_ in range(12)] for l in range(depth)]
    ttm = [[dt(f"ttm{l}_{j_}", [4 * U, 128], BF16).ap() for j_ in range(8)] for l in range(depth)]

    es = ExitStack()
    with es:
        S = lambda name, shape, dtype: es.enter_context(nc.sbuf_tensor(name, shape, dtype))
        ident = S("ident_s", [128, 128], BF16)
        identf = S("identf", [128, 128], F32)
        ones_bf = S("ones_bf", [128, 128], BF16)
        ones_f = S("ones_f", [128, 128], F32)
        o512_f = S("o512_f", [128, 128], F32)
        epsc = S("epsc", [128, 1], F32)
        hT = S("hT", [128, 8, U], BF16)
        brT = S("brT", [128, 12 * U], BF16)
        arena = S("arena", [128, 22528], BF16)
        wbuf = [S(f"wbuf{i}", [128, 4608], BF16) for i in range(2)]
        xt = [S(f"xt{i}", [128, D], F32) for i in range(3)]
        hb = [S(f"hb{i}", [128, D], BF16) for i in range(2)]
        pT = [S(f"pT{i}", [128, 768], BF16) for i in range(4)]
        tmp = [S(f"tmp{i}", [128, 768], F32) for i in range(4)]
        ytmp = S("ytmp", [128, D], F32)
        lamb = ytmp[:, 0:256]
        biasbuf = S("biasbuf", [128, 4096], F32)
        gpost = S("gpost", [128, 2, D], F32)
        pcol = S("pcol_s", [128, NPC], F32)
        small = S("small", [128, 64], F32)
        t5c = S("t5c_s", [128, 4, 2], F32)
        t5cs = S("t5cs_s", [128, 4, 4], F32)
        cflag = S("cflag_s", [128, 2], F32)
        ps = es.enter_context(nc.psum_tensor("ps", [128, 4096], F32))

        rr = {"ps": 0, "x": 0, "hb": 0, "pT": 0, "tmp": 0, "wb": 0, "ev": 0, "col": 0, "psn": 8, "dfs": 0}

        def nxt(name, n):
            v = rr[name]
            rr[name] = (v + 1) % n
            return v

        def bank(b, n=512):
            return ps[:, b * 512:b * 512 + n]

        def nbank():
            return nxt("ps", rr["psn"])

        def nbank2():
            b = rr["ps"]
            if b % 2:
                b = (b + 1) % 8
            b = b % 8
            rr["ps"] = (b + 2) % 8
            return b

        def col():
            c = 8 + nxt("col", 48)
            return small[:, c:c + 1], ("small", c)

        def regs(*aps):
            r = set()
            for a in aps:
                if a is None or isinstance(a, (int, float)):
                    continue
                n = a.name
                if n == "arena":
                    r.add("RA")
                elif n == "brT":
                    r.add("RB")
            return tuple(r)

        def dma(q, out, in_, reads, writes):
            return P.add(q, lambda e, out=out, in_=in_: e.dma_start(out=out, in_=in_), reads, writes, kind="d", regions=regs(out, in_))

        def dma_dyn(out, in_fn, reads, writes):
            return P.add("sync", lambda e: e.dma_start(out=out, in_=in_fn(e)), reads, writes, kind="d", regions=regs(out))

        def mm(out, lhsT, rhs, start, stop, reads, writes):
            return P.add("tensor", lambda e: e.matmul(out, lhsT=lhsT, rhs=rhs, start=start, stop=stop), reads, writes,
                         regions=regs(lhsT, rhs))

        def tr(out, in_, reads, writes):
            return P.add("tensor", lambda e: e.transpose(out=out, in_=in_, identity=ident[:]), reads + [("ident",)], writes,
                         regions=regs(in_))

        def act(out, in_, func, reads, writes, **kw):
            return P.add("scalar", lambda e: e.activation(out=out, in_=in_, func=func, **kw), reads, writes, regions=regs(out, in_))

        def amul(out, in_, mul, reads, writes):
            return P.add("scalar", lambda e: e.mul(out=out, in_=in_, mul=mul), reads, writes, regions=regs(out, in_))

        def tt(out, in0, in1, op, reads, writes, eng="vector"):
            return P.add(eng, lambda e: e.tensor_tensor(out=out, in0=in0, in1=in1, op=op), reads, writes, regions=regs(out, in0, in1))

        def tsc(out, in0, s1, s2, op0, op1, reads, writes, eng="vector"):
            if op1 is None:
                return P.add(eng, lambda e: e.tensor_scalar(out=out, in0=in0, scalar1=s1, scalar2=None, op0=op0), reads, writes,
                             regions=regs(out, in0))
            return P.add(eng, lambda e: e.tensor_scalar(out=out, in0=in0, scalar1=s1, scalar2=s2, op0=op0, op1=op1), reads, writes,
                         regions=regs(out, in0))

        def stt(out, in0, scalar, in1, op0, op1, reads, writes, eng="vector"):
            return P.add(eng, lambda e: e.scalar_tensor_tensor(out=out, in0=in0, scalar=scalar, in1=in1, op0=op0, op1=op1), reads, writes,
                         regions=regs(out, in0, in1))

        def cpy(out, in_, reads, writes, eng="vector"):
            return P.add(eng, lambda e: e.tensor_copy(out=out, in_=in_), reads, writes, regions=regs(out, in_))

        def memset(ap, val, writes, eng="vector"):
            return P.add(eng, lambda e: e.memset(ap, val), [], writes, regions=regs(ap))

        def recip(out, in_, reads, writes):
            return P.add("vector", lambda e: e.reciprocal(out=out, in_=in_), reads, writes, regions=regs(out, in_))

        def fence(eng, old, new):
            raise RuntimeError("unused")

        def evac(out, in_, reads, writes):
            if nxt("ev", 2) == 0:
                return cpy(out, in_, reads, writes)
            return P.add("scalar", lambda e: e.copy(out=out, in_=in_), reads, writes, regions=regs(out, in_))

        def rsqrt_col(dst, dkey, src, skey, scale):
            act(dst, src, AF.Ln, [skey, ("epsc",)], [dkey], scale=scale, bias=epsc[:, 0:1])
            act(dst, dst, AF.Exp, [dkey], [dkey], scale=-0.5)

        def tap(name, idx, src, reads):
            if name in tap_d:
                dma("sync", tap_d[name][idx], src, reads, [("tap", name, str(idx))])

        def ar(off, n):
            return arena[:, off:off + n]

        AR_ALL = [("ar", i) for i in range(11)]

        def arkeys(off, n):
            return [("ar", i) for i in range(off // 2048, (off + n - 1) // 2048 + 1)]

        dma("sync", identf[:], ident_d[:, :], [], [("identf",)])
        cpy(ident[:], identf[:], [("identf",)], [("ident",)])
        memset(ones_bf[:], 1.0, [("ones_bf",)])
        memset(ones_f[:], 1.0 / 128, [("ones_f",)])
        memset(o512_f[:], 1.0 / 512, [("o512_f",)])
        memset(epsc[:], EPS, [("epsc",)])
        dma("sync", t5c[:], t5c_d[:, :, :], [], [("t5c",)])
        dma("sync", t5cs[:], t5cs_d[:, :, :], [], [("t5cs",)])
        dma("sync", cflag[:], cflag_d[:, :], [], [("cflag",)])

        _pid = {}

        def rank_of(e, ro):
            if id(e) not in _pid:
                _pid[id(e)] = e.partition_id() % 4
            return (_pid[id(e)] + ro) % 4

        def load_w(dst_ap, src_ap, wkeys):
            if len(dst_ap.shape) == 4:
                ax = 2 if dst_ap.shape[2] <= dst_ap.shape[1] else 1
                for i_ in range(dst_ap.shape[ax]):
                    if ax == 2:
                        dma("gpsimd", dst_ap[:, :, i_, :], src_ap[:, :, i_, :], [], wkeys)
                    else:
                        dma("gpsimd", dst_ap[:, i_, :, :], src_ap[:, i_, :, :], [], wkeys)
                return
            dma("gpsimd", dst_ap, src_ap, [], wkeys)

        def win_cols(l, c0, n):
            return w_in[l].rearrange("(kc p) n -> p kc n", p=128)[:, :, c0:c0 + n]

        def wslot():
            s = nxt("wb", 2)
            return s, [("wb", s)]

        def prenorm_tile(xtile, xkey, t, gcol0):
            ss, sk = col()
            hi = nxt("hb", 2)
            act(hb[hi][:], xtile, AF.Square, [xkey], [sk, ("hb", hi)], accum_out=ss)
            rs, rk = col()
            rsqrt_col(rs, rk, ss, sk, 1.0 / D)
            amul(hb[hi][:], xtile, rs, [xkey, rk], [("hb", hi)])
            b = nbank()
            pb = bank(b).bitcast(BF16).rearrange("p (a b) -> p a b", a=8)
            for k in range(8):
                tr(pb[:, k, :], hb[hi][:, k * 128:(k + 1) * 128], [("hb", hi)], [("ps", b)])
            tt(hT[:, :, t * 128:(t + 1) * 128], pb,
               pcol[:, gcol0:gcol0 + 8].unsqueeze(2).broadcast_to([128, 8, 128]), ALU.mult,
               [("ps", b), ("pcol",)], [("hT", t)])

        def phase_a(xsrc, u, l):
            for t in range(NT):
                xi = nxt("x", 3)
                dma("sync", xt[xi][:], xsrc[u, t * 128:(t + 1) * 128, :], [("xd", l, u, t)], [("xt", xi)])
                prenorm_tile(xt[xi][:], ("xt", xi), t, PC_GPRE)

        def proj_fm(dst_fn, wk_fn, nk, rhs_fn, rkeys_fn, dkeys_fn, post=None):
            for tg in range(4):
                b = nbank()
                for k in range(nk):
                    mm(bank(b), wk_fn(k), rhs_fn(k, tg), k == 0, k == nk - 1, rkeys_fn(tg), [("ps", b)])
                if post is None:
                    evac(dst_fn(tg), bank(b), [("ps", b)], dkeys_fn(tg))
                else:
                    post(tg, b)

        hT_rhs = lambda k, tg: hT[:, k, tg * 512:(tg + 1) * 512]
        hT_keys = lambda tg: [("hT", t) for t in range(tg * 4, tg * 4 + 4)]

        def sample_prepass(xsrc, l):
            phase_a(xsrc, 0, l)
            P.new_epoch("RA")
            for sec, c0 in ((0, OFF_DIFF + 512), (1, 512)):
                s, wk = wslot()
                wv = wbuf[s][:, 0:4096].rearrange("p (k n) -> p k n", k=8)
                load_w(wv, win_cols(l, c0, 512), wk)
                for c in range(4):
                    st = ar((c % 2) * 2048, 2048)
                    skeys = lambda tg, c=c: [("ar", c % 2, "s", tg)]
                    proj_fm(lambda tg, st=st: st[:, tg * 512:(tg + 1) * 512], lambda k, c=c: wv[:, k, c * 128:(c + 1) * 128], 8,
                            hT_rhs, lambda tg: hT_keys(tg) + wk, skeys)
                    dma("sync", cf[l][sec * 4 + c], st,
                        [("ar", c % 2, "s", tg) for tg in range(4)], [("cf", l, sec * 4 + c)])
            sa, wka = wslot()
            wa = wbuf[sa][:, 0:4096].rearrange("p (k n) -> p k n", k=8)
            load_w(wa, win_cols(l, OFF_CONV, 512), wka)
            sg_, wkg = wslot()
            wg = wbuf[sg_][:, 0:4096].rearrange("p (k n) -> p k n", k=8)
            load_w(wg, win_cols(l, OFF_CONV + 512, 512), wkg)
            for c in range(4):
                st = ar((c % 2) * 2048, 2048)
                for tg in range(4):
                    glu_group(lambda k: wa[:, k, c * 128:(c + 1) * 128], lambda k: wg[:, k, c * 128:(c + 1) * 128],
                              wka + wkg, tg, st[:, tg * 512:(tg + 1) * 512], [("ar", c % 2, "s", tg)])
                dma("sync", cf[l][8 + c], st,
                    [("ar", c % 2, "s", tg) for tg in range(4)], [("cf", l, 8 + c)])
            sv, wkv = wslot()
            wv1 = wbuf[sv][:, 0:4096].rearrange("p (k n) -> p k n", k=8)
            load_w(wv1, win_cols(l, OFF_DIFF + 1024, 512), wkv)
            sn, wkn = wslot()
            wv2 = wbuf[sn][:, 0:4096].rearrange("p (k n) -> p k n", k=8)
            load_w(wv2, win_cols(l, 1024, 512), wkn)
            stg = ar(4096, 16384).rearrange("p (t n) -> p t n", t=16)
            for t in range(NT):
                sk = [("ar", "v", t)]
                b = nbank()
                for k in range(8):
                    mm(bank(b), hT[:, k, t * 128:(t + 1) * 128], wv1[:, k, :], k == 0, k == 7, [("hT", t)] + wkv, [("ps", b)])
                evac(stg[:, t, 0:512], bank(b), [("ps", b)], sk)
                b = nbank()
                for k in range(8):
                    mm(bank(b), hT[:, k, t * 128:(t + 1) * 128], wv2[:, k, :], k == 0, k == 7, [("hT", t)] + wkn, [("ps", b)])
                evac(stg[:, t, 512:1024], bank(b), [("ps", b)], sk)
            for j_ in range(8):
                dma("sync", ct[l][j_].rearrange("(t p) n -> p t n", p=128), stg[:, :, j_ * 128:(j_ + 1) * 128],
                    [("ar", "v", t) for t in range(NT)], [("ct", l, j_)])
            rg = [[0, 1, 2, 3], [4, 5, 6, 7]]
            for s_ in range(12):
                P.add("gpsimd", lambda e, s_=s_: e.collective_compute("AllGather", ALU.bypass, replica_groups=rg,
                                                                      ins=[cf[l][s_].opt()], outs=[tfm[l][s_].opt()]),
                      [("cf", l, s_)], [("tfm", l, s_)], kind="cc")
            for j_ in range(8):
                P.add("gpsimd", lambda e, j_=j_: e.collective_compute("AllGather", ALU.bypass, replica_groups=rg,
                                                                      ins=[ct[l][j_].opt()], outs=[ttm[l][j_].opt()]),
                      [("ct", l, j_)], [("ttm", l, j_)], kind="cc")

        def tfv(l, s_):
            return tfm[l][s_].rearrange("(r m) n -> r m n", r=4)

        def ttv(l, j_):
            return ttm[l][j_].rearrange("(r t p) n -> r p t n", r=4, p=128)

        def glu_group(wa_fn, wg_fn, wkeys, tg, dst, dkeys):
            ba = nbank()
            for k in range(8):
                mm(bank(ba), wa_fn(k), hT_rhs(k, tg), k == 0, k == 7, hT_keys(tg) + wkeys, [("ps", ba)])
            bg = nbank()
            for k in range(8):
                mm(bank(bg), wg_fn(k), hT_rhs(k, tg), k == 0, k == 7, hT_keys(tg) + wkeys, [("ps", bg)])
            ti = nxt("tmp", 4)
            act(tmp[ti][:, 0:512], bank(bg), AF.Sigmoid, [("ps", bg)], [("tmp", ti)])
            tt(dst, bank(ba), tmp[ti][:, 0:512], ALU.mult, [("ps", ba), ("tmp", ti)], dkeys)

        def na_phase(u, l, is_sample):
            P.new_epoch("RA")
            P.new_epoch("RB")
            nab_src = nabs_d if is_sample else nabp_d
            for c in range(4):
                base = (c % 2) * 7232
                qT = ar(base, 2048)
                kT = ar(base + 2048, 2560)
                va = ar(base + 4608, 2600).rearrange("p (t h e) -> p t h e", t=20, h=2)
                otm = ar(14464 + (c % 2) * 2048, 2048).rearrange("p (i f) -> p i f", i=16)
                kq, kk, kv, ko = ("na", c % 2, "q"), ("na", c % 2, "k"), ("na", c % 2, "v"), ("na", c % 2, "o")
                s, wk = wslot()
                wv = wbuf[s][:, 0:3072].rearrange("p (k b n) -> p k b n", k=8, b=3)
                nb_w = 1 if is_sample else 3
                wsrc = w_in[l].rearrange("(kc p) (b n) -> p kc b n", p=128, n=512)[:, :, 0:nb_w, c * 128:(c + 1) * 128]
                load_w(wv[:, :, 0:nb_w, :], wsrc, wk)
                proj_fm(lambda tg: qT[:, tg * 512:(tg + 1) * 512], lambda k: wv[:, k, 0, :], 8, hT_rhs,
                        lambda tg: hT_keys(tg) + wk, lambda tg: [kq + (tg,)])
                if not is_sample:
                    proj_fm(lambda tg: kT[:, 256 + tg * 512:256 + (tg + 1) * 512], lambda k: wv[:, k, 1, :], 8, hT_rhs,
                            lambda tg: hT_keys(tg) + wk, lambda tg: [kk + (tg,)])
                    for t4 in range(4):
                        b = nbank()
                        for tl in range(4):
                            t = t4 * 4 + tl
                            for k in range(8):
                                mm(bank(b)[:, tl * 128:(tl + 1) * 128], hT[:, k, t * 128:(t + 1) * 128], wv[:, k, 2, :],
                                   k == 0, k == 7, [("hT", t)] + wk, [("ps", b)])
                        evac(va[:, 2 + t4 * 4:6 + t4 * 4, :, 0:64], bank(b).rearrange("p (t h e) -> p t h e", t=4, h=2),
                             [("ps", b)], [kv + (t4,)])
                    memset(va[:, :, :, 64:65], 1.0, [kv + ("ones",)])
                    kkeys_all = [kk + (tg,) for tg in range(4)]
                    vkeys_all = [kv + (t4,) for t4 in range(4)] + [kv + ("ones",)]
                else:
                    fk, fv = ("tfm", l, 4 + c), ("ttm", l, 4 + c)
                    dma_dyn(kT[:, 256:2304], lambda e, c=c: tfv(l, 4 + c)[rank_of(e, 0)][:, :], [fk], [kk + (0,)])
                    dma_dyn(kT[:, 0:256], lambda e, c=c: tfv(l, 4 + c)[rank_of(e, 3)][:, 1792:2048], [fk], [kk + (1,)])
                    dma_dyn(kT[:, 2304:2560], lambda e, c=c: tfv(l, 4 + c)[rank_of(e, 1)][:, 0:256], [fk], [kk + (2,)])
                    for hh_ in range(2):
                        hs = slice(hh_ * 64, (hh_ + 1) * 64)
                        dma_dyn(va[:, 2:18, hh_, 0:64], lambda e, c=c, hs=hs: ttv(l, 4 + c)[rank_of(e, 0)][:, :, hs], [fv], [kv + (0, hh_)])
                        dma_dyn(va[:, 0:2, hh_, 0:64], lambda e, c=c, hs=hs: ttv(l, 4 + c)[rank_of(e, 3)][:, 14:16, hs], [fv], [kv + (1, hh_)])
                        dma_dyn(va[:, 18:20, hh_, 0:64], lambda e, c=c, hs=hs: ttv(l, 4 + c)[rank_of(e, 1)][:, 0:2, hs], [fv], [kv + (2, hh_)])
                    memset(va[:, :, :, 64:65], 1.0, [kv + ("ones",)])
                    kkeys_all = [kk + (i,) for i in range(3)]
                    vkeys_all = [kv + (i, hh_) for i in range(3) for hh_ in range(2)] + [kv + ("ones",)]
                for hh in range(2):
                    h = 2 * c + hh
                    pb_ = hh * 64
                    dma("sync", biasbuf[:, 0:3456].rearrange("p (s n) -> p s n", s=27), nab_src[l, h], [], [("bias",)])
                    nabv = biasbuf[:, 0:3456]
                    def na_S(i):
                        goff, jrels = NAB_GRP[_grp_of_block(i)]
                        slots = [(s_, jr) for s_, jr in enumerate(jrels) if is_sample or 0 <= i + jr <= 15]
                        ns = len(slots)
                        s0 = slots[0][0]
                        b = nbank2()
                        for n_, (s_, jr) in enumerate(slots):
                            co = 256 + (i + jr) * 128
                            mm(ps[:, b * 512 + n_ * 128: b * 512 + (n_ + 1) * 128], kT[pb_:pb_ + 64, co:co + 128],
                               qT[pb_:pb_ + 64, i * 128:(i + 1) * 128], True, True,
                               kkeys_all + [kq + (i // 4,)], [("ps", b + n_ // 4)])
                        pkeys = [("ps", b)] + ([("ps", b + 1)] if ns > 4 else [])
                        ti = nxt("tmp", 4)
                        stt(tmp[ti][:, 0:ns * 128], ps[:, b * 512:b * 512 + ns * 128], SCALE,
                            nabv[:, (goff + s0) * 128:(goff + s0 + ns) * 128], ALU.mult, ALU.add,
                            pkeys + [("bias",)], [("tmp", ti)])
                        pi = nxt("pT", 4)
                        act(pT[pi][:, 0:ns * 128], tmp[ti][:, 0:ns * 128], AF.Exp, [("tmp", ti)], [("pT", pi)])
                        return (i, slots, pi)

                    def na_PV(ctx):
                        i, slots, pi = ctx
                        ns = len(slots)
                        bo = nbank()
                        for n_, (s_, jr) in enumerate(slots):
                            mm(bank(bo, 65), pT[pi][:, n_ * 128:(n_ + 1) * 128], va[:, 2 + i + jr, hh, :], n_ == 0, n_ == ns - 1,
                               [("pT", pi)] + vkeys_all, [("ps", bo)])
                        rc, rk = col()
                        recip(rc, bank(bo)[:, 64:65], [("ps", bo)], [rk])
                        tsc(otm[:, i, hh * 64:(hh + 1) * 64], bank(bo)[:, 0:64], rc, None, ALU.mult, None,
                            [("ps", bo), rk], [ko + (i, hh)])

                    ctxs = {}
                    for j_ in range(-1, 16):
                        if j_ + 1 < 16:
                            ctxs[j_ + 1] = na_S(j_ + 1)
                        if j_ >= 0:
                            na_PV(ctxs.pop(j_))
                for i4 in range(4):
                    b = nbank()
                    pbv = bank(b).bitcast(BF16)
                    for il in range(4):
                        i = i4 * 4 + il
                        tr(pbv[:, il * 128:(il + 1) * 128], otm[:, i, :], [ko + (i, 0), ko + (i, 1)], [("ps", b)])
                    evac(brT[:, c * U + i4 * 512: c * U + (i4 + 1) * 512], pbv[:, 0:512], [("ps", b)], [("brT", c, i4)])

        def conv_phase(u, l, is_sample):
            P.new_epoch("RA")
            acc4 = arena[:, 0:16384].bitcast(F32).rearrange("p (c n) -> p c n", c=4)
            for c in range(4):
                g = ar(16384 + (c % 2) * 2080, 2078)
                gk = ("cv", "g", c % 2)
                if not is_sample:
                    s, wk = wslot()
                    wv = wbuf[s][:, 0:2048].rearrange("p (k b n) -> p k b n", k=8, b=2)
                    wsrc = w_in[l][:, OFF_CONV:OFF_CONV + 1024].rearrange("(kc p) (b n) -> p kc b n", p=128, n=512)[:, :, :, c * 128:(c + 1) * 128]
                    load_w(wv, wsrc, wk)
                    memset(g[:, 0:15], 0.0, [gk + ("h0",)])
                    memset(g[:, 2063:2078], 0.0, [gk + ("h1",)])
                    for tg in range(4):
                        glu_group(lambda k: wv[:, k, 0, :], lambda k: wv[:, k, 1, :], wk, tg,
                                  g[:, 15 + tg * 512: 15 + (tg + 1) * 512], [gk + (tg,)])
                    gkeys = [gk + (tg,) for tg in range(4)] + [gk + ("h0",), gk + ("h1",)]
                else:
                    fk = ("tfm", l, 8 + c)
                    dma_dyn(g[:, 15:2063], lambda e, c=c: tfv(l, 8 + c)[rank_of(e, 0)][:, :], [fk], [gk + (0,)])
                    dma_dyn(g[:, 0:15], lambda e, c=c: tfv(l, 8 + c)[rank_of(e, 3)][:, 2033:2048], [fk], [gk + ("h0",)])
                    dma_dyn(g[:, 2063:2078], lambda e, c=c: tfv(l, 8 + c)[rank_of(e, 1)][:, 0:15], [fk], [gk + ("h1",)])
                    tsc(g[:, 0:15], g[:, 0:15], cflag[:, 0:1], None, ALU.mult, None, [gk + ("h0",), ("cflag",)], [gk + ("h0",)])
                    tsc(g[:, 2063:2078], g[:, 2063:2078], cflag[:, 1:2], None, ALU.mult, None, [gk + ("h1",), ("cflag",)], [gk + ("h1",)])
                    gkeys = [gk + (0,), gk + ("h0",), gk + ("h1",)]
                ak = ("cv", "acc", c)
                wc = PC_CW + c * 31
                dg = biasbuf[:, :].bitcast(BF16)[:, (c % 2) * 3968:(c % 2 + 1) * 3968].rearrange("p (j n) -> p j n", j=31)
                dk = ("dg", c % 2)
                tt(dg, identf[:].unsqueeze(1).broadcast_to([128, 31, 128]),
                   pcol[:, wc:wc + 31].unsqueeze(2).broadcast_to([128, 31, 128]), ALU.mult,
                   [("identf",), ("pcol",)], [dk, ("bias",)])
                for tg in range(4):
                    b = nbank()
                    for j in range(31):
                        mm(bank(b), dg[:, j, :], g[:, j + tg * 512: j + tg * 512 + 512], j == 0, j == 30, gkeys + [dk, ("bias",)], [("ps", b)])
                    tsc(acc4[:, c, tg * 512:(tg + 1) * 512], bank(b), pcol[:, PC_CB + c:PC_CB + c + 1], None, ALU.add, None,
                        [("ps", b), ("pcol",)], [ak + (tg,)])
            for tg in range(4):
                akeys = [("cv", "acc", c, tg) for c in range(4)]
                sl = slice(tg * 512, (tg + 1) * 512)
                bm = nbank()
                for c in range(4):
                    mm(bank(bm), o512_f[:], acc4[:, c, sl], c == 0, c == 3, akeys + [("o512_f",)], [("ps", bm)])
                bs = nbank()
                for c in range(4):
                    ti = nxt("tmp", 4)
                    act(tmp[ti][:, 0:512], acc4[:, c, sl], AF.Square, akeys, [("tmp", ti)])
                    mm(bank(bs), o512_f[:], tmp[ti][:, 0:512], c == 0, c == 3, [("tmp", ti), ("o512_f",)], [("ps", bs)])
                tm = nxt("tmp", 4)
                P.add("scalar", lambda e, tm=tm, bm=bm: e.copy(out=tmp[tm][:, 0:512], in_=bank(bm)), [("ps", bm)], [("tmp", tm)])
                tv = nxt("tmp", 4)
                tt(tmp[tv][:, 0:512], tmp[tm][:, 0:512], tmp[tm][:, 0:512], ALU.mult, [("tmp", tm)], [("tmp", tv)])
                tt(tmp[tv][:, 0:512], bank(bs), tmp[tv][:, 0:512], ALU.subtract, [("ps", bs), ("tmp", tv)], [("tmp", tv)])
                act(tmp[tv][:, 0:512], tmp[tv][:, 0:512], AF.Ln, [("tmp", tv), ("epsc",)], [("tmp", tv)], bias=epsc[:, 0:1])
                act(tmp[tv][:, 0:512], tmp[tv][:, 0:512], AF.Exp, [("tmp", tv)], [("tmp", tv)], scale=-0.5)
                for c in range(4):
                    ty = nxt("tmp", 4)
                    while ty in (tm, tv):
                        ty = nxt("tmp", 4)
                    tt(tmp[ty][:, 0:512], acc4[:, c, sl], tmp[tm][:, 0:512], ALU.subtract, akeys + [("tmp", tm)], [("tmp", ty)])
                    tt(tmp[ty][:, 0:512], tmp[ty][:, 0:512], tmp[tv][:, 0:512], ALU.mult, [("tmp", ty), ("tmp", tv)], [("tmp", ty)])
                    act(brT[:, (4 + c) * U + tg * 512:(4 + c) * U + (tg + 1) * 512], tmp[ty][:, 0:512], AF.Silu,
                        [("tmp", ty), ("pcol",)], [("brT", 4 + c, tg)],
                        scale=pcol[:, PC_LNG + c:PC_LNG + c + 1], bias=pcol[:, PC_LNB + c:PC_LNB + c + 1])

        def diff_phase(u, l, is_sample):
            P.new_epoch("RA")
            rr["psn"] = 4
            rr["ps"] = 0
            nkc = 64 if is_sample else 16
            for h in range(4):
                if is_sample:
                    qT, kT = ar(0, 2048), ar(2048, 8192)
                    vh = ar(10240, 8192).rearrange("p (t e) -> p t e", t=64)
                    par = 0
                else:
                    par = h % 2
                    base = par * 6144
                    qT, kT = ar(base, 2048), ar(base + 2048, 2048)
                    vh = ar(base + 4096, 2048).rearrange("p (t e) -> p t e", t=16)
                kq, kk, kv = ("df", par, "q"), ("df", par, "k"), ("df", par, "v")
                s, wk = wslot()
                wv = wbuf[s][:, 0:3072].rearrange("p (k b n) -> p k b n", k=8, b=3)
                nb_w = 1 if is_sample else 3
                wsrc = w_in[l][:, OFF_DIFF:OFF_DIFF + 1536].rearrange("(kc p) (b n) -> p kc b n", p=128, n=512)[:, :, 0:nb_w, h * 128:(h + 1) * 128]
                load_w(wv[:, :, 0:nb_w, :], wsrc, wk)
                proj_fm(lambda tg: qT[:, tg * 512:(tg + 1) * 512], lambda k: wv[:, k, 0, :], 8, hT_rhs,
                        lambda tg: hT_keys(tg) + wk, lambda tg: [kq + (tg,)])
                if not is_sample:
                    proj_fm(lambda tg: kT[:, tg * 512:(tg + 1) * 512], lambda k: wv[:, k, 1, :], 8, hT_rhs,
                            lambda tg: hT_keys(tg) + wk, lambda tg: [kk + (tg,)])
                    for t4 in range(4):
                        b = nbank()
                        for tl in range(4):
                            t = t4 * 4 + tl
                            for k in range(8):
                                mm(bank(b)[:, tl * 128:(tl + 1) * 128], hT[:, k, t * 128:(t + 1) * 128], wv[:, k, 2, :],
                                   k == 0, k == 7, [("hT", t)] + wk, [("ps", b)])
                        evac(vh[:, t4 * 4:t4 * 4 + 4, :], bank(b).rearrange("p (t e) -> p t e", t=4), [("ps", b)], [kv + (t4,)])
                    kkeys_all = [kk + (tg,) for tg in range(4)]
                    vkeys_all = [kv + (t4,) for t4 in range(4)]
                else:
                    for ro in range(4):
                        dma_dyn(kT[:, ro * 2048:(ro + 1) * 2048], lambda e, ro=ro, h=h: tfv(l, h)[rank_of(e, ro)][:, :],
                                [("tfm", l, h)], [kk + (ro,)])
                        dma_dyn(vh[:, ro * 16:(ro + 1) * 16, :], lambda e, ro=ro, h=h: ttv(l, h)[rank_of(e, ro)],
                                [("ttm", l, h)], [kv + (ro,)])
                    kkeys_all = [kk + (ro,) for ro in range(4)]
                    vkeys_all = [kv + (ro,) for ro in range(4)]
                dma("sync", biasbuf[:, 0:3072].rearrange("p (m n) -> p m n", m=6), t5t_d[h], [], [("bias",)])
                if is_sample:
                    dma("sync", biasbuf[:, 3072:4096].rearrange("p (m n) -> p m n", m=2), t5j_d[h], [], [("bias",)])
                def df_S(qg, kc, c):
                    ro, kcl = kc // 16, kc % 16
                    special = None
                    cb = None
                    cbk = None
                    if ro == 0:
                        d = kcl * 128 - qg * 512
                        if -128 <= d <= 512:
                            special = (d + 128) // 128
                        else:
                            cb = t5c[:, h, 0:1] if d < 0 else t5c[:, h, 1:2]
                            cbk = ("t5c",)
                    elif ro == 1 and kcl == 0 and qg == 3:
                        special = 6
                    elif ro == 3 and kcl == 15 and qg == 0:
                        special = 7
                    else:
                        cb = t5cs[:, h, ro:ro + 1]
                        cbk = ("t5cs",)
                    b = nxt("dfs", 4)
                    mm(bank(b), kT[c * 64:(c + 1) * 64, kc * 128:(kc + 1) * 128], qT[c * 64:(c + 1) * 64, qg * 512:(qg + 1) * 512],
                       True, True, kkeys_all + [kq + (qg,)], [("ps", b)])
                    pi = nxt("pT", 4)
                    if special is not None:
                        ti = nxt("tmp", 4)
                        stt(tmp[ti][:, 0:512], bank(b), SCALE, biasbuf[:, special * 512:(special + 1) * 512], ALU.mult, ALU.add,
                            [("ps", b), ("bias",)], [("tmp", ti)])
                        act(pT[pi][:, 0:512], tmp[ti][:, 0:512], AF.Exp, [("tmp", ti)], [("pT", pi)])
                    else:
                        act(pT[pi][:, 0:512], bank(b), AF.Exp, [("ps", b), cbk], [("pT", pi)], scale=SCALE, bias=cb)
                    return pi

                def df_PV(qg, kc, c, pi):
                    mm(bank(4 + c), vh[:, kc, :], pT[pi][:, 0:512], kc == 0, kc == nkc - 1, vkeys_all + [("pT", pi)], [("ps", 4 + c)])
                    mm(bank(6 + c), ones_bf[:], pT[pi][:, 0:512], kc == 0, kc == nkc - 1, [("ones_bf",), ("pT", pi)], [("ps", 6 + c)])

                def df_post(qg):
                    tA, tB, tC, tD = 0, 1, 2, 3
                    cpy(tmp[tA][:, 0:512], bank(4), [("ps", 4)], [("tmp", tA)])
                    P.add("scalar", lambda e: e.copy(out=tmp[tC][:, 0:512], in_=bank(6)), [("ps", 6)], [("tmp", tC)])
                    cpy(tmp[tB][:, 0:512], bank(5), [("ps", 5)], [("tmp", tB)])
                    P.add("scalar", lambda e: e.copy(out=tmp[tD][:, 0:512], in_=bank(7)), [("ps", 7)], [("tmp", tD)])
                    act(tmp[tC][:, 0:512], tmp[tC][:, 0:512], AF.Ln, [("tmp", tC)], [("tmp", tC)])
                    act(tmp[tD][:, 0:512], tmp[tD][:, 0:512], AF.Ln, [("tmp", tD)], [("tmp", tD)])
                    act(tmp[tC][:, 0:512], tmp[tC][:, 0:512], AF.Exp, [("tmp", tC)], [("tmp", tC)], scale=-1.0)
                    act(tmp[tD][:, 0:512], tmp[tD][:, 0:512], AF.Exp, [("tmp", tD)], [("tmp", tD)], scale=-1.0)
                    tt(tmp[tA][:, 0:512], tmp[tA][:, 0:512], tmp[tC][:, 0:512], ALU.mult, [("tmp", tA), ("tmp", tC)], [("tmp", tA)])
                    tt(tmp[tB][:, 0:512], tmp[tB][:, 0:512], tmp[tD][:, 0:512], ALU.mult, [("tmp", tB), ("tmp", tD)], [("tmp", tB)])
                    stt(tmp[tA][:, 0:512], tmp[tB][:, 0:512], small[:, 2:3], tmp[tA][:, 0:512], ALU.mult, ALU.add,
                        [("tmp", tA), ("tmp", tB), ("lamc",)], [("tmp", tA)])
                    act(tmp[tC][:, 0:512], tmp[tA][:, 0:512], AF.Square, [("tmp", tA)], [("tmp", tC)])
                    b = nxt("dfs", 4)
                    mm(bank(b), ones_f[:], tmp[tC][:, 0:512], True, True, [("tmp", tC), ("ones_f",)], [("ps", b)])
                    act(tmp[tC][:, 0:512], bank(b), AF.Ln, [("ps", b), ("epsc",)], [("tmp", tC)], bias=epsc[:, 0:1])
                    act(tmp[tC][:, 0:512], tmp[tC][:, 0:512], AF.Exp, [("tmp", tC)], [("tmp", tC)], scale=-0.5)
                    stt(brT[:, (8 + h) * U + qg * 512:(8 + h) * U + (qg + 1) * 512], tmp[tA][:, 0:512], small[:, 3:4], tmp[tC][:, 0:512],
                        ALU.mult, ALU.mult, [("tmp", tA), ("tmp", tC), ("lamc",)], [("brT", 8 + h, qg)])

                steps = [(qg, kc) for qg in range(4) for kc in range(nkc)]
                DL = 1
                pend = {}
                for j_ in range(-DL, len(steps)):
                    if j_ + DL < len(steps):
                        qg2, kc2 = steps[j_ + DL]
                        pend[j_ + DL] = (df_S(qg2, kc2, 0), df_S(qg2, kc2, 1))
                    if j_ >= 0:
                        qg, kc = steps[j_]
                        p0, p1 = pend.pop(j_)
                        mm(bank(4), vh[:, kc, :], pT[p0][:, 0:512], kc == 0, kc == nkc - 1, vkeys_all + [("pT", p0)], [("ps", 4)])
                        mm(bank(5), vh[:, kc, :], pT[p1][:, 0:512], kc == 0, kc == nkc - 1, vkeys_all + [("pT", p1)], [("ps", 5)])
                        mm(bank(6), ones_bf[:], pT[p0][:, 0:512], kc == 0, kc == nkc - 1, [("ones_bf",), ("pT", p0)], [("ps", 6)])
                        mm(bank(7), ones_bf[:], pT[p1][:, 0:512], kc == 0, kc == nkc - 1, [("ones_bf",), ("pT", p1)], [("ps", 7)])
                        if kc == nkc - 1:
                            df_post(qg)
            rr["ps"] = 0
            rr["psn"] = 8

        def merge_phase(u, l, xsrc, xdst_mid):
            P.new_epoch("RA")
            mT = arena[:, 0:16384].rearrange("p (k n) -> p k n", k=8)
            for dc in range(8):
                s, wk = wslot()
                wg = wbuf[s][:, 0:3072].rearrange("p (k b n) -> p k b n", k=8, b=3)
                wbr = wbuf[s][:, 3072:4608].rearrange("p (b k n) -> p b k n", b=3, k=4)
                load_w(wg, w_in[l][:, OFF_GATE:].rearrange("(kc p) (b n) -> p kc b n", p=128, n=1024)[:, :, :, dc * 128:(dc + 1) * 128], wk)
                load_w(wbr, w_branch[l].rearrange("b (k p) n -> p b k n", p=128)[:, :, :, dc * 128:(dc + 1) * 128], wk)
                for tg in range(4):
                    tacc = nxt("tmp", 4)
                    for b_ in range(3):
                        bg = nbank()
                        for k in range(8):
                            mm(bank(bg), wg[:, k, b_, :], hT_rhs(k, tg), k == 0, k == 7, hT_keys(tg) + wk, [("ps", bg)])
                        bp = nbank()
                        for k in range(4):
                            mm(bank(bp), wbr[:, b_, k, :], brT[:, (b_ * 4 + k) * U + tg * 512:(b_ * 4 + k) * U + (tg + 1) * 512],
                               k == 0, k == 3, [("brT", b_ * 4 + k, tg)] + wk, [("ps", bp)])
                        tg_ = nxt("tmp", 4)
                        while tg_ == tacc:
                            tg_ = nxt("tmp", 4)
                        bc = PC_BGATE + b_ * 8 + dc
                        act(tmp[tg_][:, 0:512], bank(bg), AF.Sigmoid, [("ps", bg), ("pcol",)], [("tmp", tg_)], bias=pcol[:, bc:bc + 1])
                        if b_ == 0:
                            tt(tmp[tacc][:, 0:512], bank(bp), tmp[tg_][:, 0:512], ALU.mult, [("ps", bp), ("tmp", tg_)], [("tmp", tacc)])
                        else:
                            tt(tmp[tg_][:, 0:512], bank(bp), tmp[tg_][:, 0:512], ALU.mult, [("ps", bp), ("tmp", tg_)], [("tmp", tg_)])
                            if b_ == 1:
                                tt(tmp[tacc][:, 0:512], tmp[tacc][:, 0:512], tmp[tg_][:, 0:512], ALU.add,
                                   [("tmp", tacc), ("tmp", tg_)], [("tmp", tacc)])
                            else:
                                tt(mT[:, dc, tg * 512:(tg + 1) * 512], tmp[tacc][:, 0:512], tmp[tg_][:, 0:512], ALU.add,
                                   [("tmp", tacc), ("tmp", tg_)], [("mT", dc, tg)])
            wo = biasbuf[:, :].bitcast(BF16).rearrange("p (k n) -> p k n", k=8)
            load_w(wo, w_out[l].rearrange("(k p) n -> p k n", p=128), [("bias",)])
            LA = 2
            held = {}
            for t in range(NT + LA):
                if t < NT:
                    xi = nxt("x", 3)
                    dma("sync", xt[xi][:], xsrc[u, t * 128:(t + 1) * 128, :], [("xd", l, u, t)], [("xt", xi)])
                    b = nbank2()
                    for nh in range(2):
                        for k in range(8):
                            mm(bank(b + nh), mT[:, k, t * 128:(t + 1) * 128], wo[:, k, nh * 512:(nh + 1) * 512], k == 0, k == 7,
                               [("mT", k, t // 4), ("bias",)], [("ps", b + nh)])
                    resid_tail(ps[:, b * 512:(b + 2) * 512], [("ps", b), ("ps", b + 1)], xi, 0, l)
                    dma("sync", xdst_mid[u, t * 128:(t + 1) * 128, :], xt[xi][:], [("xt", xi)], [("xm", l, u, t)])
                    held[t] = xi
                if t - LA >= 0:
                    xj = held.pop(t - LA)
                    prenorm_tile(xt[xj][:], ("xt", xj), t - LA, PC_GFFN)

        def resid_tail(y_ps, ykeys, xi, which, l):
            ss, sk = col()
            tj = nxt("tmp", 4)
            act(tmp[tj][:, 0:512].bitcast(BF16), y_ps, AF.Square, ykeys, [sk, ("tmp", tj)], accum_out=ss)
            rs, rk = col()
            rsqrt_col(rs, rk, ss, sk, 1.0 / D)
            stt(ytmp[:], y_ps, rs, gpost[:, which, :], ALU.mult, ALU.mult, ykeys + [rk, ("gpost",)], [("ytmp",)])
            tt(xt[xi][:], xt[xi][:], ytmp[:], ALU.add, [("xt", xi), ("ytmp",)], [("xt", xi)])

        def ffn_phase(u, l, xsrc_mid, xdst, next_a=None):
            P.new_epoch("RB")
            wfo = brT[:, 0:22528].rearrange("p (k n) -> p k n", k=NHC)
            for k0 in range(0, NHC, 4):
                k1 = min(NHC, k0 + 4)
                load_w(wfo[:, k0:k1, :], w_ffn_out[l].rearrange("(k p) n -> p k n", p=128)[:, k0:k1, :], [("wfo", k0)])
            wfo_keys = [("wfo", k0) for k0 in range(0, NHC, 4)] + [("wfo",)]
            aT = arena[:, 0:22528].rearrange("p (k n) -> p k n", k=NHC)
            for half in range(2):
                if half == 0:
                    P.new_epoch("RA")
                for hc in range(NHC):
                    s, wk = wslot()
                    wv = wbuf[s][:, 0:2048].rearrange("p (k b n) -> p k b n", k=8, b=2)
                    load_w(wv, w_ffn_in[l].rearrange("(kc p) (b n) -> p kc b n", p=128, n=FFN_H)[:, :, :, hc * 128:(hc + 1) * 128], wk)
                    for t2 in range(2):
                        tg = half * 2 + t2
                        bg = nbank()
                        for k in range(8):
                            mm(bank(bg), wv[:, k, 0, :], hT_rhs(k, tg), k == 0, k == 7, hT_keys(tg) + wk, [("ps", bg)])
                        bu = nbank()
                        for k in range(8):
                            mm(bank(bu), wv[:, k, 1, :], hT_rhs(k, tg), k == 0, k == 7, hT_keys(tg) + wk, [("ps", bu)])
                        ti = nxt("tmp", 4)
                        act(tmp[ti][:, 0:512], bank(bg), AF.Silu, [("ps", bg)], [("tmp", ti)])
                        tt(aT[:, hc, t2 * 512:(t2 + 1) * 512], bank(bu), tmp[ti][:, 0:512], ALU.mult,
                           [("ps", bu), ("tmp", ti)], [("aT", hc, t2)])
                if half == 1 and next_a is not None:
                    next_a()
                for tl in range(8):
                    t = half * 8 + tl
                    xi = nxt("x", 3)
                    dma("sync", xt[xi][:], xsrc_mid[u, t * 128:(t + 1) * 128, :], [("xm", l, u, t)], [("xt", xi)])
                    b = nbank2()
                    for nh in range(2):
                        for hc in range(NHC):
                            mm(bank(b + nh), aT[:, hc, tl * 128:(tl + 1) * 128], wfo[:, hc, nh * 512:(nh + 1) * 512],
                               hc == 0, hc == NHC - 1, [("aT", hc, tl // 4)] + wfo_keys, [("ps", b + nh)])
                    resid_tail(ps[:, b * 512:(b + 2) * 512], [("ps", b), ("ps", b + 1)], xi, 1, l)
                    dma("sync", xdst[u, t * 128:(t + 1) * 128, :], xt[xi][:], [("xt", xi)], [("xd", l + 1, u, t)])

        def layer_setup(l):
            lam_init = 0.8 - 0.6 * math.exp(-0.3 * l)
            dma("sync", pcol[:], pcol_d[l], [], [("pcol",)])
            dma("sync", gpost[:, 0, :], prow_d[l, 0:1, :].broadcast_to([128, D]), [], [("gpost",)])
            dma("sync", gpost[:, 1, :], prow_d[l, 1:2, :].broadcast_to([128, D]), [], [("gpost",)])
            dma("sync", lamb, lam_d[l:l + 1, :].broadcast_to([128, 256]), [], [("ytmp",)])
            P.add("vector", lambda e: e.tensor_tensor(out=lamb[:, 0:64], in0=lamb[:, 0:64], in1=lamb[:, 64:128], op=ALU.mult), [("ytmp",)], [("ytmp",)])
            P.add("vector", lambda e: e.tensor_tensor(out=lamb[:, 128:192], in0=lamb[:, 128:192], in1=lamb[:, 192:256], op=ALU.mult), [("ytmp",)], [("ytmp",)])
            P.add("vector", lambda e: e.reduce_sum(out=small[:, 0:1], in_=lamb[:, 0:64], axis=mybir.AxisListType.X), [("ytmp",)], [("lamc",)])
            P.add("vector", lambda e: e.reduce_sum(out=small[:, 1:2], in_=lamb[:, 128:192], axis=mybir.AxisListType.X), [("ytmp",), ("lamc",)], [("lamc",)])
            act(small[:, 0:2], small[:, 0:2], AF.Exp, [("lamc",)], [("lamc",)])
            stt(small[:, 2:3], small[:, 0:1], -1.0, small[:, 1:2], ALU.mult, ALU.add, [("lamc",)], [("lamc",)])
            tsc(small[:, 2:3], small[:, 2:3], -lam_init, None, ALU.add, None, [("lamc",)], [("lamc",)])
            tsc(small[:, 3:4], pcol[:, PC_SUB:PC_SUB + 1], 1.0 - lam_init, None, ALU.mult, None, [("pcol",), ("lamc",)], [("lamc",)])

        for l in range(depth):
            xsrc = xin if l == 0 else x1
            xdst = yout if l == depth - 1 else x1
            layer_setup(l)
            if with_sample:
                P.mark(f"L{l} PRE")
                sample_prepass(xsrc, l)
            units = list(range(1, NU)) + ([0] if with_sample else [])
            for ui, u in enumerate(units):
                smp = (u == 0)
                if ui == 0:
                    P.mark(f"L{l}U{u} A")
                    phase_a(xsrc, u, l)
                P.mark(f"L{l}U{u} NA")
                na_phase(u, l, smp)
                P.mark(f"L{l}U{u} CV")
                conv_phase(u, l, smp)
                P.mark(f"L{l}U{u} DF")
                diff_phase(u, l, smp)
                P.mark(f"L{l}U{u} MG")
                merge_phase(u, l, xsrc, xmid)
                P.mark(f"L{l}U{u} FF")
                nxt_a = None
                if ui + 1 < len(units):
                    nu_ = units[ui + 1]
                    nxt_a = (lambda nu_=nu_: phase_a(xsrc, nu_, l))
                ffn_phase(u, l, xmid, xdst, nxt_a)
        P.mark("END")

        P.finalize()
        P.alloc_sems(nc, es)
        block = es.enter_context(nc.Block())

        @block.sync
        def _(e):
            P.emit("sync", e, final_wait=True)

        @block.gpsimd
        def _(e):
            P.emit("gpsimd", e)

        @block.tensor
        def _(e):
            P.emit("tensor", e)

        @block.vector
        def _(e):
            P.emit("vector", e)

        @block.scalar
        def _(e):
            P.emit("scalar", e)

    return nc, P


def prepare_inputs(inp, n_prompt=4, cores=8):
    t5 = np.asarray(inp["t5_bias"], np.float32)
    p = np.arange(128)[:, None]
    j = np.arange(512)[None, :]
    t5t = np.empty((4, 128, 6, 512), np.float32)
    for m in range(6):
        bk = _t5_bucket(((m - 1) * 128 + p - j).astype(np.int32))
        for h in range(4):
            t5t[h, :, m, :] = t5[bk, h]
    t5c = np.empty((128, 4, 2), np.float32)
    t5c[:, :, 0] = t5[15][None, :]
    t5c[:, :, 1] = t5[31][None, :]
    rpb = np.asarray(inp["na_rpb"], np.float32)
    nabp = _nab_table(rpb, 32, 0)
    shared = {
        "w_in": np.ascontiguousarray(inp["w_in"], np.float32),
        "w_branch": np.ascontiguousarray(inp["w_branch"], np.float32),
        "w_out": np.ascontiguousarray(inp["w_out"], np.float32),
        "w_ffn_in": np.ascontiguousarray(inp["w_ffn_in"], np.float32),
        "w_ffn_out": np.ascontiguousarray(inp["w_ffn_out"], np.float32),
        "pcol": np.stack([_pack_pcol(inp, l) for l in range(DEPTH)]),
        "prow": np.stack([np.stack([inp["ln_mix_post"][l], inp["ln_ffn_post"][l]]) for l in range(DEPTH)]).astype(np.float32),
        "lam": np.asarray(inp["diff_lambda"], np.float32).reshape(DEPTH, 256),
        "t5t": t5t, "t5c": t5c, "nabp": nabp,
        "ident": np.eye(128, dtype=np.float32),
    }
    nabs_q = [_nab_table(rpb, 128, 16 * q) for q in range(4)]
    in_maps = []
    xp = np.asarray(inp["x_prompt"], np.float32)
    xs = np.asarray(inp["x_sample"], np.float32)
    for c in range(cores):
        q, s = c % 4, c // 4
        m = dict(shared)
        xin = np.empty((1 + n_prompt, U, D), np.float32)
        xin[0] = xs[s, q * U:(q + 1) * U]
        for i in range(n_prompt):
            xin[1 + i] = xp[c * n_prompt + i]
        m["xin"] = xin
        t5cs = np.empty((128, 4, 4), np.float32)
        for ro in range(4):
            r = (q + ro) % 4
            t5cs[:, :, ro] = (t5[31] if r > q else t5[15])[None, :]
        m["t5cs"] = t5cs
        t5j = np.empty((4, 128, 2, 512), np.float32)
        r1 = ((q + 1) % 4 - q) * U + 0 * 128 + p - (3 * 512 + j)
        r3 = ((q + 3) % 4 - q) * U + 15 * 128 + p - j
        b1, b3 = _t5_bucket(r1.astype(np.int32)), _t5_bucket(r3.astype(np.int32))
        for h in range(4):
            t5j[h, :, 0, :] = t5[b1, h]
            t5j[h, :, 1, :] = t5[b3, h]
        m["t5j"] = t5j
        m["nabs"] = nabs_q[q]
        cf = np.zeros((128, 2), np.float32)
        cf[:, 0] = 1.0 if q > 0 else 0.0
        cf[:, 1] = 1.0 if q < 3 else 0.0
        m["cflag"] = cf
        in_maps.append(m)
    return in_maps


_CACHE = {}


def kernel(**inputs):
    inp = {k: np.asarray(v) for k, v in inputs.items()}
    if "nc" not in _CACHE:
        _CACHE["nc"] = build_program()[0]
    nc = _CACHE["nc"]
    in_maps = prepare_inputs(inp)
    res = run_bass_kernel_spmd(nc, in_maps, core_ids=list(range(8)))
    y_prompt = np.empty((32, U, D), np.float32)
    y_sample = np.empty((2, 8192, D), np.float32)
    for c in range(8):
        yo = np.asarray(res.results[c]["yout"], np.float32)
        q, s = c % 4, c // 4
        y_sample[s, q * U:(q + 1) * U] = yo[0]
        for i in range(4):
            y_prompt[c * 4 + i] = yo[1 + i]
    return (y_prompt, y_sample)
```

```python
import math
from contextlib import ExitStack

import numpy as np
import concourse.bass as bass
import concourse.mybir as mybir
from concourse.bass_utils import run_bass_kernel_spmd

F32 = mybir.dt.float32
BF16 = mybir.dt.bfloat16
AF = mybir.ActivationFunctionType
ALU = mybir.AluOpType

D = 1024
DEPTH = 2
U = 2048
NT = 16
IN_COLS = 7168
OFF_CONV = 1536
OFF_DIFF = 2560
OFF_GATE = 4096
FFN_H = 2816
NHC = 22
EPS = 1e-6
NEG = -30000.0
SCALE = 0.125
NPC = 177
PC_GPRE, PC_GFFN, PC_BGATE, PC_CW, PC_CB, PC_LNG, PC_LNB, PC_SUB = 0, 8, 16, 40, 164, 168, 172, 176
NAB_GRP = {"b0": (0, list(range(-2, 4))), "b1": (6, list(range(-2, 3))), "int": (11, list(range(-2, 3))),
           "b14": (16, list(range(-2, 3))), "b15": (21, list(range(-3, 3)))}
CH = 8000
KQ = 8


def _grp_of_block(i):
    return {0: "b0", 1: "b1", 14: "b14", 15: "b15"}.get(i, "int")


class _Op:
    __slots__ = ("eng", "fn", "deps", "kind", "sig", "needs_sig")

    def __init__(self, eng, fn, deps, kind):
        self.eng, self.fn, self.deps, self.kind = eng, fn, deps, kind
        self.sig = None
        self.needs_sig = kind != "c"


class Prog:
    def __init__(self):
        self.marks = []
        self.ops = []
        self.last_w = {}
        self.readers = {}
        self.reg_cur = {}
        self.reg_prev = {}

    def mark(self, label):
        cnt = {}
        for op in self.ops:
            cnt[op.eng] = cnt.get(op.eng, 0) + 1
        self.marks.append((label, cnt))

    def new_epoch(self, R):
        cur = self.reg_cur.pop(R, ({}, []))
        self.reg_prev[R] = list(cur[0].values()) + cur[1]

    def add(self, eng, fn, reads=(), writes=(), kind="c", regions=()):
        i = len(self.ops)
        deps = set()
        for R in regions:
            deps.update(self.reg_prev.get(R, ()))
            cur = self.reg_cur.setdefault(R, ({}, []))
            if kind == "d":
                cur[1].append(i)
            else:
                cur[0][eng] = i
        lw, rd = self.last_w, self.readers
        for k in reads:
            w = lw.get(k)
            if w is not None:
                deps.add(w)
        for k in writes:
            w = lw.get(k)
            if w is not None:
                deps.add(w)
            r = rd.get(k)
            if r:
                deps.update(r[0].values())
                deps.update(r[1])
        for k in reads:
            r = rd.get(k)
            if r is None:
                r = rd[k] = ({}, [])
            if kind == "d":
                r[1].append(i)
            else:
                r[0][eng] = i
        for k in writes:
            lw[k] = i
            rd[k] = ({}, [])
        deps.discard(i)
        self.ops.append(_Op(eng, fn, deps, kind))
        return i

    def finalize(self):
        ops = self.ops
        for op in ops:
            nd = []
            for d in op.deps:
                a = ops[d]
                if a.eng == "tensor" and op.eng == "tensor" and a.kind == "c" and op.kind == "c":
                    continue
                a.needs_sig = True
                nd.append(d)
            op.deps = nd
        self.ccount = {}
        self.dcount = {}
        self.ncc = 0
        for op in ops:
            if op.kind == "d":
                n = self.dcount.get(op.eng, 0)
                self.dcount[op.eng] = n + 1
                op.sig = ("d", op.eng, n)
            elif op.kind == "cc":
                op.sig = ("cc", self.ncc)
                self.ncc += 1
            elif op.needs_sig:
                k = self.ccount.get(op.eng, 0) + 1
                self.ccount[op.eng] = k
                op.sig = ("c", op.eng, k)
        last = {}
        for i, op in enumerate(ops):
            if op.kind == "d":
                lst = last.setdefault(op.eng, [])
                if len(lst) >= KQ:
                    op.deps.append(lst[-KQ])
                lst.append(i)

    def alloc_sems(self, nc, es):
        self.csem = {e: [es.enter_context(nc.semaphore(f"c_{e}_{j}")) for j in range((k + CH - 1) // CH)]
                     for e, k in self.ccount.items()}
        self.dsem = {e: [es.enter_context(nc.semaphore(f"d_{e}_{j}")) for j in range(KQ)] for e in self.dcount}
        self.ccsem = [es.enter_context(nc.semaphore(f"cc_{j}")) for j in range(self.ncc)]

    def _resolve(self, sig):
        if sig[0] == "c":
            k = sig[2]
            return self.csem[sig[1]][(k - 1) // CH], (k - 1) % CH + 1, 1
        if sig[0] == "d":
            n = sig[2]
            return self.dsem[sig[1]][n % KQ], 16 * (n // KQ + 1), 16
        return self.ccsem[sig[1]], 1, None

    def emit(self, engname, e, final_wait=False):
        waited = {}
        ops = self.ops
        for op in ops:
            if op.eng != engname:
                continue
            for d in op.deps:
                sem, val, _ = self._resolve(ops[d].sig)
                key = id(sem)
                if waited.get(key, 0) < val:
                    e.wait_ge(sem, val)
                    waited[key] = val
            try:
                ins = op.fn(e)
            except Exception:
                import inspect
                cv = inspect.getclosurevars(op.fn)
                print("EMIT FAIL", engname, op.kind, {k: (str(v)[:200]) for k, v in cv.nonlocals.items()}, flush=True)
                raise
            if op.sig is not None:
                sem, val, inc = self._resolve(op.sig)
                if inc is None:
                    ins.then_inc(sem)
                else:
                    ins.then_inc(sem, inc)
        if final_wait:
            for q, n in self.dcount.items():
                for j in range(min(KQ, n)):
                    cnt = (n - 1 - j) // KQ + 1
                    e.wait_ge(self.dsem[q][j], 16 * cnt)


def _t5_bucket(rel):
    nb, max_exact = 16, 8
    ret = np.where(rel > 0, nb, 0)
    n = np.abs(rel)
    nf = np.maximum(n, 1).astype(np.float32)
    large = max_exact + (np.log(nf / np.float32(max_exact)) / np.float32(math.log(128 / max_exact))
                         * np.float32(nb - max_exact)).astype(np.int32)
    large = np.minimum(large, nb - 1)
    return ret + np.where(n < max_exact, n, large)


def _na_tile(rpb_h, R, gi, gj):
    if gj < 0 or gj >= R // 2 or gi < 0 or gi >= R // 2:
        return np.full((128, 128), NEG, np.float32)
    kp = np.arange(128)
    krow, kcol = 2 * gj + kp // 64, kp % 64
    qrow, qcol = 2 * gi + kp // 64, kp % 64
    rs = np.clip(qrow - 4, 0, R - 8)
    cs = np.clip(qcol - 8, 0, 48)
    valid = ((krow[:, None] >= rs[None, :]) & (krow[:, None] < rs[None, :] + 8)
             & (kcol[:, None] >= cs[None, :]) & (kcol[:, None] < cs[None, :] + 16))
    val = rpb_h[np.clip(krow[:, None] - qrow[None, :] + 7, 0, 14), np.clip(kcol[:, None] - qcol[None, :] + 15, 0, 30)]
    return np.where(valid, val, np.float32(NEG)).astype(np.float32)


def _nab_table(rpb, R, base_block):
    L = rpb.shape[0]
    out = np.empty((L, 8, 128, 27, 128), np.float32)
    loc = {"b0": 0, "b1": 1, "int": 5, "b14": 14, "b15": 15}
    for l in range(L):
        for h in range(8):
            for g, (off, jrels) in NAB_GRP.items():
                gi = base_block + loc[g]
                for s, jr in enumerate(jrels):
                    out[l, h, :, off + s, :] = _na_tile(rpb[l, h], R, gi, gi + jr)
    return out


def _pack_pcol(inp, l):
    pc = np.zeros((128, NPC), np.float32)
    pc[:, PC_GPRE:PC_GPRE + 8] = inp["ln_mix_pre"][l].reshape(8, 128).T
    pc[:, PC_GFFN:PC_GFFN + 8] = inp["ln_ffn_pre"][l].reshape(8, 128).T
    pc[:, PC_BGATE:PC_BGATE + 24] = inp["b_gate"][l].reshape(24, 128).T
    pc[:, PC_CW:PC_CW + 124] = inp["conv_dw_w"][l].reshape(31, 4, 128).transpose(2, 1, 0).reshape(128, 124)
    pc[:, PC_CB:PC_CB + 4] = inp["conv_dw_b"][l].reshape(4, 128).T
    pc[:, PC_LNG:PC_LNG + 4] = inp["conv_ln_g"][l].reshape(4, 128).T
    pc[:, PC_LNB:PC_LNB + 4] = inp["conv_ln_b"][l].reshape(4, 128).T
    pc[:, PC_SUB] = inp["diff_subln_g"][l]
    return pc


def build_program(n_prompt=4, depth=DEPTH, with_sample=True, taps=()):
    nc = bass.Bass("TRN2", target_bir_lowering=False)
    NU = 1 + n_prompt
    P = Prog()
    dt = nc.dram_tensor

    xin = dt("xin", [NU, U, D], F32, kind="ExternalInput").ap()
    w_in = dt("w_in", [DEPTH, D, IN_COLS], F32, kind="ExternalInput").ap()
    w_branch = dt("w_branch", [DEPTH, 3, 512, D], F32, kind="ExternalInput").ap()
    w_out = dt("w_out", [DEPTH, D, D], F32, kind="ExternalInput").ap()
    w_ffn_in = dt("w_ffn_in", [DEPTH, D, 2 * FFN_H], F32, kind="ExternalInput").ap()
    w_ffn_out = dt("w_ffn_out", [DEPTH, FFN_H, D], F32, kind="ExternalInput").ap()
    pcol_d = dt("pcol", [DEPTH, 128, NPC], F32, kind="ExternalInput").ap()
    prow_d = dt("prow", [DEPTH, 2, D], F32, kind="ExternalInput").ap()
    lam_d = dt("lam", [DEPTH, 256], F32, kind="ExternalInput").ap()
    t5t_d = dt("t5t", [4, 128, 6, 512], F32, kind="ExternalInput").ap()
    t5c_d = dt("t5c", [128, 4, 2], F32, kind="ExternalInput").ap()
    t5cs_d = dt("t5cs", [128, 4, 4], F32, kind="ExternalInput").ap()
    t5j_d = dt("t5j", [4, 128, 2, 512], F32, kind="ExternalInput").ap()
    nabp_d = dt("nabp", [DEPTH, 8, 128, 27, 128], F32, kind="ExternalInput").ap()
    nabs_d = dt("nabs", [DEPTH, 8, 128, 27, 128], F32, kind="ExternalInput").ap()
    cflag_d = dt("cflag", [128, 2], F32, kind="ExternalInput").ap()
    ident_d = dt("ident", [128, 128], F32, kind="ExternalInput").ap()
    yout = dt("yout", [NU, U, D], F32, kind="ExternalOutput").ap()
    tap_d = {}
    for name, shape in taps:
        tap_d[name] = dt("tap_" + name, list(shape), F32, kind="ExternalOutput").ap()

    xmid = dt("xmid", [NU, U, D], F32).ap()
    x1 = dt("x1", [NU, U, D], F32).ap()
    cf = [[dt(f"cf{l}_{s_}", [128, U], BF16).ap() for s_ in range(12)] for l in range(depth)]
    ct = [[dt(f"ct{l}_{j_}", [U, 128], BF16).ap() for j_ in range(8)] for l in range(depth)]
    tfm = [[dt(f"tfm{l}_{s_}", [4 * 128, U], BF16).ap() for s_ in range(12)] for l in range(depth)]
    ttm = [[dt(f"ttm{l}_{j_}", [4 * U, 128], BF16).ap() for j_ in range(8)] for l in range(depth)]

    es = ExitStack()
    with es:
        S = lambda name, shape, dtype: es.enter_context(nc.sbuf_tensor(name, shape, dtype))
        ident = S("ident_s", [128, 128], BF16)
        identf = S("identf", [128, 128], F32)
        ones_bf = S("ones_bf", [128, 128], BF16)
        ones_f = S("ones_f", [128, 128], F32)
        o512_f = S("o512_f", [128, 128], F32)
        epsc = S("epsc", [128, 1], F32)
        hT = S("hT", [128, 8, U], BF16)
        brT = S("brT", [128, 12 * U], BF16)
        arena = S("arena", [128, 22528], BF16)
        wbuf = [S(f"wbuf{i}", [128, 4608], BF16) for i in range(2)]
        xt = [S(f"xt{i}", [128, D], F32) for i in range(3)]
        hb = [S(f"hb{i}", [128, D], BF16) for i in range(2)]
        pT = [S(f"pT{i}", [128, 768], BF16) for i in range(4)]
        tmp = [S(f"tmp{i}", [128, 768], F32) for i in range(4)]
        ytmp = S("ytmp", [128, D], F32)
        lamb = ytmp[:, 0:256]
        biasbuf = S("biasbuf", [128, 4096], F32)
        gpost = S("gpost", [128, 2, D], F32)
        pcol = S("pcol_s", [128, NPC], F32)
        small = S("small", [128, 64], F32)
        t5c = S("t5c_s", [128, 4, 2], F32)
        t5cs = S("t5cs_s", [128, 4, 4], F32)
        cflag = S("cflag_s", [128, 2], F32)
        ps = es.enter_context(nc.psum_tensor("ps", [128, 4096], F32))

        rr = {"ps": 0, "x": 0, "hb": 0, "pT": 0, "tmp": 0, "wb": 0, "ev": 0, "col": 0, "psn": 8, "dfs": 0}

        def nxt(name, n):
            v = rr[name]
            rr[name] = (v + 1) % n
            return v

        def bank(b, n=512):
            return ps[:, b * 512:b * 512 + n]

        def nbank():
            return nxt("ps", rr["psn"])

        def nbank2():
            b = rr["ps"]
            if b % 2:
                b = (b + 1) % 8
            b = b % 8
            rr["ps"] = (b + 2) % 8
            return b

        def col():
            c = 8 + nxt("col", 48)
            return small[:, c:c + 1], ("small", c)

        def regs(*aps):
            r = set()
            for a in aps:
                if a is None or isinstance(a, (int, float)):
                    continue
                n = a.name
                if n == "arena":
                    r.add("RA")
                elif n == "brT":
                    r.add("RB")
            return tuple(r)

        def dma(q, out, in_, reads, writes):
            return P.add(q, lambda e, out=out, in_=in_: e.dma_start(out=out, in_=in_), reads, writes, kind="d", regions=regs(out, in_))

        def dma_dyn(out, in_fn, reads, writes):
            return P.add("sync", lambda e: e.dma_start(out=out, in_=in_fn(e)), reads, writes, kind="d", regions=regs(out))

        def mm(out, lhsT, rhs, start, stop, reads, writes):
            return P.add("tensor", lambda e: e.matmul(out, lhsT=lhsT, rhs=rhs, start=start, stop=stop), reads, writes,
                         regions=regs(lhsT, rhs))

        def tr(out, in_, reads, writes):
            return P.add("tensor", lambda e: e.transpose(out=out, in_=in_, identity=ident[:]), reads + [("ident",)], writes,
                         regions=regs(in_))

        def act(out, in_, func, reads, writes, **kw):
            return P.add("scalar", lambda e: e.activation(out=out, in_=in_, func=func, **kw), reads, writes, regions=regs(out, in_))

        def amul(out, in_, mul, reads, writes):
            return P.add("scalar", lambda e: e.mul(out=out, in_=in_, mul=mul), reads, writes, regions=regs(out, in_))

        def tt(out, in0, in1, op, reads, writes, eng="vector"):
            return P.add(eng, lambda e: e.tensor_tensor(out=out, in0=in0, in1=in1, op=op), reads, writes, regions=regs(out, in0, in1))

        def tsc(out, in0, s1, s2, op0, op1, reads, writes, eng="vector"):
            if op1 is None:
                return P.add(eng, lambda e: e.tensor_scalar(out=out, in0=in0, scalar1=s1, scalar2=None, op0=op0), reads, writes,
                             regions=regs(out, in0))
            return P.add(eng, lambda e: e.tensor_scalar(out=out, in0=in0, scalar1=s1, scalar2=s2, op0=op0, op1=op1), reads, writes,
                         regions=regs(out, in0))

        def stt(out, in0, scalar, in1, op0, op1, reads, writes, eng="vector"):
            return P.add(eng, lambda e: e.scalar_tensor_tensor(out=out, in0=in0, scalar=scalar, in1=in1, op0=op0, op1=op1), reads, writes,
                         regions=regs(out, in0, in1))

        def cpy(out, in_, reads, writes, eng="vector"):
            return P.add(eng, lambda e: e.tensor_copy(out=out, in_=in_), reads, writes, regions=regs(out, in_))

        def memset(ap, val, writes, eng="vector"):
            return P.add(eng, lambda e: e.memset(ap, val), [], writes, regions=regs(ap))

        def recip(out, in_, reads, writes):
            return P.add("vector", lambda e: e.reciprocal(out=out, in_=in_), reads, writes, regions=regs(out, in_))

        def fence(eng, old, new):
            raise RuntimeError("unused")

        def evac(out, in_, reads, writes):
            if nxt("ev", 2) == 0:
                return cpy(out, in_, reads, writes)
            return P.add("scalar", lambda e: e.copy(out=out, in_=in_), reads, writes, regions=regs(out, in_))

        def rsqrt_col(dst, dkey, src, skey, scale):
            act(dst, src, AF.Ln, [skey, ("epsc",)], [dkey], scale=scale, bias=epsc[:, 0:1])
            act(dst, dst, AF.Exp, [dkey], [dkey], scale=-0.5)

        def tap(name, idx, src, reads):
            if name in tap_d:
                dma("sync", tap_d[name][idx], src, reads, [("tap", name, str(idx))])

        def ar(off, n):
            return arena[:, off:off + n]

        AR_ALL = [("ar", i) for i in range(11)]

        def arkeys(off, n):
            return [("ar", i) for i in range(off // 2048, (off + n - 1) // 2048 + 1)]

        dma("sync", identf[:], ident_d[:, :], [], [("identf",)])
        cpy(ident[:], identf[:], [("identf",)], [("ident",)])
        memset(ones_bf[:], 1.0, [("ones_bf",)])
        memset(ones_f[:], 1.0 / 128, [("ones_f",)])
        memset(o512_f[:], 1.0 / 512, [("o512_f",)])
        memset(epsc[:], EPS, [("epsc",)])
        dma("sync", t5c[:], t5c_d[:, :, :], [], [("t5c",)])
        dma("sync", t5cs[:], t5cs_d[:, :, :], [], [("t5cs",)])
        dma("sync", cflag[:], cflag_d[:, :], [], [("cflag",)])

        _pid = {}

        def rank_of(e, ro):
            if id(e) not in _pid:
                _pid[id(e)] = e.partition_id() % 4
            return (_pid[id(e)] + ro) % 4

        def load_w(dst_ap, src_ap, wkeys):
            if len(dst_ap.shape) == 4:
                ax = 2 if dst_ap.shape[2] <= dst_ap.shape[1] else 1
                for i_ in range(dst_ap.shape[ax]):
                    if ax == 2:
                        dma("gpsimd", dst_ap[:, :, i_, :], src_ap[:, :, i_, :], [], wkeys)
                    else:
                        dma("gpsimd", dst_ap[:, i_, :, :], src_ap[:, i_, :, :], [], wkeys)
                return
            dma("gpsimd", dst_ap, src_ap, [], wkeys)

        def win_cols(l, c0, n):
            return w_in[l].rearrange("(kc p) n -> p kc n", p=128)[:, :, c0:c0 + n]

        def wslot():
            s = nxt("wb", 2)
            return s, [("wb", s)]

        def prenorm_tile(xtile, xkey, t, gcol0):
            ss, sk = col()
            hi = nxt("hb", 2)
            act(hb[hi][:], xtile, AF.Square, [xkey], [sk, ("hb", hi)], accum_out=ss)
            rs, rk = col()
            rsqrt_col(rs, rk, ss, sk, 1.0 / D)
            amul(hb[hi][:], xtile, rs, [xkey, rk], [("hb", hi)])
            b = nbank()
            pb = bank(b).bitcast(BF16).rearrange("p (a b) -> p a b", a=8)
            for k in range(8):
                tr(pb[:, k, :], hb[hi][:, k * 128:(k + 1) * 128], [("hb", hi)], [("ps", b)])
            tt(hT[:, :, t * 128:(t + 1) * 128], pb,
               pcol[:, gcol0:gcol0 + 8].unsqueeze(2).broadcast_to([128, 8, 128]), ALU.mult,
               [("ps", b), ("pcol",)], [("hT", t)])

        def phase_a(xsrc, u, l):
            for t in range(NT):
                xi = nxt("x", 3)
                dma("sync", xt[xi][:], xsrc[u, t * 128:(t + 1) * 128, :], [("xd", l, u, t)], [("xt", xi)])
                prenorm_tile(xt[xi][:], ("xt", xi), t, PC_GPRE)

        def proj_fm(dst_fn, wk_fn, nk, rhs_fn, rkeys_fn, dkeys_fn, post=None):
            for tg in range(4):
                b = nbank()
                for k in range(nk):
                    mm(bank(b), wk_fn(k), rhs_fn(k, tg), k == 0, k == nk - 1, rkeys_fn(tg), [("ps", b)])
                if post is None:
                    evac(dst_fn(tg), bank(b), [("ps", b)], dkeys_fn(tg))
                else:
                    post(tg, b)

        hT_rhs = lambda k, tg: hT[:, k, tg * 512:(tg + 1) * 512]
        hT_keys = lambda tg: [("hT", t) for t in range(tg * 4, tg * 4 + 4)]

        def sample_prepass(xsrc, l):
            phase_a(xsrc, 0, l)
            P.new_epoch("RA")
            for sec, c0 in ((0, OFF_DIFF + 512), (1, 512)):
                s, wk = wslot()
                wv = wbuf[s][:, 0:4096].rearrange("p (k n) -> p k n", k=8)
                load_w(wv, win_cols(l, c0, 512), wk)
                for c in range(4):
                    st = ar((c % 2) * 2048, 2048)
                    skeys = lambda tg, c=c: [("ar", c % 2, "s", tg)]
                    proj_fm(lambda tg, st=st: st[:, tg * 512:(tg + 1) * 512], lambda k, c=c: wv[:, k, c * 128:(c + 1) * 128], 8,
                            hT_rhs, lambda tg: hT_keys(tg) + wk, skeys)
                    dma("sync", cf[l][sec * 4 + c], st,
                        [("ar", c % 2, "s", tg) for tg in range(4)], [("cf", l, sec * 4 + c)])
            sa, wka = wslot()
            wa = wbuf[sa][:, 0:4096].rearrange("p (k n) -> p k n", k=8)
            load_w(wa, win_cols(l, OFF_CONV, 512), wka)
            sg_, wkg = wslot()
            wg = wbuf[sg_][:, 0:4096].rearrange("p (k n) -> p k n", k=8)
            load_w(wg, win_cols(l, OFF_CONV + 512, 512), wkg)
            for c in range(4):
                st = ar((c % 2) * 2048, 2048)
                for tg in range(4):
                    glu_group(lambda k: wa[:, k, c * 128:(c + 1) * 128], lambda k: wg[:, k, c * 128:(c + 1) * 128],
                              wka + wkg, tg, st[:, tg * 512:(tg + 1) * 512], [("ar", c % 2, "s", tg)])
                dma("sync", cf[l][8 + c], st,
                    [("ar", c % 2, "s", tg) for tg in range(4)], [("cf", l, 8 + c)])
            sv, wkv = wslot()
            wv1 = wbuf[sv][:, 0:4096].rearrange("p (k n) -> p k n", k=8)
            load_w(wv1, win_cols(l, OFF_DIFF + 1024, 512), wkv)
            sn, wkn = wslot()
            wv2 = wbuf[sn][:, 0:4096].rearrange("p (k n) -> p k n", k=8)
            load_w(wv2, win_cols(l, 1024, 512), wkn)
            stg = ar(4096, 16384).rearrange("p (t n) -> p t n", t=16)
            for t in range(NT):
                sk = [("ar", "v", t)]
                b = nbank()
                for k in range(8):
                    mm(bank(b), hT[:, k, t * 128:(t + 1) * 128], wv1[:, k, :], k == 0, k == 7, [("hT", t)] + wkv, [("ps", b)])
                evac(stg[:, t, 0:512], bank(b), [("ps", b)], sk)
                b = nbank()
                for k in range(8):
                    mm(bank(b), hT[:, k, t * 128:(t + 1) * 128], wv2[:, k, :], k == 0, k == 7, [("hT", t)] + wkn, [("ps", b)])
                evac(stg[:, t, 512:1024], bank(b), [("ps", b)], sk)
            for j_ in range(8):
                dma("sync", ct[l][j_].rearrange("(t p) n -> p t n", p=128), stg[:, :, j_ * 128:(j_ + 1) * 128],
                    [("ar", "v", t) for t in range(NT)], [("ct", l, j_)])
            rg = [[0, 1, 2, 3], [4, 5, 6, 7]]
            for s_ in range(12):
                P.add("gpsimd", lambda e, s_=s_: e.collective_compute("AllGather", ALU.bypass, replica_groups=rg,
                                                                      ins=[cf[l][s_].opt()], outs=[tfm[l][s_].opt()]),
                      [("cf", l, s_)], [("tfm", l, s_)], kind="cc")
            for j_ in range(8):
                P.add("gpsimd", lambda e, j_=j_: e.collective_compute("AllGather", ALU.bypass, replica_groups=rg,
                                                                      ins=[ct[l][j_].opt()], outs=[ttm[l][j_].opt()]),
                      [("ct", l, j_)], [("ttm", l, j_)], kind="cc")

        def tfv(l, s_):
            return tfm[l][s_].rearrange("(r m) n -> r m n", r=4)

        def ttv(l, j_):
            return ttm[l][j_].rearrange("(r t p) n -> r p t n", r=4, p=128)

        def glu_group(wa_fn, wg_fn, wkeys, tg, dst, dkeys):
            ba = nbank()
            for k in range(8):
                mm(bank(ba), wa_fn(k), hT_rhs(k, tg), k == 0, k == 7, hT_keys(tg) + wkeys, [("ps", ba)])
            bg = nbank()
            for k in range(8):
                mm(bank(bg), wg_fn(k), hT_rhs(k, tg), k == 0, k == 7, hT_keys(tg) + wkeys, [("ps", bg)])
            ti = nxt("tmp", 4)
            act(tmp[ti][:, 0:512], bank(bg), AF.Sigmoid, [("ps", bg)], [("tmp", ti)])
            tt(dst, bank(ba), tmp[ti][:, 0:512], ALU.mult, [("ps", ba), ("tmp", ti)], dkeys)

        def na_phase(u, l, is_sample):
            P.new_epoch("RA")
            P.new_epoch("RB")
            nab_src = nabs_d if is_sample else nabp_d
            for c in range(4):
                base = (c % 2) * 7232
                qT = ar(base, 2048)
                kT = ar(base + 2048, 2560)
                va = ar(base + 4608, 2600).rearrange("p (t h e) -> p t h e", t=20, h=2)
                otm = ar(14464 + (c % 2) * 2048, 2048).rearrange("p (i f) -> p i f", i=16)
                kq, kk, kv, ko = ("na", c % 2, "q"), ("na", c % 2, "k"), ("na", c % 2, "v"), ("na", c % 2, "o")
                s, wk = wslot()
                wv = wbuf[s][:, 0:3072].rearrange("p (k b n) -> p k b n", k=8, b=3)
                nb_w = 1 if is_sample else 3
                wsrc = w_in[l].rearrange("(kc p) (b n) -> p kc b n", p=128, n=512)[:, :, 0:nb_w, c * 128:(c + 1) * 128]
                load_w(wv[:, :, 0:nb_w, :], wsrc, wk)
                proj_fm(lambda tg: qT[:, tg * 512:(tg + 1) * 512], lambda k: wv[:, k, 0, :], 8, hT_rhs,
                        lambda tg: hT_keys(tg) + wk, lambda tg: [kq + (tg,)])
                if not is_sample:
                    proj_fm(lambda tg: kT[:, 256 + tg * 512:256 + (tg + 1) * 512], lambda k: wv[:, k, 1, :], 8, hT_rhs,
                            lambda tg: hT_keys(tg) + wk, lambda tg: [kk + (tg,)])
                    for t4 in range(4):
                        b = nbank()
                        for tl in range(4):
                            t = t4 * 4 + tl
                            for k in range(8):
                                mm(bank(b)[:, tl * 128:(tl + 1) * 128], hT[:, k, t * 128:(t + 1) * 128], wv[:, k, 2, :],
                                   k == 0, k == 7, [("hT", t)] + wk, [("ps", b)])
                        evac(va[:, 2 + t4 * 4:6 + t4 * 4, :, 0:64], bank(b).rearrange("p (t h e) -> p t h e", t=4, h=2),
                             [("ps", b)], [kv + (t4,)])
                    memset(va[:, :, :, 64:65], 1.0, [kv + ("ones",)])
                    kkeys_all = [kk + (tg,) for tg in range(4)]
                    vkeys_all = [kv + (t4,) for t4 in range(4)] + [kv + ("ones",)]
                else:
                    fk, fv = ("tfm", l, 4 + c), ("ttm", l, 4 + c)
                    dma_dyn(kT[:, 256:2304], lambda e, c=c: tfv(l, 4 + c)[rank_of(e, 0)][:, :], [fk], [kk + (0,)])
                    dma_dyn(kT[:, 0:256], lambda e, c=c: tfv(l, 4 + c)[rank_of(e, 3)][:, 1792:2048], [fk], [kk + (1,)])
                    dma_dyn(kT[:, 2304:2560], lambda e, c=c: tfv(l, 4 + c)[rank_of(e, 1)][:, 0:256], [fk], [kk + (2,)])
                    for hh_ in range(2):
                        hs = slice(hh_ * 64, (hh_ + 1) * 64)
                        dma_dyn(va[:, 2:18, hh_, 0:64], lambda e, c=c, hs=hs: ttv(l, 4 + c)[rank_of(e, 0)][:, :, hs], [fv], [kv + (0, hh_)])
                        dma_dyn(va[:, 0:2, hh_, 0:64], lambda e, c=c, hs=hs: ttv(l, 4 + c)[rank_of(e, 3)][:, 14:16, hs], [fv], [kv + (1, hh_)])
                        dma_dyn(va[:, 18:20, hh_, 0:64], lambda e, c=c, hs=hs: ttv(l, 4 + c)[rank_of(e, 1)][:, 0:2, hs], [fv], [kv + (2, hh_)])
                    memset(va[:, :, :, 64:65], 1.0, [kv + ("ones",)])
                    kkeys_all = [kk + (i,) for i in range(3)]
                    vkeys_all = [kv + (i, hh_) for i in range(3) for hh_ in range(2)] + [kv + ("ones",)]
                for hh in range(2):
                    h = 2 * c + hh
                    pb_ = hh * 64
                    dma("sync", biasbuf[:, 0:3456].rearrange("p (s n) -> p s n", s=27), nab_src[l, h], [], [("bias",)])
                    nabv = biasbuf[:, 0:3456]
                    def na_S(i):
                        goff, jrels = NAB_GRP[_grp_of_block(i)]
                        slots = [(s_, jr) for s_, jr in enumerate(jrels) if is_sample or 0 <= i + jr <= 15]
                        ns = len(slots)
                        s0 = slots[0][0]
                        b = nbank2()
                        for n_, (s_, jr) in enumerate(slots):
                            co = 256 + (i + jr) * 128
                            mm(ps[:, b * 512 + n_ * 128: b * 512 + (n_ + 1) * 128], kT[pb_:pb_ + 64, co:co + 128],
                               qT[pb_:pb_ + 64, i * 128:(i + 1) * 128], True, True,
                               kkeys_all + [kq + (i // 4,)], [("ps", b + n_ // 4)])
                        pkeys = [("ps", b)] + ([("ps", b + 1)] if ns > 4 else [])
                        ti = nxt("tmp", 4)
                        stt(tmp[ti][:, 0:ns * 128], ps[:, b * 512:b * 512 + ns * 128], SCALE,
                            nabv[:, (goff + s0) * 128:(goff + s0 + ns) * 128], ALU.mult, ALU.add,
                            pkeys + [("bias",)], [("tmp", ti)])
                        pi = nxt("pT", 4)
                        act(pT[pi][:, 0:ns * 128], tmp[ti][:, 0:ns * 128], AF.Exp, [("tmp", ti)], [("pT", pi)])
                        return (i, slots, pi)

                    def na_PV(ctx):
                        i, slots, pi = ctx
                        ns = len(slots)
                        bo = nbank()
                        for n_, (s_, jr) in enumerate(slots):
                            mm(bank(bo, 65), pT[pi][:, n_ * 128:(n_ + 1) * 128], va[:, 2 + i + jr, hh, :], n_ == 0, n_ == ns - 1,
                               [("pT", pi)] + vkeys_all, [("ps", bo)])
                        rc, rk = col()
                        recip(rc, bank(bo)[:, 64:65], [("ps", bo)], [rk])
                        tsc(otm[:, i, hh * 64:(hh + 1) * 64], bank(bo)[:, 0:64], rc, None, ALU.mult, None,
                            [("ps", bo), rk], [ko + (i, hh)])

                    ctxs = {}
                    for j_ in range(-1, 16):
                        if j_ + 1 < 16:
                            ctxs[j_ + 1] = na_S(j_ + 1)
                        if j_ >= 0:
                            na_PV(ctxs.pop(j_))
                for i4 in range(4):
                    b = nbank()
                    pbv = bank(b).bitcast(BF16)
                    for il in range(4):
                        i = i4 * 4 + il
                        tr(pbv[:, il * 128:(il + 1) * 128], otm[:, i, :], [ko + (i, 0), ko + (i, 1)], [("ps", b)])
                    evac(brT[:, c * U + i4 * 512: c * U + (i4 + 1) * 512], pbv[:, 0:512], [("ps", b)], [("brT", c, i4)])

        def conv_phase(u, l, is_sample):
            P.new_epoch("RA")
            acc4 = arena[:, 0:16384].bitcast(F32).rearrange("p (c n) -> p c n", c=4)
            for c in range(4):
                g = ar(16384 + (c % 2) * 2080, 2078)
                gk = ("cv", "g", c % 2)
                if not is_sample:
                    s, wk = wslot()
                    wv = wbuf[s][:, 0:2048].rearrange("p (k b n) -> p k b n", k=8, b=2)
                    wsrc = w_in[l][:, OFF_CONV:OFF_CONV + 1024].rearrange("(kc p) (b n) -> p kc b n", p=128, n=512)[:, :, :, c * 128:(c + 1) * 128]
                    load_w(wv, wsrc, wk)
                    memset(g[:, 0:15], 0.0, [gk + ("h0",)])
                    memset(g[:, 2063:2078], 0.0, [gk + ("h1",)])
                    for tg in range(4):
                        glu_group(lambda k: wv[:, k, 0, :], lambda k: wv[:, k, 1, :], wk, tg,
                                  g[:, 15 + tg * 512: 15 + (tg + 1) * 512], [gk + (tg,)])
                    gkeys = [gk + (tg,) for tg in range(4)] + [gk + ("h0",), gk + ("h1",)]
                else:
                    fk = ("tfm", l, 8 + c)
                    dma_dyn(g[:, 15:2063], lambda e, c=c: tfv(l, 8 + c)[rank_of(e, 0)][:, :], [fk], [gk + (0,)])
                    dma_dyn(g[:, 0:15], lambda e, c=c: tfv(l, 8 + c)[rank_of(e, 3)][:, 2033:2048], [fk], [gk + ("h0",)])
                    dma_dyn(g[:, 2063:2078], lambda e, c=c: tfv(l, 8 + c)[rank_of(e, 1)][:, 0:15], [fk], [gk + ("h1",)])
                    tsc(g[:, 0:15], g[:, 0:15], cflag[:, 0:1], None, ALU.mult, None, [gk + ("h0",), ("cflag",)], [gk + ("h0",)])
                    tsc(g[:, 2063:2078], g[:, 2063:2078], cflag[:, 1:2], None, ALU.mult, None, [gk + ("h1",), ("cflag",)], [gk + ("h1",)])
                    gkeys = [gk + (0,), gk + ("h0",), gk + ("h1",)]
                ak = ("cv", "acc", c)
                wc = PC_CW + c * 31
                dg = biasbuf[:, :].bitcast(BF16)[:, (c % 2) * 3968:(c % 2 + 1) * 3968].rearrange("p (j n) -> p j n", j=31)
                dk = ("dg", c % 2)
                tt(dg, identf[:].unsqueeze(1).broadcast_to([128, 31, 128]),
                   pcol[:, wc:wc + 31].unsqueeze(2).broadcast_to([128, 31, 128]), ALU.mult,
                   [("identf",), ("pcol",)], [dk, ("bias",)])
                for tg in range(4):
                    b = nbank()
                    for j in range(31):
                        mm(bank(b), dg[:, j, :], g[:, j + tg * 512: j + tg * 512 + 512], j == 0, j == 30, gkeys + [dk, ("bias",)], [("ps", b)])
                    tsc(acc4[:, c, tg * 512:(tg + 1) * 512], bank(b), pcol[:, PC_CB + c:PC_CB + c + 1], None, ALU.add, None,
                        [("ps", b), ("pcol",)], [ak + (tg,)])
            for tg in range(4):
                akeys = [("cv", "acc", c, tg) for c in range(4)]
                sl = slice(tg * 512, (tg + 1) * 512)
                bm = nbank()
                for c in range(4):
                    mm(bank(bm), o512_f[:], acc4[:, c, sl], c == 0, c == 3, akeys + [("o512_f",)], [("ps", bm)])
                bs = nbank()
                for c in range(4):
                    ti = nxt("tmp", 4)
                    act(tmp[ti][:, 0:512], acc4[:, c, sl], AF.Square, akeys, [("tmp", ti)])
                    mm(bank(bs), o512_f[:], tmp[ti][:, 0:512], c == 0, c == 3, [("tmp", ti), ("o512_f",)], [("ps", bs)])
                tm = nxt("tmp", 4)
                P.add("scalar", lambda e, tm=tm, bm=bm: e.copy(out=tmp[tm][:, 0:512], in_=bank(bm)), [("ps", bm)], [("tmp", tm)])
                tv = nxt("tmp", 4)
                tt(tmp[tv][:, 0:512], tmp[tm][:, 0:512], tmp[tm][:, 0:512], ALU.mult, [("tmp", tm)], [("tmp", tv)])
                tt(tmp[tv][:, 0:512], bank(bs), tmp[tv][:, 0:512], ALU.subtract, [("ps", bs), ("tmp", tv)], [("tmp", tv)])
                act(tmp[tv][:, 0:512], tmp[tv][:, 0:512], AF.Ln, [("tmp", tv), ("epsc",)], [("tmp", tv)], bias=epsc[:, 0:1])
                act(tmp[tv][:, 0:512], tmp[tv][:, 0:512], AF.Exp, [("tmp", tv)], [("tmp", tv)], scale=-0.5)
                for c in range(4):
                    ty = nxt("tmp", 4)
                    while ty in (tm, tv):
                        ty = nxt("tmp", 4)
                    tt(tmp[ty][:, 0:512], acc4[:, c, sl], tmp[tm][:, 0:512], ALU.subtract, akeys + [("tmp", tm)], [("tmp", ty)])
                    tt(tmp[ty][:, 0:512], tmp[ty][:, 0:512], tmp[tv][:, 0:512], ALU.mult, [("tmp", ty), ("tmp", tv)], [("tmp", ty)])
                    act(brT[:, (4 + c) * U + tg * 512:(4 + c) * U + (tg + 1) * 512], tmp[ty][:, 0:512], AF.Silu,
                        [("tmp", ty), ("pcol",)], [("brT", 4 + c, tg)],
                        scale=pcol[:, PC_LNG + c:PC_LNG + c + 1], bias=pcol[:, PC_LNB + c:PC_LNB + c + 1])

        def diff_phase(u, l, is_sample):
            P.new_epoch("RA")
            rr["psn"] = 4
            rr["ps"] = 0
            nkc = 64 if is_sample else 16
            for h in range(4):
                if is_sample:
                    qT, kT = ar(0, 2048), ar(2048, 8192)
                    vh = ar(10240, 8192).rearrange("p (t e) -> p t e", t=64)
                    par = 0
                else:
                    par = h % 2
                    base = par * 6144
                    qT, kT = ar(base, 2048), ar(base + 2048, 2048)
                    vh = ar(base + 4096, 2048).rearrange("p (t e) -> p t e", t=16)
                kq, kk, kv = ("df", par, "q"), ("df", par, "k"), ("df", par, "v")
                s, wk = wslot()
                wv = wbuf[s][:, 0:3072].rearrange("p (k b n) -> p k b n", k=8, b=3)
                nb_w = 1 if is_sample else 3
                wsrc = w_in[l][:, OFF_DIFF:OFF_DIFF + 1536].rearrange("(kc p) (b n) -> p kc b n", p=128, n=512)[:, :, 0:nb_w, h * 128:(h + 1) * 128]
                load_w(wv[:, :, 0:nb_w, :], wsrc, wk)
                proj_fm(lambda tg: qT[:, tg * 512:(tg + 1) * 512], lambda k: wv[:, k, 0, :], 8, hT_rhs,
                        lambda tg: hT_keys(tg) + wk, lambda tg: [kq + (tg,)])
                if not is_sample:
                    proj_fm(lambda tg: kT[:, tg * 512:(tg + 1) * 512], lambda k: wv[:, k, 1, :], 8, hT_rhs,
                            lambda tg: hT_keys(tg) + wk, lambda tg: [kk + (tg,)])
                    for t4 in range(4):
                        b = nbank()
                        for tl in range(4):
                            t = t4 * 4 + tl
                            for k in range(8):
                                mm(bank(b)[:, tl * 128:(tl + 1) * 128], hT[:, k, t * 128:(t + 1) * 128], wv[:, k, 2, :],
                                   k == 0, k == 7, [("hT", t)] + wk, [("ps", b)])
                        evac(vh[:, t4 * 4:t4 * 4 + 4, :], bank(b).rearrange("p (t e) -> p t e", t=4), [("ps", b)], [kv + (t4,)])
                    kkeys_all = [kk + (tg,) for tg in range(4)]
                    vkeys_all = [kv + (t4,) for t4 in range(4)]
                else:
                    for ro in range(4):
                        dma_dyn(kT[:, ro * 2048:(ro + 1) * 2048], lambda e, ro=ro, h=h: tfv(l, h)[rank_of(e, ro)][:, :],
                                [("tfm", l, h)], [kk + (ro,)])
                        dma_dyn(vh[:, ro * 16:(ro + 1) * 16, :], lambda e, ro=ro, h=h: ttv(l, h)[rank_of(e, ro)],
                                [("ttm", l, h)], [kv + (ro,)])
                    kkeys_all = [kk + (ro,) for ro in range(4)]
                    vkeys_all = [kv + (ro,) for ro in range(4)]
                dma("sync", biasbuf[:, 0:3072].rearrange("p (m n) -> p m n", m=6), t5t_d[h], [], [("bias",)])
                if is_sample:
                    dma("sync", biasbuf[:, 3072:4096].rearrange("p (m n) -> p m n", m=2), t5j_d[h], [], [("bias",)])
                def df_class(qg, kc):
                    ro, kcl = kc // 16, kc % 16
                    special = None
                    cb = None
                    cbk = None
                    if ro == 0:
                        d = kcl * 128 - qg * 512
                        if -128 <= d <= 512:
                            special = (d + 128) // 128
                        else:
                            cb = t5c[:, h, 0:1] if d < 0 else t5c[:, h, 1:2]
                            cbk = ("t5c",)
                    elif ro == 1 and kcl == 0 and qg == 3:
                        special = 6
                    elif ro == 3 and kcl == 15 and qg == 0:
                        special = 7
                    else:
                        cb = t5cs[:, h, ro:ro + 1]
                        cbk = ("t5cs",)
                    return special, cb, cbk

                def df_S(qg, kc, c):
                    special, cb, cbk = df_class(qg, kc)
                    b = nxt("dfs", 4)
                    mm(bank(b), kT[c * 64:(c + 1) * 64, kc * 128:(kc + 1) * 128], qT[c * 64:(c + 1) * 64, qg * 512:(qg + 1) * 512],
                       True, True, kkeys_all + [kq + (qg,)], [("ps", b)])
                    pi = nxt("pT", 4)
                    if special is not None:
                        ti = nxt("tmp", 4)
                        stt(tmp[ti][:, 0:512], bank(b), SCALE, biasbuf[:, special * 512:(special + 1) * 512], ALU.mult, ALU.add,
                            [("ps", b), ("bias",)], [("tmp", ti)])
                        act(pT[pi][:, 0:512], tmp[ti][:, 0:512], AF.Exp, [("tmp", ti)], [("pT", pi)])
                    else:
                        act(pT[pi][:, 0:512], bank(b), AF.Exp, [("ps", b), cbk], [("pT", pi)], scale=SCALE, bias=cb)
                    return pi

                def df_PV(qg, kc, c, pi):
                    mm(bank(4 + c), vh[:, kc, :], pT[pi][:, 0:512], kc == 0, kc == nkc - 1, vkeys_all + [("pT", pi)], [("ps", 4 + c)])
                    mm(bank(6 + c), ones_bf[:], pT[pi][:, 0:512], kc == 0, kc == nkc - 1, [("ones_bf",), ("pT", pi)], [("ps", 6 + c)])

                def df_post(qg):
                    tA, tB, tC, tD = 0, 1, 2, 3
                    cpy(tmp[tA][:, 0:512], bank(4), [("ps", 4)], [("tmp", tA)])
                    P.add("scalar", lambda e: e.copy(out=tmp[tC][:, 0:512], in_=bank(6)), [("ps", 6)], [("tmp", tC)])
                    cpy(tmp[tB][:, 0:512], bank(5), [("ps", 5)], [("tmp", tB)])
                    P.add("scalar", lambda e: e.copy(out=tmp[tD][:, 0:512], in_=bank(7)), [("ps", 7)], [("tmp", tD)])
                    A_, B_, C_, D_ = tmp[tA][:, 0:512], tmp[tB][:, 0:512], tmp[tC][:, 0:512], tmp[tD][:, 0:512]
                    kA, kB, kC, kD = ("tmp", tA), ("tmp", tB), ("tmp", tC), ("tmp", tD)
                    dst = brT[:, (8 + h) * U + qg * 512:(8 + h) * U + (qg + 1) * 512]
                    bb = [None]

                    def s1():
                        act(C_, C_, AF.Ln, [kC], [kC])
                        act(D_, D_, AF.Ln, [kD], [kD])

                    def s2():
                        act(C_, C_, AF.Exp, [kC], [kC], scale=-1.0)
                        act(D_, D_, AF.Exp, [kD], [kD], scale=-1.0)

                    def s3():
                        tt(A_, A_, C_, ALU.mult, [kA, kC], [kA])
                        tt(B_, B_, D_, ALU.mult, [kB, kD], [kB])

                    def s4():
                        stt(A_, B_, small[:, 2:3], A_, ALU.mult, ALU.add, [kA, kB, ("lamc",)], [kA])

                    def s5():
                        act(C_, A_, AF.Square, [kA], [kC])

                    def s6():
                        bb[0] = nxt("dfs", 4)
                        mm(bank(bb[0]), ones_f[:], C_, True, True, [kC, ("ones_f",)], [("ps", bb[0])])

                    def s7():
                        act(C_, bank(bb[0]), AF.Ln, [("ps", bb[0]), ("epsc",)], [kC], bias=epsc[:, 0:1])

                    def s8():
                        act(C_, C_, AF.Exp, [kC], [kC], scale=-0.5)

                    def s9():
                        stt(dst, A_, small[:, 3:4], C_, ALU.mult, ALU.mult, [kA, kC, ("lamc",)], [("brT", 8 + h, qg)])

                    return [s1, s2, s3, s4, s5, s6, s7, s8, s9]

                steps = []
                for qg in range(4):
                    far = [kc for kc in range(nkc) if df_class(qg, kc)[0] is None]
                    spc = [kc for kc in range(nkc) if df_class(qg, kc)[0] is not None]
                    order = far + spc
                    for pos, kc in enumerate(order):
                        steps.append((qg, kc, pos == 0, pos == nkc - 1))
                DL = 1
                pend = {}
                post_q = []
                for j_ in range(-DL, len(steps)):
                    if j_ + DL < len(steps):
                        qg2, kc2, _, _ = steps[j_ + DL]
                        pend[j_ + DL] = (df_S(qg2, kc2, 0), df_S(qg2, kc2, 1))
                    if j_ >= 0:
                        qg, kc, first, last = steps[j_]
                        p0, p1 = pend.pop(j_)
                        mm(bank(4), vh[:, kc, :], pT[p0][:, 0:512], first, last, vkeys_all + [("pT", p0)], [("ps", 4)])
                        mm(bank(5), vh[:, kc, :], pT[p1][:, 0:512], first, last, vkeys_all + [("pT", p1)], [("ps", 5)])
                        mm(bank(6), ones_bf[:], pT[p0][:, 0:512], first, last, [("ones_bf",), ("pT", p0)], [("ps", 6)])
                        mm(bank(7), ones_bf[:], pT[p1][:, 0:512], first, last, [("ones_bf",), ("pT", p1)], [("ps", 7)])
                        if post_q:
                            post_q.pop(0)()
                        if last:
                            while post_q:
                                post_q.pop(0)()
                            post_q = df_post(qg)
                while post_q:
                    post_q.pop(0)()
            rr["ps"] = 0
            rr["psn"] = 8

        def merge_phase(u, l, xsrc, xdst_mid):
            P.new_epoch("RA")
            mT = arena[:, 0:16384].rearrange("p (k n) -> p k n", k=8)
            for dc in range(8):
                s, wk = wslot()
                wg = wbuf[s][:, 0:3072].rearrange("p (k b n) -> p k b n", k=8, b=3)
                wbr = wbuf[s][:, 3072:4608].rearrange("p (b k n) -> p b k n", b=3, k=4)
                load_w(wg, w_in[l][:, OFF_GATE:].rearrange("(kc p) (b n) -> p kc b n", p=128, n=1024)[:, :, :, dc * 128:(dc + 1) * 128], wk)
                load_w(wbr, w_branch[l].rearrange("b (k p) n -> p b k n", p=128)[:, :, :, dc * 128:(dc + 1) * 128], wk)
                for tg in range(4):
                    tacc = nxt("tmp", 4)
                    for b_ in range(3):
                        bg = nbank()
                        for k in range(8):
                            mm(bank(bg), wg[:, k, b_, :], hT_rhs(k, tg), k == 0, k == 7, hT_keys(tg) + wk, [("ps", bg)])
                        bp = nbank()
                        for k in range(4):
                            mm(bank(bp), wbr[:, b_, k, :], brT[:, (b_ * 4 + k) * U + tg * 512:(b_ * 4 + k) * U + (tg + 1) * 512],
                               k == 0, k == 3, [("brT", b_ * 4 + k, tg)] + wk, [("ps", bp)])
                        tg_ = nxt("tmp", 4)
                        while tg_ == tacc:
                            tg_ = nxt("tmp", 4)
                        bc = PC_BGATE + b_ * 8 + dc
                        act(tmp[tg_][:, 0:512], bank(bg), AF.Sigmoid, [("ps", bg), ("pcol",)], [("tmp", tg_)], bias=pcol[:, bc:bc + 1])
                        if b_ == 0:
                            tt(tmp[tacc][:, 0:512], bank(bp), tmp[tg_][:, 0:512], ALU.mult, [("ps", bp), ("tmp", tg_)], [("tmp", tacc)])
                        else:
                            tt(tmp[tg_][:, 0:512], bank(bp), tmp[tg_][:, 0:512], ALU.mult, [("ps", bp), ("tmp", tg_)], [("tmp", tg_)])
                            if b_ == 1:
                                tt(tmp[tacc][:, 0:512], tmp[tacc][:, 0:512], tmp[tg_][:, 0:512], ALU.add,
                                   [("tmp", tacc), ("tmp", tg_)], [("tmp", tacc)])
                            else:
                                tt(mT[:, dc, tg * 512:(tg + 1) * 512], tmp[tacc][:, 0:512], tmp[tg_][:, 0:512], ALU.add,
                                   [("tmp", tacc), ("tmp", tg_)], [("mT", dc, tg)])
            wo = biasbuf[:, :].bitcast(BF16).rearrange("p (k n) -> p k n", k=8)
            load_w(wo, w_out[l].rearrange("(k p) n -> p k n", p=128), [("bias",)])
            LA = 2
            held = {}
            for t in range(NT + LA):
                if t < NT:
                    xi = nxt("x", 3)
                    dma("sync", xt[xi][:], xsrc[u, t * 128:(t + 1) * 128, :], [("xd", l, u, t)], [("xt", xi)])
                    b = nbank2()
                    for nh in range(2):
                        for k in range(8):
                            mm(bank(b + nh), mT[:, k, t * 128:(t + 1) * 128], wo[:, k, nh * 512:(nh + 1) * 512], k == 0, k == 7,
                               [("mT", k, t // 4), ("bias",)], [("ps", b + nh)])
                    resid_tail(ps[:, b * 512:(b + 2) * 512], [("ps", b), ("ps", b + 1)], xi, 0, l)
                    dma("sync", xdst_mid[u, t * 128:(t + 1) * 128, :], xt[xi][:], [("xt", xi)], [("xm", l, u, t)])
                    held[t] = xi
                if t - LA >= 0:
                    xj = held.pop(t - LA)
                    prenorm_tile(xt[xj][:], ("xt", xj), t - LA, PC_GFFN)

        def resid_tail(y_ps, ykeys, xi, which, l):
            ss, sk = col()
            tj = nxt("tmp", 4)
            act(tmp[tj][:, 0:512].bitcast(BF16), y_ps, AF.Square, ykeys, [sk, ("tmp", tj)], accum_out=ss)
            rs, rk = col()
            rsqrt_col(rs, rk, ss, sk, 1.0 / D)
            stt(ytmp[:], y_ps, rs, gpost[:, which, :], ALU.mult, ALU.mult, ykeys + [rk, ("gpost",)], [("ytmp",)])
            tt(xt[xi][:], xt[xi][:], ytmp[:], ALU.add, [("xt", xi), ("ytmp",)], [("xt", xi)])

        def ffn_phase(u, l, xsrc_mid, xdst, next_a=None):
            P.new_epoch("RB")
            wfo = brT[:, 0:22528].rearrange("p (k n) -> p k n", k=NHC)
            for k0 in range(0, NHC, 4):
                k1 = min(NHC, k0 + 4)
                load_w(wfo[:, k0:k1, :], w_ffn_out[l].rearrange("(k p) n -> p k n", p=128)[:, k0:k1, :], [("wfo", k0)])
            wfo_keys = [("wfo", k0) for k0 in range(0, NHC, 4)] + [("wfo",)]
            aT = arena[:, 0:22528].rearrange("p (k n) -> p k n", k=NHC)
            for half in range(2):
                if half == 0:
                    P.new_epoch("RA")
                for hc in range(NHC):
                    s, wk = wslot()
                    wv = wbuf[s][:, 0:2048].rearrange("p (k b n) -> p k b n", k=8, b=2)
                    load_w(wv, w_ffn_in[l].rearrange("(kc p) (b n) -> p kc b n", p=128, n=FFN_H)[:, :, :, hc * 128:(hc + 1) * 128], wk)
                    for t2 in range(2):
                        tg = half * 2 + t2
                        bg = nbank()
                        for k in range(8):
                            mm(bank(bg), wv[:, k, 0, :], hT_rhs(k, tg), k == 0, k == 7, hT_keys(tg) + wk, [("ps", bg)])
                        bu = nbank()
                        for k in range(8):
                            mm(bank(bu), wv[:, k, 1, :], hT_rhs(k, tg), k == 0, k == 7, hT_keys(tg) + wk, [("ps", bu)])
                        ti = nxt("tmp", 4)
                        act(tmp[ti][:, 0:512], bank(bg), AF.Silu, [("ps", bg)], [("tmp", ti)])
                        tt(aT[:, hc, t2 * 512:(t2 + 1) * 512], bank(bu), tmp[ti][:, 0:512], ALU.mult,
                           [("ps", bu), ("tmp", ti)], [("aT", hc, t2)])
                if half == 1 and next_a is not None:
                    next_a()
                for tl in range(8):
                    t = half * 8 + tl
                    xi = nxt("x", 3)
                    dma("sync", xt[xi][:], xsrc_mid[u, t * 128:(t + 1) * 128, :], [("xm", l, u, t)], [("xt", xi)])
                    b = nbank2()
                    for nh in range(2):
                        for hc in range(NHC):
                            mm(bank(b + nh), aT[:, hc, tl * 128:(tl + 1) * 128], wfo[:, hc, nh * 512:(nh + 1) * 512],
                               hc == 0, hc == NHC - 1, [("aT", hc, tl // 4)] + wfo_keys, [("ps", b + nh)])
                    resid_tail(ps[:, b * 512:(b + 2) * 512], [("ps", b), ("ps", b + 1)], xi, 1, l)
                    dma("sync", xdst[u, t * 128:(t + 1) * 128, :], xt[xi][:], [("xt", xi)], [("xd", l + 1, u, t)])

        def layer_setup(l):
            lam_init = 0.8 - 0.6 * math.exp(-0.3 * l)
            dma("sync", pcol[:], pcol_d[l], [], [("pcol",)])
            dma("sync", gpost[:, 0, :], prow_d[l, 0:1, :].broadcast_to([128, D]), [], [("gpost",)])
            dma("sync", gpost[:, 1, :], prow_d[l, 1:2, :].broadcast_to([128, D]), [], [("gpost",)])
            dma("sync", lamb, lam_d[l:l + 1, :].broadcast_to([128, 256]), [], [("ytmp",)])
            P.add("vector", lambda e: e.tensor_tensor(out=lamb[:, 0:64], in0=lamb[:, 0:64], in1=lamb[:, 64:128], op=ALU.mult), [("ytmp",)], [("ytmp",)])
            P.add("vector", lambda e: e.tensor_tensor(out=lamb[:, 128:192], in0=lamb[:, 128:192], in1=lamb[:, 192:256], op=ALU.mult), [("ytmp",)], [("ytmp",)])
            P.add("vector", lambda e: e.reduce_sum(out=small[:, 0:1], in_=lamb[:, 0:64], axis=mybir.AxisListType.X), [("ytmp",)], [("lamc",)])
            P.add("vector", lambda e: e.reduce_sum(out=small[:, 1:2], in_=lamb[:, 128:192], axis=mybir.AxisListType.X), [("ytmp",), ("lamc",)], [("lamc",)])
            act(small[:, 0:2], small[:, 0:2], AF.Exp, [("lamc",)], [("lamc",)])
            stt(small[:, 2:3], small[:, 0:1], -1.0, small[:, 1:2], ALU.mult, ALU.add, [("lamc",)], [("lamc",)])
            tsc(small[:, 2:3], small[:, 2:3], -lam_init, None, ALU.add, None, [("lamc",)], [("lamc",)])
            tsc(small[:, 3:4], pcol[:, PC_SUB:PC_SUB + 1], 1.0 - lam_init, None, ALU.mult, None, [("pcol",), ("lamc",)], [("lamc",)])

        for l in range(depth):
            xsrc = xin if l == 0 else x1
            xdst = yout if l == depth - 1 else x1
            layer_setup(l)
            if with_sample:
                P.mark(f"L{l} PRE")
                sample_prepass(xsrc, l)
            units = list(range(1, NU)) + ([0] if with_sample else [])
            for ui, u in enumerate(units):
                smp = (u == 0)
                if ui == 0:
                    P.mark(f"L{l}U{u} A")
                    phase_a(xsrc, u, l)
                P.mark(f"L{l}U{u} NA")
                na_phase(u, l, smp)
                P.mark(f"L{l}U{u} CV")
                conv_phase(u, l, smp)
                P.mark(f"L{l}U{u} DF")
                diff_phase(u, l, smp)
                P.mark(f"L{l}U{u} MG")
                merge_phase(u, l, xsrc, xmid)
                P.mark(f"L{l}U{u} FF")
                nxt_a = None
                if ui + 1 < len(units):
                    nu_ = units[ui + 1]
                    nxt_a = (lambda nu_=nu_: phase_a(xsrc, nu_, l))
                ffn_phase(u, l, xmid, xdst, nxt_a)
        P.mark("END")

        P.finalize()
        P.alloc_sems(nc, es)
        block = es.enter_context(nc.Block())

        @block.sync
        def _(e):
            P.emit("sync", e, final_wait=True)

        @block.gpsimd
        def _(e):
            P.emit("gpsimd", e)

        @block.tensor
        def _(e):
            P.emit("tensor", e)

        @block.vector
        def _(e):
            P.emit("vector", e)

        @block.scalar
        def _(e):
            P.emit("scalar", e)

    return nc, P


def prepare_inputs(inp, n_prompt=4, cores=8):
    t5 = np.asarray(inp["t5_bias"], np.float32)
    p = np.arange(128)[:, None]
    j = np.arange(512)[None, :]
    t5t = np.empty((4, 128, 6, 512), np.float32)
    for m in range(6):
        bk = _t5_bucket(((m - 1) * 128 + p - j).astype(np.int32))
        for h in range(4):
            t5t[h, :, m, :] = t5[bk, h]
    t5c = np.empty((128, 4, 2), np.float32)
    t5c[:, :, 0] = t5[15][None, :]
    t5c[:, :, 1] = t5[31][None, :]
    rpb = np.asarray(inp["na_rpb"], np.float32)
    nabp = _nab_table(rpb, 32, 0)
    shared = {
        "w_in": np.ascontiguousarray(inp["w_in"], np.float32),
        "w_branch": np.ascontiguousarray(inp["w_branch"], np.float32),
        "w_out": np.ascontiguousarray(inp["w_out"], np.float32),
        "w_ffn_in": np.ascontiguousarray(inp["w_ffn_in"], np.float32),
        "w_ffn_out": np.ascontiguousarray(inp["w_ffn_out"], np.float32),
        "pcol": np.stack([_pack_pcol(inp, l) for l in range(DEPTH)]),
        "prow": np.stack([np.stack([inp["ln_mix_post"][l], inp["ln_ffn_post"][l]]) for l in range(DEPTH)]).astype(np.float32),
        "lam": np.asarray(inp["diff_lambda"], np.float32).reshape(DEPTH, 256),
        "t5t": t5t, "t5c": t5c, "nabp": nabp,
        "ident": np.eye(128, dtype=np.float32),
    }
    nabs_q = [_nab_table(rpb, 128, 16 * q) for q in range(4)]
    in_maps = []
    xp = np.asarray(inp["x_prompt"], np.float32)
    xs = np.asarray(inp["x_sample"], np.float32)
    for c in range(cores):
        q, s = c % 4, c // 4
        m = dict(shared)
        xin = np.empty((1 + n_prompt, U, D), np.float32)
        xin[0] = xs[s, q * U:(q + 1) * U]
        for i in range(n_prompt):
            xin[1 + i] = xp[c * n_prompt + i]
        m["xin"] = xin
        t5cs = np.empty((128, 4, 4), np.float32)
        for ro in range(4):
            r = (q + ro) % 4
            t5cs[:, :, ro] = (t5[31] if r > q else t5[15])[None, :]
        m["t5cs"] = t5cs
        t5j = np.empty((4, 128, 2, 512), np.float32)
        r1 = ((q + 1) % 4 - q) * U + 0 * 128 + p - (3 * 512 + j)
        r3 = ((q + 3) % 4 - q) * U + 15 * 128 + p - j
        b1, b3 = _t5_bucket(r1.astype(np.int32)), _t5_bucket(r3.astype(np.int32))
        for h in range(4):
            t5j[h, :, 0, :] = t5[b1, h]
            t5j[h, :, 1, :] = t5[b3, h]
        m["t5j"] = t5j
        m["nabs"] = nabs_q[q]
        cf = np.zeros((128, 2), np.float32)
        cf[:, 0] = 1.0 if q > 0 else 0.0
        cf[:, 1] = 1.0 if q < 3 else 0.0
        m["cflag"] = cf
        in_maps.append(m)
    return in_maps


_CACHE = {}


def kernel(**inputs):
    inp = {k: np.asarray(v) for k, v in inputs.items()}
    if "nc" not in _CACHE:
        _CACHE["nc"] = build_program()[0]
    nc = _CACHE["nc"]
    in_maps = prepare_inputs(inp)
    res = run_bass_kernel_spmd(nc, in_maps, core_ids=list(range(8)))
    y_prompt = np.empty((32, U, D), np.float32)
    y_sample = np.empty((2, 8192, D), np.float32)
    for c in range(8):
        yo = np.asarray(res.results[c]["yout"], np.float32)
        q, s = c % 4, c // 4
        y_sample[s, q * U:(q + 1) * U] = yo[0]
        for i in range(4):
            y_prompt[c * 4 + i] = yo[1 + i]
    return (y_prompt, y_sample)
```

```python
import math
from contextlib import ExitStack

import numpy as np
import concourse.bass as bass
import concourse.mybir as mybir
from concourse.bass_utils import run_bass_kernel_spmd

F32 = mybir.dt.float32
BF16 = mybir.dt.bfloat16
AF = mybir.ActivationFunctionType
ALU = mybir.AluOpType

D = 1024
DEPTH = 2
U = 2048
NT = 16
IN_COLS = 7168
OFF_CONV = 1536
OFF_DIFF = 2560
OFF_GATE = 4096
FFN_H = 2816
NHC = 22
EPS = 1e-6
NEG = -30000.0
SCALE = 0.125
NPC = 177
PC_GPRE, PC_GFFN, PC_BGATE, PC_CW, PC_CB, PC_LNG, PC_LNB, PC_SUB = 0, 8, 16, 40, 164, 168, 172, 176
NAB_GRP = {"b0": (0, list(range(-2, 4))), "b1": (6, list(range(-2, 3))), "int": (11, list(range(-2, 3))),
           "b14": (16, list(range(-2, 3))), "b15": (21, list(range(-3, 3)))}
CH = 8000
KQ = 8


def _grp_of_block(i):
    return {0: "b0", 1: "b1", 14: "b14", 15: "b15"}.get(i, "int")


class _Op:
    __slots__ = ("eng", "fn", "deps", "kind", "sig", "needs_sig")

    def __init__(self, eng, fn, deps, kind):
        self.eng, self.fn, self.deps, self.kind = eng, fn, deps, kind
        self.sig = None
        self.needs_sig = kind != "c"


class Prog:
    def __init__(self):
        self.marks = []
        self.ops = []
        self.last_w = {}
        self.readers = {}
        self.reg_cur = {}
        self.reg_prev = {}

    def mark(self, label):
        cnt = {}
        for op in self.ops:
            cnt[op.eng] = cnt.get(op.eng, 0) + 1
        self.marks.append((label, cnt))

    def new_epoch(self, R):
        cur = self.reg_cur.pop(R, ({}, []))
        self.reg_prev[R] = list(cur[0].values()) + cur[1]

    def add(self, eng, fn, reads=(), writes=(), kind="c", regions=()):
        i = len(self.ops)
        deps = set()
        for R in regions:
            deps.update(self.reg_prev.get(R, ()))
            cur = self.reg_cur.setdefault(R, ({}, []))
            if kind == "d":
                cur[1].append(i)
            else:
                cur[0][eng] = i
        lw, rd = self.last_w, self.readers
        for k in reads:
            w = lw.get(k)
            if w is not None:
                deps.add(w)
        for k in writes:
            w = lw.get(k)
            if w is not None:
                deps.add(w)
            r = rd.get(k)
            if r:
                deps.update(r[0].values())
                deps.update(r[1])
        for k in reads:
            r = rd.get(k)
            if r is None:
                r = rd[k] = ({}, [])
            if kind == "d":
                r[1].append(i)
            else:
                r[0][eng] = i
        for k in writes:
            lw[k] = i
            rd[k] = ({}, [])
        deps.discard(i)
        self.ops.append(_Op(eng, fn, deps, kind))
        return i

    def finalize(self):
        ops = self.ops
        for op in ops:
            nd = []
            for d in op.deps:
                a = ops[d]
                if a.eng == "tensor" and op.eng == "tensor" and a.kind == "c" and op.kind == "c":
                    continue
                a.needs_sig = True
                nd.append(d)
            op.deps = nd
        self.ccount = {}
        self.dcount = {}
        self.ncc = 0
        for op in ops:
            if op.kind == "d":
                n = self.dcount.get(op.eng, 0)
                self.dcount[op.eng] = n + 1
                op.sig = ("d", op.eng, n)
            elif op.kind == "cc":
                op.sig = ("cc", self.ncc)
                self.ncc += 1
            elif op.needs_sig:
                k = self.ccount.get(op.eng, 0) + 1
                self.ccount[op.eng] = k
                op.sig = ("c", op.eng, k)
        last = {}
        for i, op in enumerate(ops):
            if op.kind == "d":
                lst = last.setdefault(op.eng, [])
                if len(lst) >= KQ:
                    op.deps.append(lst[-KQ])
                lst.append(i)

    def alloc_sems(self, nc, es):
        self.csem = {e: [es.enter_context(nc.semaphore(f"c_{e}_{j}")) for j in range((k + CH - 1) // CH)]
                     for e, k in self.ccount.items()}
        self.dsem = {e: [es.enter_context(nc.semaphore(f"d_{e}_{j}")) for j in range(KQ)] for e in self.dcount}
        self.ccsem = [es.enter_context(nc.semaphore(f"cc_{j}")) for j in range(self.ncc)]

    def _resolve(self, sig):
        if sig[0] == "c":
            k = sig[2]
            return self.csem[sig[1]][(k - 1) // CH], (k - 1) % CH + 1, 1
        if sig[0] == "d":
            n = sig[2]
            return self.dsem[sig[1]][n % KQ], 16 * (n // KQ + 1), 16
        return self.ccsem[sig[1]], 1, None

    def emit(self, engname, e, final_wait=False):
        waited = {}
        ops = self.ops
        for op in ops:
            if op.eng != engname:
                continue
            for d in op.deps:
                sem, val, _ = self._resolve(ops[d].sig)
                key = id(sem)
                if waited.get(key, 0) < val:
                    e.wait_ge(sem, val)
                    waited[key] = val
            try:
                ins = op.fn(e)
            except Exception:
                import inspect
                cv = inspect.getclosurevars(op.fn)
                print("EMIT FAIL", engname, op.kind, {k: (str(v)[:200]) for k, v in cv.nonlocals.items()}, flush=True)
                raise
            if op.sig is not None:
                sem, val, inc = self._resolve(op.sig)
                if inc is None:
                    ins.then_inc(sem)
                else:
                    ins.then_inc(sem, inc)
        if final_wait:
            for q, n in self.dcount.items():
                for j in range(min(KQ, n)):
                    cnt = (n - 1 - j) // KQ + 1
                    e.wait_ge(self.dsem[q][j], 16 * cnt)


def _t5_bucket(rel):
    nb, max_exact = 16, 8
    ret = np.where(rel > 0, nb, 0)
    n = np.abs(rel)
    nf = np.maximum(n, 1).astype(np.float32)
    large = max_exact + (np.log(nf / np.float32(max_exact)) / np.float32(math.log(128 / max_exact))
                         * np.float32(nb - max_exact)).astype(np.int32)
    large = np.minimum(large, nb - 1)
    return ret + np.where(n < max_exact, n, large)


def _na_tile(rpb_h, R, gi, gj):
    if gj < 0 or gj >= R // 2 or gi < 0 or gi >= R // 2:
        return np.full((128, 128), NEG, np.float32)
    kp = np.arange(128)
    krow, kcol = 2 * gj + kp // 64, kp % 64
    qrow, qcol = 2 * gi + kp // 64, kp % 64
    rs = np.clip(qrow - 4, 0, R - 8)
    cs = np.clip(qcol - 8, 0, 48)
    valid = ((krow[:, None] >= rs[None, :]) & (krow[:, None] < rs[None, :] + 8)
             & (kcol[:, None] >= cs[None, :]) & (kcol[:, None] < cs[None, :] + 16))
    val = rpb_h[np.clip(krow[:, None] - qrow[None, :] + 7, 0, 14), np.clip(kcol[:, None] - qcol[None, :] + 15, 0, 30)]
    return np.where(valid, val, np.float32(NEG)).astype(np.float32)


def _nab_table(rpb, R, base_block):
    L = rpb.shape[0]
    out = np.empty((L, 8, 128, 27, 128), np.float32)
    loc = {"b0": 0, "b1": 1, "int": 5, "b14": 14, "b15": 15}
    for l in range(L):
        for h in range(8):
            for g, (off, jrels) in NAB_GRP.items():
                gi = base_block + loc[g]
                for s, jr in enumerate(jrels):
                    out[l, h, :, off + s, :] = _na_tile(rpb[l, h], R, gi, gi + jr)
    return out


def _pack_pcol(inp, l):
    pc = np.zeros((128, NPC), np.float32)
    pc[:, PC_GPRE:PC_GPRE + 8] = inp["ln_mix_pre"][l].reshape(8, 128).T
    pc[:, PC_GFFN:PC_GFFN + 8] = inp["ln_ffn_pre"][l].reshape(8, 128).T
    pc[:, PC_BGATE:PC_BGATE + 24] = inp["b_gate"][l].reshape(24, 128).T
    pc[:, PC_CW:PC_CW + 124] = inp["conv_dw_w"][l].reshape(31, 4, 128).transpose(2, 1, 0).reshape(128, 124)
    pc[:, PC_CB:PC_CB + 4] = inp["conv_dw_b"][l].reshape(4, 128).T
    pc[:, PC_LNG:PC_LNG + 4] = inp["conv_ln_g"][l].reshape(4, 128).T
    pc[:, PC_LNB:PC_LNB + 4] = inp["conv_ln_b"][l].reshape(4, 128).T
    pc[:, PC_SUB] = inp["diff_subln_g"][l]
    return pc


def build_program(n_prompt=4, depth=DEPTH, with_sample=True, taps=()):
    nc = bass.Bass("TRN2", target_bir_lowering=False)
    NU = 1 + n_prompt
    P = Prog()
    dt = nc.dram_tensor

    xin = dt("xin", [NU, U, D], F32, kind="ExternalInput").ap()
    w_in = dt("w_in", [DEPTH, D, IN_COLS], F32, kind="ExternalInput").ap()
    w_branch = dt("w_branch", [DEPTH, 3, 512, D], F32, kind="ExternalInput").ap()
    w_out = dt("w_out", [DEPTH, D, D], F32, kind="ExternalInput").ap()
    w_ffn_in = dt("w_ffn_in", [DEPTH, D, 2 * FFN_H], F32, kind="ExternalInput").ap()
    w_ffn_out = dt("w_ffn_out", [DEPTH, FFN_H, D], F32, kind="ExternalInput").ap()
    pcol_d = dt("pcol", [DEPTH, 128, NPC], F32, kind="ExternalInput").ap()
    prow_d = dt("prow", [DEPTH, 2, D], F32, kind="ExternalInput").ap()
    lam_d = dt("lam", [DEPTH, 256], F32, kind="ExternalInput").ap()
    t5t_d = dt("t5t", [4, 128, 6, 512], F32, kind="ExternalInput").ap()
    t5c_d = dt("t5c", [128, 4, 2], F32, kind="ExternalInput").ap()
    t5cs_d = dt("t5cs", [128, 4, 4], F32, kind="ExternalInput").ap()
    t5j_d = dt("t5j", [4, 128, 2, 512], F32, kind="ExternalInput").ap()
    nabp_d = dt("nabp", [DEPTH, 8, 128, 27, 128], F32, kind="ExternalInput").ap()
    nabs_d = dt("nabs", [DEPTH, 8, 128, 27, 128], F32, kind="ExternalInput").ap()
    cflag_d = dt("cflag", [128, 2], F32, kind="ExternalInput").ap()
    ident_d = dt("ident", [128, 128], F32, kind="ExternalInput").ap()
    yout = dt("yout", [NU, U, D], F32, kind="ExternalOutput").ap()
    tap_d = {}
    for name, shape in taps:
        tap_d[name] = dt("tap_" + name, list(shape), F32, kind="ExternalOutput").ap()

    xmid = dt("xmid", [NU, U, D], F32).ap()
    x1 = dt("x1", [NU, U, D], F32).ap()
    cf = [[dt(f"cf{l}_{s_}", [128, U], BF16).ap() for s_ in range(12)] for l in range(depth)]
    ct = [[dt(f"ct{l}_{j_}", [U, 128], BF16).ap() for j_ in range(8)] for l in range(depth)]
    tfm = [[dt(f"tfm{l}_{s_}", [4 * 128, U], BF16).ap() for s_ in range(12)] for l in range(depth)]
    ttm = [[dt(f"ttm{l}_{j_}", [4 * U, 128], BF16).ap() for j_ in range(8)] for l in range(depth)]

    es = ExitStack()
    with es:
        S = lambda name, shape, dtype: es.enter_context(nc.sbuf_tensor(name, shape, dtype))
        ident = S("ident_s", [128, 128], BF16)
        identf = S("identf", [128, 128], F32)
        ones_bf = S("ones_bf", [128, 128], BF16)
        ones_f = S("ones_f", [128, 128], F32)
        o512_f = S("o512_f", [128, 128], F32)
        epsc = S("epsc", [128, 1], F32)
        hT = S("hT", [128, 8, U], BF16)
        brT = S("brT", [128, 12 * U], BF16)
        arena = S("arena", [128, 22528], BF16)
        wbuf = [S(f"wbuf{i}", [128, 4608], BF16) for i in range(2)]
        xt = [S(f"xt{i}", [128, D], F32) for i in range(3)]
        hb = [S(f"hb{i}", [128, D], BF16) for i in range(2)]
        pT = [S(f"pT{i}", [128, 768], BF16) for i in range(4)]
        tmp = [S(f"tmp{i}", [128, 768], F32) for i in range(4)]
        ytmp = S("ytmp", [128, D], F32)
        lamb = ytmp[:, 0:256]
        biasbuf = S("biasbuf", [128, 4096], F32)
        gpost = S("gpost", [128, 2, D], F32)
        pcol = S("pcol_s", [128, NPC], F32)
        small = S("small", [128, 64], F32)
        t5c = S("t5c_s", [128, 4, 2], F32)
        t5cs = S("t5cs_s", [128, 4, 4], F32)
        cflag = S("cflag_s", [128, 2], F32)
        ps = es.enter_context(nc.psum_tensor("ps", [128, 4096], F32))

        rr = {"ps": 0, "x": 0, "hb": 0, "pT": 0, "tmp": 0, "wb": 0, "ev": 0, "col": 0, "psn": 8, "dfs": 0}

        def nxt(name, n):
            v = rr[name]
            rr[name] = (v + 1) % n
            return v

        def bank(b, n=512):
            return ps[:, b * 512:b * 512 + n]

        def nbank():
            return nxt("ps", rr["psn"])

        def nbank2():
            b = rr["ps"]
            if b % 2:
                b = (b + 1) % 8
            b = b % 8
            rr["ps"] = (b + 2) % 8
            return b

        def col():
            c = 8 + nxt("col", 48)
            return small[:, c:c + 1], ("small", c)

        def regs(*aps):
            r = set()
            for a in aps:
                if a is None or isinstance(a, (int, float)):
                    continue
                n = a.name
                if n == "arena":
                    r.add("RA")
                elif n == "brT":
                    r.add("RB")
            return tuple(r)

        def dma(q, out, in_, reads, writes):
            return P.add(q, lambda e, out=out, in_=in_: e.dma_start(out=out, in_=in_), reads, writes, kind="d", regions=regs(out, in_))

        def dma_dyn(out, in_fn, reads, writes):
            return P.add("sync", lambda e: e.dma_start(out=out, in_=in_fn(e)), reads, writes, kind="d", regions=regs(out))

        def mm(out, lhsT, rhs, start, stop, reads, writes):
            return P.add("tensor", lambda e: e.matmul(out, lhsT=lhsT, rhs=rhs, start=start, stop=stop), reads, writes,
                         regions=regs(lhsT, rhs))

        def tr(out, in_, reads, writes):
            return P.add("tensor", lambda e: e.transpose(out=out, in_=in_, identity=ident[:]), reads + [("ident",)], writes,
                         regions=regs(in_))

        def act(out, in_, func, reads, writes, **kw):
            return P.add("scalar", lambda e: e.activation(out=out, in_=in_, func=func, **kw), reads, writes, regions=regs(out, in_))

        def amul(out, in_, mul, reads, writes):
            return P.add("scalar", lambda e: e.mul(out=out, in_=in_, mul=mul), reads, writes, regions=regs(out, in_))

        def tt(out, in0, in1, op, reads, writes, eng="vector"):
            return P.add(eng, lambda e: e.tensor_tensor(out=out, in0=in0, in1=in1, op=op), reads, writes, regions=regs(out, in0, in1))

        def tsc(out, in0, s1, s2, op0, op1, reads, writes, eng="vector"):
            if op1 is None:
                return P.add(eng, lambda e: e.tensor_scalar(out=out, in0=in0, scalar1=s1, scalar2=None, op0=op0), reads, writes,
                             regions=regs(out, in0))
            return P.add(eng, lambda e: e.tensor_scalar(out=out, in0=in0, scalar1=s1, scalar2=s2, op0=op0, op1=op1), reads, writes,
                         regions=regs(out, in0))

        def stt(out, in0, scalar, in1, op0, op1, reads, writes, eng="vector"):
            return P.add(eng, lambda e: e.scalar_tensor_tensor(out=out, in0=in0, scalar=scalar, in1=in1, op0=op0, op1=op1), reads, writes,
                         regions=regs(out, in0, in1))

        def cpy(out, in_, reads, writes, eng="vector"):
            return P.add(eng, lambda e: e.tensor_copy(out=out, in_=in_), reads, writes, regions=regs(out, in_))

        def memset(ap, val, writes, eng="vector"):
            return P.add(eng, lambda e: e.memset(ap, val), [], writes, regions=regs(ap))

        def recip(out, in_, reads, writes):
            return P.add("vector", lambda e: e.reciprocal(out=out, in_=in_), reads, writes, regions=regs(out, in_))

        def fence(eng, old, new):
            raise RuntimeError("unused")

        def evac(out, in_, reads, writes):
            if nxt("ev", 2) == 0:
                return cpy(out, in_, reads, writes)
            return P.add("scalar", lambda e: e.copy(out=out, in_=in_), reads, writes, regions=regs(out, in_))

        def rsqrt_col(dst, dkey, src, skey, scale):
            act(dst, src, AF.Ln, [skey, ("epsc",)], [dkey], scale=scale, bias=epsc[:, 0:1])
            act(dst, dst, AF.Exp, [dkey], [dkey], scale=-0.5)

        def tap(name, idx, src, reads):
            if name in tap_d:
                dma("sync", tap_d[name][idx], src, reads, [("tap", name, str(idx))])

        def ar(off, n):
            return arena[:, off:off + n]

        AR_ALL = [("ar", i) for i in range(11)]
        BKALL = [("bb", i) for i in range(5)]
        BK_NA = {"A": [("bb", 0)], "B": [("bb", 1), ("bb", 2)], "C": [("bb", 3)]}
        BK_DG = [[("bb", 0), ("bb", 1)], [("bb", 2), ("bb", 3), ("bb", 4)]]

        def arkeys(off, n):
            return [("ar", i) for i in range(off // 2048, (off + n - 1) // 2048 + 1)]

        dma("sync", identf[:], ident_d[:, :], [], [("identf",)])
        cpy(ident[:], identf[:], [("identf",)], [("ident",)])
        memset(ones_bf[:], 1.0, [("ones_bf",)])
        memset(ones_f[:], 1.0 / 128, [("ones_f",)])
        memset(o512_f[:], 1.0 / 512, [("o512_f",)])
        memset(epsc[:], EPS, [("epsc",)])
        dma("sync", t5c[:], t5c_d[:, :, :], [], [("t5c",)])
        dma("sync", t5cs[:], t5cs_d[:, :, :], [], [("t5cs",)])
        dma("sync", cflag[:], cflag_d[:, :], [], [("cflag",)])

        _pid = {}

        def rank_of(e, ro):
            if id(e) not in _pid:
                _pid[id(e)] = e.partition_id() % 4
            return (_pid[id(e)] + ro) % 4

        def load_w(dst_ap, src_ap, wkeys):
            if len(dst_ap.shape) == 4:
                ax = 2 if dst_ap.shape[2] <= dst_ap.shape[1] else 1
                for i_ in range(dst_ap.shape[ax]):
                    if ax == 2:
                        dma("gpsimd", dst_ap[:, :, i_, :], src_ap[:, :, i_, :], [], wkeys)
                    else:
                        dma("gpsimd", dst_ap[:, i_, :, :], src_ap[:, i_, :, :], [], wkeys)
                return
            dma("gpsimd", dst_ap, src_ap, [], wkeys)

        def win_cols(l, c0, n):
            return w_in[l].rearrange("(kc p) n -> p kc n", p=128)[:, :, c0:c0 + n]

        def wslot():
            s = nxt("wb", 2)
            return s, [("wb", s)]

        def prenorm_tile(xtile, xkey, t, gcol0):
            ss, sk = col()
            hi = nxt("hb", 2)
            act(hb[hi][:], xtile, AF.Square, [xkey], [sk, ("hb", hi)], accum_out=ss)
            rs, rk = col()
            rsqrt_col(rs, rk, ss, sk, 1.0 / D)
            amul(hb[hi][:], xtile, rs, [xkey, rk], [("hb", hi)])
            b = nbank()
            pb = bank(b).bitcast(BF16).rearrange("p (a b) -> p a b", a=8)
            for k in range(8):
                tr(pb[:, k, :], hb[hi][:, k * 128:(k + 1) * 128], [("hb", hi)], [("ps", b)])
            tt(hT[:, :, t * 128:(t + 1) * 128], pb,
               pcol[:, gcol0:gcol0 + 8].unsqueeze(2).broadcast_to([128, 8, 128]), ALU.mult,
               [("ps", b), ("pcol",)], [("hT", t)])

        def phase_a(xsrc, u, l):
            for t in range(NT):
                xi = nxt("x", 3)
                dma("sync", xt[xi][:], xsrc[u, t * 128:(t + 1) * 128, :], [("xd", l, u, t)], [("xt", xi)])
                prenorm_tile(xt[xi][:], ("xt", xi), t, PC_GPRE)

        def proj_fm(dst_fn, wk_fn, nk, rhs_fn, rkeys_fn, dkeys_fn, post=None):
            for tg in range(4):
                b = nbank()
                for k in range(nk):
                    mm(bank(b), wk_fn(k), rhs_fn(k, tg), k == 0, k == nk - 1, rkeys_fn(tg), [("ps", b)])
                if post is None:
                    evac(dst_fn(tg), bank(b), [("ps", b)], dkeys_fn(tg))
                else:
                    post(tg, b)

        hT_rhs = lambda k, tg: hT[:, k, tg * 512:(tg + 1) * 512]
        hT_keys = lambda tg: [("hT", t) for t in range(tg * 4, tg * 4 + 4)]

        def sample_prepass(xsrc, l):
            phase_a(xsrc, 0, l)
            P.new_epoch("RA")
            for sec, c0 in ((0, OFF_DIFF + 512), (1, 512)):
                s, wk = wslot()
                wv = wbuf[s][:, 0:4096].rearrange("p (k n) -> p k n", k=8)
                load_w(wv, win_cols(l, c0, 512), wk)
                for c in range(4):
                    st = ar((c % 2) * 2048, 2048)
                    skeys = lambda tg, c=c: [("ar", c % 2, "s", tg)]
                    proj_fm(lambda tg, st=st: st[:, tg * 512:(tg + 1) * 512], lambda k, c=c: wv[:, k, c * 128:(c + 1) * 128], 8,
                            hT_rhs, lambda tg: hT_keys(tg) + wk, skeys)
                    dma("sync", cf[l][sec * 4 + c], st,
                        [("ar", c % 2, "s", tg) for tg in range(4)], [("cf", l, sec * 4 + c)])
            sa, wka = wslot()
            wa = wbuf[sa][:, 0:4096].rearrange("p (k n) -> p k n", k=8)
            load_w(wa, win_cols(l, OFF_CONV, 512), wka)
            sg_, wkg = wslot()
            wg = wbuf[sg_][:, 0:4096].rearrange("p (k n) -> p k n", k=8)
            load_w(wg, win_cols(l, OFF_CONV + 512, 512), wkg)
            for c in range(4):
                st = ar((c % 2) * 2048, 2048)
                for tg in range(4):
                    glu_group(lambda k: wa[:, k, c * 128:(c + 1) * 128], lambda k: wg[:, k, c * 128:(c + 1) * 128],
                              wka + wkg, tg, st[:, tg * 512:(tg + 1) * 512], [("ar", c % 2, "s", tg)])
                dma("sync", cf[l][8 + c], st,
                    [("ar", c % 2, "s", tg) for tg in range(4)], [("cf", l, 8 + c)])
            sv, wkv = wslot()
            wv1 = wbuf[sv][:, 0:4096].rearrange("p (k n) -> p k n", k=8)
            load_w(wv1, win_cols(l, OFF_DIFF + 1024, 512), wkv)
            sn, wkn = wslot()
            wv2 = wbuf[sn][:, 0:4096].rearrange("p (k n) -> p k n", k=8)
            load_w(wv2, win_cols(l, 1024, 512), wkn)
            stg = ar(4096, 16384).rearrange("p (t n) -> p t n", t=16)
            for t in range(NT):
                sk = [("ar", "v", t)]
                b = nbank()
                for k in range(8):
                    mm(bank(b), hT[:, k, t * 128:(t + 1) * 128], wv1[:, k, :], k == 0, k == 7, [("hT", t)] + wkv, [("ps", b)])
                evac(stg[:, t, 0:512], bank(b), [("ps", b)], sk)
                b = nbank()
                for k in range(8):
                    mm(bank(b), hT[:, k, t * 128:(t + 1) * 128], wv2[:, k, :], k == 0, k == 7, [("hT", t)] + wkn, [("ps", b)])
                evac(stg[:, t, 512:1024], bank(b), [("ps", b)], sk)
            for j_ in range(8):
                dma("sync", ct[l][j_].rearrange("(t p) n -> p t n", p=128), stg[:, :, j_ * 128:(j_ + 1) * 128],
                    [("ar", "v", t) for t in range(NT)], [("ct", l, j_)])
            rg = [[0, 1, 2, 3], [4, 5, 6, 7]]
            for s_ in range(12):
                P.add("gpsimd", lambda e, s_=s_: e.collective_compute("AllGather", ALU.bypass, replica_groups=rg,
                                                                      ins=[cf[l][s_].opt()], outs=[tfm[l][s_].opt()]),
                      [("cf", l, s_)], [("tfm", l, s_)], kind="cc")
            for j_ in range(8):
                P.add("gpsimd", lambda e, j_=j_: e.collective_compute("AllGather", ALU.bypass, replica_groups=rg,
                                                                      ins=[ct[l][j_].opt()], outs=[ttm[l][j_].opt()]),
                      [("ct", l, j_)], [("ttm", l, j_)], kind="cc")

        def tfv(l, s_):
            return tfm[l][s_].rearrange("(r m) n -> r m n", r=4)

        def ttv(l, j_):
            return ttm[l][j_].rearrange("(r t p) n -> r p t n", r=4, p=128)

        def glu_group(wa_fn, wg_fn, wkeys, tg, dst, dkeys):
            ba = nbank()
            for k in range(8):
                mm(bank(ba), wa_fn(k), hT_rhs(k, tg), k == 0, k == 7, hT_keys(tg) + wkeys, [("ps", ba)])
            bg = nbank()
            for k in range(8):
                mm(bank(bg), wg_fn(k), hT_rhs(k, tg), k == 0, k == 7, hT_keys(tg) + wkeys, [("ps", bg)])
            ti = nxt("tmp", 4)
            act(tmp[ti][:, 0:512], bank(bg), AF.Sigmoid, [("ps", bg)], [("tmp", ti)])
            tt(dst, bank(ba), tmp[ti][:, 0:512], ALU.mult, [("ps", ba), ("tmp", ti)], dkeys)

        def na_phase(u, l, is_sample):
            P.new_epoch("RA")
            P.new_epoch("RB")
            nab_src = nabs_d if is_sample else nabp_d
            for c in range(4):
                base = (c % 2) * 7232
                qT = ar(base, 2048)
                kT = ar(base + 2048, 2560)
                va = ar(base + 4608, 2600).rearrange("p (t h e) -> p t h e", t=20, h=2)
                otm = ar(14464 + (c % 2) * 2048, 2048).rearrange("p (i f) -> p i f", i=16)
                kq, kk, kv, ko = ("na", c % 2, "q"), ("na", c % 2, "k"), ("na", c % 2, "v"), ("na", c % 2, "o")
                s, wk = wslot()
                wv = wbuf[s][:, 0:3072].rearrange("p (k b n) -> p k b n", k=8, b=3)
                nb_w = 1 if is_sample else 3
                wsrc = w_in[l].rearrange("(kc p) (b n) -> p kc b n", p=128, n=512)[:, :, 0:nb_w, c * 128:(c + 1) * 128]
                load_w(wv[:, :, 0:nb_w, :], wsrc, wk)
                proj_fm(lambda tg: qT[:, tg * 512:(tg + 1) * 512], lambda k: wv[:, k, 0, :], 8, hT_rhs,
                        lambda tg: hT_keys(tg) + wk, lambda tg: [kq + (tg,)])
                if not is_sample:
                    proj_fm(lambda tg: kT[:, 256 + tg * 512:256 + (tg + 1) * 512], lambda k: wv[:, k, 1, :], 8, hT_rhs,
                            lambda tg: hT_keys(tg) + wk, lambda tg: [kk + (tg,)])
                    for t4 in range(4):
                        b = nbank()
                        for tl in range(4):
                            t = t4 * 4 + tl
                            for k in range(8):
                                mm(bank(b)[:, tl * 128:(tl + 1) * 128], hT[:, k, t * 128:(t + 1) * 128], wv[:, k, 2, :],
                                   k == 0, k == 7, [("hT", t)] + wk, [("ps", b)])
                        evac(va[:, 2 + t4 * 4:6 + t4 * 4, :, 0:64], bank(b).rearrange("p (t h e) -> p t h e", t=4, h=2),
                             [("ps", b)], [kv + (t4,)])
                    memset(va[:, :, :, 64:65], 1.0, [kv + ("ones",)])
                    kkeys_all = [kk + (tg,) for tg in range(4)]
                    vkeys_all = [kv + (t4,) for t4 in range(4)] + [kv + ("ones",)]
                else:
                    fk, fv = ("tfm", l, 4 + c), ("ttm", l, 4 + c)
                    dma_dyn(kT[:, 256:2304], lambda e, c=c: tfv(l, 4 + c)[rank_of(e, 0)][:, :], [fk], [kk + (0,)])
                    dma_dyn(kT[:, 0:256], lambda e, c=c: tfv(l, 4 + c)[rank_of(e, 3)][:, 1792:2048], [fk], [kk + (1,)])
                    dma_dyn(kT[:, 2304:2560], lambda e, c=c: tfv(l, 4 + c)[rank_of(e, 1)][:, 0:256], [fk], [kk + (2,)])
                    for hh_ in range(2):
                        hs = slice(hh_ * 64, (hh_ + 1) * 64)
                        dma_dyn(va[:, 2:18, hh_, 0:64], lambda e, c=c, hs=hs: ttv(l, 4 + c)[rank_of(e, 0)][:, :, hs], [fv], [kv + (0, hh_)])
                        dma_dyn(va[:, 0:2, hh_, 0:64], lambda e, c=c, hs=hs: ttv(l, 4 + c)[rank_of(e, 3)][:, 14:16, hs], [fv], [kv + (1, hh_)])
                        dma_dyn(va[:, 18:20, hh_, 0:64], lambda e, c=c, hs=hs: ttv(l, 4 + c)[rank_of(e, 1)][:, 0:2, hs], [fv], [kv + (2, hh_)])
                    memset(va[:, :, :, 64:65], 1.0, [kv + ("ones",)])
                    kkeys_all = [kk + (i,) for i in range(3)]
                    vkeys_all = [kv + (i, hh_) for i in range(3) for hh_ in range(2)] + [kv + ("ones",)]
                for hh in range(2):
                    h = 2 * c + hh
                    pb_ = hh * 64
                    nabv = biasbuf[:, 0:3456]

                    def load_nab(hd, grp):
                        lo_, hi_ = {"A": (0, 11), "B": (11, 16), "C": (16, 27)}[grp]
                        dma("sync", biasbuf[:, lo_ * 128:hi_ * 128].rearrange("p (s n) -> p s n", s=hi_ - lo_),
                            nab_src[l, hd][:, lo_:hi_, :], [], BK_NA[grp])

                    if h == 0:
                        load_nab(0, "A")
                        load_nab(0, "B")
                    load_nab(h, "C")
                    def na_S(i):
                        goff, jrels = NAB_GRP[_grp_of_block(i)]
                        slots = [(s_, jr) for s_, jr in enumerate(jrels) if is_sample or 0 <= i + jr <= 15]
                        ns = len(slots)
                        s0 = slots[0][0]
                        b = nbank2()
                        for n_, (s_, jr) in enumerate(slots):
                            co = 256 + (i + jr) * 128
                            mm(ps[:, b * 512 + n_ * 128: b * 512 + (n_ + 1) * 128], kT[pb_:pb_ + 64, co:co + 128],
                               qT[pb_:pb_ + 64, i * 128:(i + 1) * 128], True, True,
                               kkeys_all + [kq + (i // 4,)], [("ps", b + n_ // 4)])
                        pkeys = [("ps", b)] + ([("ps", b + 1)] if ns > 4 else [])
                        ti = nxt("tmp", 4)
                        stt(tmp[ti][:, 0:ns * 128], ps[:, b * 512:b * 512 + ns * 128], SCALE,
                            nabv[:, (goff + s0) * 128:(goff + s0 + ns) * 128], ALU.mult, ALU.add,
                            pkeys + BK_NA[{"b0": "A", "b1": "A", "int": "B", "b14": "C", "b15": "C"}[_grp_of_block(i)]], [("tmp", ti)])
                        pi = nxt("pT", 4)
                        act(pT[pi][:, 0:ns * 128], tmp[ti][:, 0:ns * 128], AF.Exp, [("tmp", ti)], [("pT", pi)])
                        if h < 7 and i == 2:
                            load_nab(h + 1, "A")
                        if h < 7 and i == 14:
                            load_nab(h + 1, "B")
                        return (i, slots, pi)

                    def na_PV(ctx):
                        i, slots, pi = ctx
                        ns = len(slots)
                        bo = nbank()
                        for n_, (s_, jr) in enumerate(slots):
                            mm(bank(bo, 65), pT[pi][:, n_ * 128:(n_ + 1) * 128], va[:, 2 + i + jr, hh, :], n_ == 0, n_ == ns - 1,
                               [("pT", pi)] + vkeys_all, [("ps", bo)])
                        rc, rk = col()
                        recip(rc, bank(bo)[:, 64:65], [("ps", bo)], [rk])
                        tsc(otm[:, i, hh * 64:(hh + 1) * 64], bank(bo)[:, 0:64], rc, None, ALU.mult, None,
                            [("ps", bo), rk], [ko + (i, hh)])

                    ctxs = {}
                    for j_ in range(-1, 16):
                        if j_ + 1 < 16:
                            ctxs[j_ + 1] = na_S(j_ + 1)
                        if j_ >= 0:
                            na_PV(ctxs.pop(j_))
                for i4 in range(4):
                    b = nbank()
                    pbv = bank(b).bitcast(BF16)
                    for il in range(4):
                        i = i4 * 4 + il
                        tr(pbv[:, il * 128:(il + 1) * 128], otm[:, i, :], [ko + (i, 0), ko + (i, 1)], [("ps", b)])
                    evac(brT[:, c * U + i4 * 512: c * U + (i4 + 1) * 512], pbv[:, 0:512], [("ps", b)], [("brT", c, i4)])

        def conv_phase(u, l, is_sample):
            P.new_epoch("RA")
            acc4 = arena[:, 0:16384].bitcast(F32).rearrange("p (c n) -> p c n", c=4)
            for c in range(4):
                g = ar(16384 + (c % 2) * 2080, 2078)
                gk = ("cv", "g", c % 2)
                if not is_sample:
                    s, wk = wslot()
                    wv = wbuf[s][:, 0:2048].rearrange("p (k b n) -> p k b n", k=8, b=2)
                    wsrc = w_in[l][:, OFF_CONV:OFF_CONV + 1024].rearrange("(kc p) (b n) -> p kc b n", p=128, n=512)[:, :, :, c * 128:(c + 1) * 128]
                    load_w(wv, wsrc, wk)
                    memset(g[:, 0:15], 0.0, [gk + ("h0",)])
                    memset(g[:, 2063:2078], 0.0, [gk + ("h1",)])
                    for tg in range(4):
                        glu_group(lambda k: wv[:, k, 0, :], lambda k: wv[:, k, 1, :], wk, tg,
                                  g[:, 15 + tg * 512: 15 + (tg + 1) * 512], [gk + (tg,)])
                    gkeys = [gk + (tg,) for tg in range(4)] + [gk + ("h0",), gk + ("h1",)]
                else:
                    fk = ("tfm", l, 8 + c)
                    dma_dyn(g[:, 15:2063], lambda e, c=c: tfv(l, 8 + c)[rank_of(e, 0)][:, :], [fk], [gk + (0,)])
                    dma_dyn(g[:, 0:15], lambda e, c=c: tfv(l, 8 + c)[rank_of(e, 3)][:, 2033:2048], [fk], [gk + ("h0",)])
                    dma_dyn(g[:, 2063:2078], lambda e, c=c: tfv(l, 8 + c)[rank_of(e, 1)][:, 0:15], [fk], [gk + ("h1",)])
                    tsc(g[:, 0:15], g[:, 0:15], cflag[:, 0:1], None, ALU.mult, None, [gk + ("h0",), ("cflag",)], [gk + ("h0",)])
                    tsc(g[:, 2063:2078], g[:, 2063:2078], cflag[:, 1:2], None, ALU.mult, None, [gk + ("h1",), ("cflag",)], [gk + ("h1",)])
                    gkeys = [gk + (0,), gk + ("h0",), gk + ("h1",)]
                ak = ("cv", "acc", c)
                wc = PC_CW + c * 31
                dg = biasbuf[:, :].bitcast(BF16)[:, (c % 2) * 3968:(c % 2 + 1) * 3968].rearrange("p (j n) -> p j n", j=31)
                dk = ("dg", c % 2)
                tt(dg, identf[:].unsqueeze(1).broadcast_to([128, 31, 128]),
                   pcol[:, wc:wc + 31].unsqueeze(2).broadcast_to([128, 31, 128]), ALU.mult,
                   [("identf",), ("pcol",)], [dk] + BK_DG[c % 2])
                for tg in range(4):
                    b = nbank()
                    for j in range(31):
                        mm(bank(b), dg[:, j, :], g[:, j + tg * 512: j + tg * 512 + 512], j == 0, j == 30, gkeys + [dk] + BK_DG[c % 2], [("ps", b)])
                    tsc(acc4[:, c, tg * 512:(tg + 1) * 512], bank(b), pcol[:, PC_CB + c:PC_CB + c + 1], None, ALU.add, None,
                        [("ps", b), ("pcol",)], [ak + (tg,)])
            for tg in range(4):
                akeys = [("cv", "acc", c, tg) for c in range(4)]
                sl = slice(tg * 512, (tg + 1) * 512)
                bm = nbank()
                for c in range(4):
                    mm(bank(bm), o512_f[:], acc4[:, c, sl], c == 0, c == 3, akeys + [("o512_f",)], [("ps", bm)])
                bs = nbank()
                for c in range(4):
                    ti = nxt("tmp", 4)
                    act(tmp[ti][:, 0:512], acc4[:, c, sl], AF.Square, akeys, [("tmp", ti)])
                    mm(bank(bs), o512_f[:], tmp[ti][:, 0:512], c == 0, c == 3, [("tmp", ti), ("o512_f",)], [("ps", bs)])
                tm = nxt("tmp", 4)
                P.add("scalar", lambda e, tm=tm, bm=bm: e.copy(out=tmp[tm][:, 0:512], in_=bank(bm)), [("ps", bm)], [("tmp", tm)])
                tv = nxt("tmp", 4)
                tt(tmp[tv][:, 0:512], tmp[tm][:, 0:512], tmp[tm][:, 0:512], ALU.mult, [("tmp", tm)], [("tmp", tv)])
                tt(tmp[tv][:, 0:512], bank(bs), tmp[tv][:, 0:512], ALU.subtract, [("ps", bs), ("tmp", tv)], [("tmp", tv)])
                act(tmp[tv][:, 0:512], tmp[tv][:, 0:512], AF.Ln, [("tmp", tv), ("epsc",)], [("tmp", tv)], bias=epsc[:, 0:1])
                act(tmp[tv][:, 0:512], tmp[tv][:, 0:512], AF.Exp, [("tmp", tv)], [("tmp", tv)], scale=-0.5)
                for c in range(4):
                    ty = nxt("tmp", 4)
                    while ty in (tm, tv):
                        ty = nxt("tmp", 4)
                    tt(tmp[ty][:, 0:512], acc4[:, c, sl], tmp[tm][:, 0:512], ALU.subtract, akeys + [("tmp", tm)], [("tmp", ty)])
                    tt(tmp[ty][:, 0:512], tmp[ty][:, 0:512], tmp[tv][:, 0:512], ALU.mult, [("tmp", ty), ("tmp", tv)], [("tmp", ty)])
                    act(brT[:, (4 + c) * U + tg * 512:(4 + c) * U + (tg + 1) * 512], tmp[ty][:, 0:512], AF.Silu,
                        [("tmp", ty), ("pcol",)], [("brT", 4 + c, tg)],
                        scale=pcol[:, PC_LNG + c:PC_LNG + c + 1], bias=pcol[:, PC_LNB + c:PC_LNB + c + 1])

        def diff_phase(u, l, is_sample):
            P.new_epoch("RA")
            rr["psn"] = 4
            rr["ps"] = 0
            nkc = 64 if is_sample else 16
            for h in range(4):
                if is_sample:
                    qT, kT = ar(0, 2048), ar(2048, 8192)
                    vh = ar(10240, 8192).rearrange("p (t e) -> p t e", t=64)
                    par = 0
                else:
                    par = h % 2
                    base = par * 6144
                    qT, kT = ar(base, 2048), ar(base + 2048, 2048)
                    vh = ar(base + 4096, 2048).rearrange("p (t e) -> p t e", t=16)
                kq, kk, kv = ("df", par, "q"), ("df", par, "k"), ("df", par, "v")
                s, wk = wslot()
                wv = wbuf[s][:, 0:3072].rearrange("p (k b n) -> p k b n", k=8, b=3)
                nb_w = 1 if is_sample else 3
                wsrc = w_in[l][:, OFF_DIFF:OFF_DIFF + 1536].rearrange("(kc p) (b n) -> p kc b n", p=128, n=512)[:, :, 0:nb_w, h * 128:(h + 1) * 128]
                load_w(wv[:, :, 0:nb_w, :], wsrc, wk)
                proj_fm(lambda tg: qT[:, tg * 512:(tg + 1) * 512], lambda k: wv[:, k, 0, :], 8, hT_rhs,
                        lambda tg: hT_keys(tg) + wk, lambda tg: [kq + (tg,)])
                if not is_sample:
                    proj_fm(lambda tg: kT[:, tg * 512:(tg + 1) * 512], lambda k: wv[:, k, 1, :], 8, hT_rhs,
                            lambda tg: hT_keys(tg) + wk, lambda tg: [kk + (tg,)])
                    for t4 in range(4):
                        b = nbank()
                        for tl in range(4):
                            t = t4 * 4 + tl
                            for k in range(8):
                                mm(bank(b)[:, tl * 128:(tl + 1) * 128], hT[:, k, t * 128:(t + 1) * 128], wv[:, k, 2, :],
                                   k == 0, k == 7, [("hT", t)] + wk, [("ps", b)])
                        evac(vh[:, t4 * 4:t4 * 4 + 4, :], bank(b).rearrange("p (t e) -> p t e", t=4), [("ps", b)], [kv + (t4,)])
                    kkeys_all = [kk + (tg,) for tg in range(4)]
                    vkeys_all = [kv + (t4,) for t4 in range(4)]
                else:
                    for ro in range(4):
                        dma_dyn(kT[:, ro * 2048:(ro + 1) * 2048], lambda e, ro=ro, h=h: tfv(l, h)[rank_of(e, ro)][:, :],
                                [("tfm", l, h)], [kk + (ro,)])
                        dma_dyn(vh[:, ro * 16:(ro + 1) * 16, :], lambda e, ro=ro, h=h: ttv(l, h)[rank_of(e, ro)],
                                [("ttm", l, h)], [kv + (ro,)])
                    kkeys_all = [kk + (ro,) for ro in range(4)]
                    vkeys_all = [kv + (ro,) for ro in range(4)]
                dma("sync", biasbuf[:, 0:3072].rearrange("p (m n) -> p m n", m=6), t5t_d[h], [], BKALL)
                if is_sample:
                    dma("sync", biasbuf[:, 3072:4096].rearrange("p (m n) -> p m n", m=2), t5j_d[h], [], BKALL)
                def df_class(qg, kc):
                    ro, kcl = kc // 16, kc % 16
                    special = None
                    cb = None
                    cbk = None
                    if ro == 0:
                        d = kcl * 128 - qg * 512
                        if -128 <= d <= 512:
                            special = (d + 128) // 128
                        else:
                            cb = t5c[:, h, 0:1] if d < 0 else t5c[:, h, 1:2]
                            cbk = ("t5c",)
                    elif ro == 1 and kcl == 0 and qg == 3:
                        special = 6
                    elif ro == 3 and kcl == 15 and qg == 0:
                        special = 7
                    else:
                        cb = t5cs[:, h, ro:ro + 1]
                        cbk = ("t5cs",)
                    return special, cb, cbk

                def df_S(qg, kc, c):
                    special, cb, cbk = df_class(qg, kc)
                    b = nxt("dfs", 4)
                    mm(bank(b), kT[c * 64:(c + 1) * 64, kc * 128:(kc + 1) * 128], qT[c * 64:(c + 1) * 64, qg * 512:(qg + 1) * 512],
                       True, True, kkeys_all + [kq + (qg,)], [("ps", b)])
                    pi = nxt("pT", 4)
                    if special is not None:
                        ti = nxt("tmp", 4)
                        stt(tmp[ti][:, 0:512], bank(b), SCALE, biasbuf[:, special * 512:(special + 1) * 512], ALU.mult, ALU.add,
                            [("ps", b)] + BKALL, [("tmp", ti)])
                        act(pT[pi][:, 0:512], tmp[ti][:, 0:512], AF.Exp, [("tmp", ti)], [("pT", pi)])
                    else:
                        act(pT[pi][:, 0:512], bank(b), AF.Exp, [("ps", b), cbk], [("pT", pi)], scale=SCALE, bias=cb)
                    return pi

                def df_PV(qg, kc, c, pi):
                    mm(bank(4 + c), vh[:, kc, :], pT[pi][:, 0:512], kc == 0, kc == nkc - 1, vkeys_all + [("pT", pi)], [("ps", 4 + c)])
                    mm(bank(6 + c), ones_bf[:], pT[pi][:, 0:512], kc == 0, kc == nkc - 1, [("ones_bf",), ("pT", pi)], [("ps", 6 + c)])

                def df_post(qg):
                    tA, tB, tC, tD = 0, 1, 2, 3
                    cpy(tmp[tA][:, 0:512], bank(4), [("ps", 4)], [("tmp", tA)])
                    P.add("scalar", lambda e: e.copy(out=tmp[tC][:, 0:512], in_=bank(6)), [("ps", 6)], [("tmp", tC)])
                    cpy(tmp[tB][:, 0:512], bank(5), [("ps", 5)], [("tmp", tB)])
                    P.add("scalar", lambda e: e.copy(out=tmp[tD][:, 0:512], in_=bank(7)), [("ps", 7)], [("tmp", tD)])
                    A_, B_, C_, D_ = tmp[tA][:, 0:512], tmp[tB][:, 0:512], tmp[tC][:, 0:512], tmp[tD][:, 0:512]
                    kA, kB, kC, kD = ("tmp", tA), ("tmp", tB), ("tmp", tC), ("tmp", tD)
                    dst = brT[:, (8 + h) * U + qg * 512:(8 + h) * U + (qg + 1) * 512]
                    bb = [None]

                    def s1():
                        act(C_, C_, AF.Ln, [kC], [kC])
                        act(D_, D_, AF.Ln, [kD], [kD])

                    def s2():
                        act(C_, C_, AF.Exp, [kC], [kC], scale=-1.0)
                        act(D_, D_, AF.Exp, [kD], [kD], scale=-1.0)

                    def s3():
                        tt(A_, A_, C_, ALU.mult, [kA, kC], [kA])
                        tt(B_, B_, D_, ALU.mult, [kB, kD], [kB])

                    def s4():
                        stt(A_, B_, small[:, 2:3], A_, ALU.mult, ALU.add, [kA, kB, ("lamc",)], [kA])

                    def s5():
                        act(C_, A_, AF.Square, [kA], [kC])

                    def s6():
                        bb[0] = nxt("dfs", 4)
                        mm(bank(bb[0]), ones_f[:], C_, True, True, [kC, ("ones_f",)], [("ps", bb[0])])

                    def s7():
                        act(C_, bank(bb[0]), AF.Ln, [("ps", bb[0]), ("epsc",)], [kC], bias=epsc[:, 0:1])

                    def s8():
                        act(C_, C_, AF.Exp, [kC], [kC], scale=-0.5)

                    def s9():
                        stt(dst, A_, small[:, 3:4], C_, ALU.mult, ALU.mult, [kA, kC, ("lamc",)], [("brT", 8 + h, qg)])

                    return [s1, s2, s3, s4, s5, s6, s7, s8, s9]

                steps = []
                for qg in range(4):
                    far = [kc for kc in range(nkc) if df_class(qg, kc)[0] is None]
                    spc = [kc for kc in range(nkc) if df_class(qg, kc)[0] is not None]
                    order = far + spc
                    for pos, kc in enumerate(order):
                        steps.append((qg, kc, pos == 0, pos == nkc - 1))
                DL = 1
                pend = {}
                post_q = []
                for j_ in range(-DL, len(steps)):
                    if j_ + DL < len(steps):
                        qg2, kc2, _, _ = steps[j_ + DL]
                        pend[j_ + DL] = (df_S(qg2, kc2, 0), df_S(qg2, kc2, 1))
                    if j_ >= 0:
                        qg, kc, first, last = steps[j_]
                        p0, p1 = pend.pop(j_)
                        mm(bank(4), vh[:, kc, :], pT[p0][:, 0:512], first, last, vkeys_all + [("pT", p0)], [("ps", 4)])
                        mm(bank(5), vh[:, kc, :], pT[p1][:, 0:512], first, last, vkeys_all + [("pT", p1)], [("ps", 5)])
                        mm(bank(6), ones_bf[:], pT[p0][:, 0:512], first, last, [("ones_bf",), ("pT", p0)], [("ps", 6)])
                        mm(bank(7), ones_bf[:], pT[p1][:, 0:512], first, last, [("ones_bf",), ("pT", p1)], [("ps", 7)])
                        if post_q:
                            post_q.pop(0)()
                        if last:
                            while post_q:
                                post_q.pop(0)()
                            post_q = df_post(qg)
                while post_q:
                    post_q.pop(0)()
            rr["ps"] = 0
            rr["psn"] = 8

        def merge_phase(u, l, xsrc, xdst_mid):
            P.new_epoch("RA")
            mT = arena[:, 0:16384].rearrange("p (k n) -> p k n", k=8)
            for dc in range(8):
                s, wk = wslot()
                wg = wbuf[s][:, 0:3072].rearrange("p (k b n) -> p k b n", k=8, b=3)
                wbr = wbuf[s][:, 3072:4608].rearrange("p (b k n) -> p b k n", b=3, k=4)
                load_w(wg, w_in[l][:, OFF_GATE:].rearrange("(kc p) (b n) -> p kc b n", p=128, n=1024)[:, :, :, dc * 128:(dc + 1) * 128], wk)
                load_w(wbr, w_branch[l].rearrange("b (k p) n -> p b k n", p=128)[:, :, :, dc * 128:(dc + 1) * 128], wk)
                for tg in range(4):
                    tacc = nxt("tmp", 4)
                    for b_ in range(3):
                        bg = nbank()
                        for k in range(8):
                            mm(bank(bg), wg[:, k, b_, :], hT_rhs(k, tg), k == 0, k == 7, hT_keys(tg) + wk, [("ps", bg)])
                        bp = nbank()
                        for k in range(4):
                            mm(bank(bp), wbr[:, b_, k, :], brT[:, (b_ * 4 + k) * U + tg * 512:(b_ * 4 + k) * U + (tg + 1) * 512],
                               k == 0, k == 3, [("brT", b_ * 4 + k, tg)] + wk, [("ps", bp)])
                        tg_ = nxt("tmp", 4)
                        while tg_ == tacc:
                            tg_ = nxt("tmp", 4)
                        bc = PC_BGATE + b_ * 8 + dc
                        act(tmp[tg_][:, 0:512], bank(bg), AF.Sigmoid, [("ps", bg), ("pcol",)], [("tmp", tg_)], bias=pcol[:, bc:bc + 1])
                        if b_ == 0:
                            tt(tmp[tacc][:, 0:512], bank(bp), tmp[tg_][:, 0:512], ALU.mult, [("ps", bp), ("tmp", tg_)], [("tmp", tacc)])
                        else:
                            tt(tmp[tg_][:, 0:512], bank(bp), tmp[tg_][:, 0:512], ALU.mult, [("ps", bp), ("tmp", tg_)], [("tmp", tg_)])
                            if b_ == 1:
                                tt(tmp[tacc][:, 0:512], tmp[tacc][:, 0:512], tmp[tg_][:, 0:512], ALU.add,
                                   [("tmp", tacc), ("tmp", tg_)], [("tmp", tacc)])
                            else:
                                tt(mT[:, dc, tg * 512:(tg + 1) * 512], tmp[tacc][:, 0:512], tmp[tg_][:, 0:512], ALU.add,
                                   [("tmp", tacc), ("tmp", tg_)], [("mT", dc, tg)])
            wo = biasbuf[:, :].bitcast(BF16).rearrange("p (k n) -> p k n", k=8)
            load_w(wo, w_out[l].rearrange("(k p) n -> p k n", p=128), BKALL)
            LA = 2
            held = {}
            for t in range(NT + LA):
                if t - LA >= 0:
                    xj = held.pop(t - LA)
                    prenorm_tile(xt[xj][:], ("xt", xj), t - LA, PC_GFFN)
                if t < NT:
                    xi = nxt("x", 3)
                    dma("sync", xt[xi][:], xsrc[u, t * 128:(t + 1) * 128, :], [("xd", l, u, t)], [("xt", xi)])
                    b = nbank2()
                    for nh in range(2):
                        for k in range(8):
                            mm(bank(b + nh), mT[:, k, t * 128:(t + 1) * 128], wo[:, k, nh * 512:(nh + 1) * 512], k == 0, k == 7,
                               [("mT", k, t // 4)] + BKALL, [("ps", b + nh)])
                    resid_tail(ps[:, b * 512:(b + 2) * 512], [("ps", b), ("ps", b + 1)], xi, 0, l)
                    dma("sync", xdst_mid[u, t * 128:(t + 1) * 128, :], xt[xi][:], [("xt", xi)], [("xm", l, u, t)])
                    held[t] = xi

        def resid_tail(y_ps, ykeys, xi, which, l):
            ss, sk = col()
            tj = nxt("tmp", 4)
            act(tmp[tj][:, 0:512].bitcast(BF16), y_ps, AF.Square, ykeys, [sk, ("tmp", tj)], accum_out=ss)
            rs, rk = col()
            rsqrt_col(rs, rk, ss, sk, 1.0 / D)
            stt(ytmp[:], y_ps, rs, gpost[:, which, :], ALU.mult, ALU.mult, ykeys + [rk, ("gpost",)], [("ytmp",)])
            tt(xt[xi][:], xt[xi][:], ytmp[:], ALU.add, [("xt", xi), ("ytmp",)], [("xt", xi)])

        def ffn_phase(u, l, xsrc_mid, xdst, next_a=None):
            P.new_epoch("RB")
            wfo = brT[:, 0:22528].rearrange("p (k n) -> p k n", k=NHC)
            for k0 in range(0, NHC, 4):
                k1 = min(NHC, k0 + 4)
                load_w(wfo[:, k0:k1, :], w_ffn_out[l].rearrange("(k p) n -> p k n", p=128)[:, k0:k1, :], [("wfo", k0)])
            wfo_keys = [("wfo", k0) for k0 in range(0, NHC, 4)] + [("wfo",)]
            aT = arena[:, 0:22528].rearrange("p (k n) -> p k n", k=NHC)
            for half in range(2):
                if half == 0:
                    P.new_epoch("RA")
                for hc in range(NHC):
                    s, wk = wslot()
                    wv = wbuf[s][:, 0:2048].rearrange("p (k b n) -> p k b n", k=8, b=2)
                    load_w(wv, w_ffn_in[l].rearrange("(kc p) (b n) -> p kc b n", p=128, n=FFN_H)[:, :, :, hc * 128:(hc + 1) * 128], wk)
                    for t2 in range(2):
                        tg = half * 2 + t2
                        bg = nbank()
                        for k in range(8):
                            mm(bank(bg), wv[:, k, 0, :], hT_rhs(k, tg), k == 0, k == 7, hT_keys(tg) + wk, [("ps", bg)])
                        bu = nbank()
                        for k in range(8):
                            mm(bank(bu), wv[:, k, 1, :], hT_rhs(k, tg), k == 0, k == 7, hT_keys(tg) + wk, [("ps", bu)])
                        ti = nxt("tmp", 4)
                        act(tmp[ti][:, 0:512], bank(bg), AF.Silu, [("ps", bg)], [("tmp", ti)])
                        tt(aT[:, hc, t2 * 512:(t2 + 1) * 512], bank(bu), tmp[ti][:, 0:512], ALU.mult,
                           [("ps", bu), ("tmp", ti)], [("aT", hc, t2)])
                if half == 1 and next_a is not None:
                    next_a()
                for tl in range(8):
                    t = half * 8 + tl
                    xi = nxt("x", 3)
                    dma("sync", xt[xi][:], xsrc_mid[u, t * 128:(t + 1) * 128, :], [("xm", l, u, t)], [("xt", xi)])
                    b = nbank2()
                    for nh in range(2):
                        for hc in range(NHC):
                            mm(bank(b + nh), aT[:, hc, tl * 128:(tl + 1) * 128], wfo[:, hc, nh * 512:(nh + 1) * 512],
                               hc == 0, hc == NHC - 1, [("aT", hc, tl // 4)] + wfo_keys, [("ps", b + nh)])
                    resid_tail(ps[:, b * 512:(b + 2) * 512], [("ps", b), ("ps", b + 1)], xi, 1, l)
                    dma("sync", xdst[u, t * 128:(t + 1) * 128, :], xt[xi][:], [("xt", xi)], [("xd", l + 1, u, t)])

        def layer_setup(l):
            lam_init = 0.8 - 0.6 * math.exp(-0.3 * l)
            dma("sync", pcol[:], pcol_d[l], [], [("pcol",)])
            dma("sync", gpost[:, 0, :], prow_d[l, 0:1, :].broadcast_to([128, D]), [], [("gpost",)])
            dma("sync", gpost[:, 1, :], prow_d[l, 1:2, :].broadcast_to([128, D]), [], [("gpost",)])
            dma("sync", lamb, lam_d[l:l + 1, :].broadcast_to([128, 256]), [], [("ytmp",)])
            P.add("vector", lambda e: e.tensor_tensor(out=lamb[:, 0:64], in0=lamb[:, 0:64], in1=lamb[:, 64:128], op=ALU.mult), [("ytmp",)], [("ytmp",)])
            P.add("vector", lambda e: e.tensor_tensor(out=lamb[:, 128:192], in0=lamb[:, 128:192], in1=lamb[:, 192:256], op=ALU.mult), [("ytmp",)], [("ytmp",)])
            P.add("vector", lambda e: e.reduce_sum(out=small[:, 0:1], in_=lamb[:, 0:64], axis=mybir.AxisListType.X), [("ytmp",)], [("lamc",)])
            P.add("vector", lambda e: e.reduce_sum(out=small[:, 1:2], in_=lamb[:, 128:192], axis=mybir.AxisListType.X), [("ytmp",), ("lamc",)], [("lamc",)])
            act(small[:, 0:2], small[:, 0:2], AF.Exp, [("lamc",)], [("lamc",)])
            stt(small[:, 2:3], small[:, 0:1], -1.0, small[:, 1:2], ALU.mult, ALU.add, [("lamc",)], [("lamc",)])
            tsc(small[:, 2:3], small[:, 2:3], -lam_init, None, ALU.add, None, [("lamc",)], [("lamc",)])
            tsc(small[:, 3:4], pcol[:, PC_SUB:PC_SUB + 1], 1.0 - lam_init, None, ALU.mult, None, [("pcol",), ("lamc",)], [("lamc",)])

        for l in range(depth):
            xsrc = xin if l == 0 else x1
            xdst = yout if l == depth - 1 else x1
            layer_setup(l)
            if with_sample:
                P.mark(f"L{l} PRE")
                sample_prepass(xsrc, l)
            units = list(range(1, NU)) + ([0] if with_sample else [])
            for ui, u in enumerate(units):
                smp = (u == 0)
                if ui == 0:
                    P.mark(f"L{l}U{u} A")
                    phase_a(xsrc, u, l)
                P.mark(f"L{l}U{u} NA")
                na_phase(u, l, smp)
                P.mark(f"L{l}U{u} CV")
                conv_phase(u, l, smp)
                P.mark(f"L{l}U{u} DF")
                diff_phase(u, l, smp)
                P.mark(f"L{l}U{u} MG")
                merge_phase(u, l, xsrc, xmid)
                P.mark(f"L{l}U{u} FF")
                nxt_a = None
                if ui + 1 < len(units):
                    nu_ = units[ui + 1]
                    nxt_a = (lambda nu_=nu_: phase_a(xsrc, nu_, l))
                ffn_phase(u, l, xmid, xdst, nxt_a)
        P.mark("END")

        P.finalize()
        P.alloc_sems(nc, es)
        block = es.enter_context(nc.Block())

        @block.sync
        def _(e):
            P.emit("sync", e, final_wait=True)

        @block.gpsimd
        def _(e):
            P.emit("gpsimd", e)

        @block.tensor
        def _(e):
            P.emit("tensor", e)

        @block.vector
        def _(e):
            P.emit("vector", e)

        @block.scalar
        def _(e):
            P.emit("scalar", e)

    return nc, P


def prepare_inputs(inp, n_prompt=4, cores=8):
    t5 = np.asarray(inp["t5_bias"], np.float32)
    p = np.arange(128)[:, None]
    j = np.arange(512)[None, :]
    t5t = np.empty((4, 128, 6, 512), np.float32)
    for m in range(6):
        bk = _t5_bucket(((m - 1) * 128 + p - j).astype(np.int32))
        for h in range(4):
            t5t[h, :, m, :] = t5[bk, h]
    t5c = np.empty((128, 4, 2), np.float32)
    t5c[:, :, 0] = t5[15][None, :]
    t5c[:, :, 1] = t5[31][None, :]
    rpb = np.asarray(inp["na_rpb"], np.float32)
    nabp = _nab_table(rpb, 32, 0)
    shared = {
        "w_in": np.ascontiguousarray(inp["w_in"], np.float32),
        "w_branch": np.ascontiguousarray(inp["w_branch"], np.float32),
        "w_out": np.ascontiguousarray(inp["w_out"], np.float32),
        "w_ffn_in": np.ascontiguousarray(inp["w_ffn_in"], np.float32),
        "w_ffn_out": np.ascontiguousarray(inp["w_ffn_out"], np.float32),
        "pcol": np.stack([_pack_pcol(inp, l) for l in range(DEPTH)]),
        "prow": np.stack([np.stack([inp["ln_mix_post"][l], inp["ln_ffn_post"][l]]) for l in range(DEPTH)]).astype(np.float32),
        "lam": np.asarray(inp["diff_lambda"], np.float32).reshape(DEPTH, 256),
        "t5t": t5t, "t5c": t5c, "nabp": nabp,
        "ident": np.eye(128, dtype=np.float32),
    }
    nabs_q = [_nab_table(rpb, 128, 16 * q) for q in range(4)]
    in_maps = []
    xp = np.asarray(inp["x_prompt"], np.float32)
    xs = np.asarray(inp["x_sample"], np.float32)
    for c in range(cores):
        q, s = c % 4, c // 4
        m = dict(shared)
        xin = np.empty((1 + n_prompt, U, D), np.float32)
        xin[0] = xs[s, q * U:(q + 1) * U]
        for i in range(n_prompt):
            xin[1 + i] = xp[c * n_prompt + i]
        m["xin"] = xin
        t5cs = np.empty((128, 4, 4), np.float32)
        for ro in range(4):
            r = (q + ro) % 4
            t5cs[:, :, ro] = (t5[31] if r > q else t5[15])[None, :]
        m["t5cs"] = t5cs
        t5j = np.empty((4, 128, 2, 512), np.float32)
        r1 = ((q + 1) % 4 - q) * U + 0 * 128 + p - (3 * 512 + j)
        r3 = ((q + 3) % 4 - q) * U + 15 * 128 + p - j
        b1, b3 = _t5_bucket(r1.astype(np.int32)), _t5_bucket(r3.astype(np.int32))
        for h in range(4):
            t5j[h, :, 0, :] = t5[b1, h]
            t5j[h, :, 1, :] = t5[b3, h]
        m["t5j"] = t5j
        m["nabs"] = nabs_q[q]
        cf = np.zeros((128, 2), np.float32)
        cf[:, 0] = 1.0 if q > 0 else 0.0
        cf[:, 1] = 1.0 if q < 3 else 0.0
        m["cflag"] = cf
        in_maps.append(m)
    return in_maps


_CACHE = {}


def kernel(**inputs):
    inp = {k: np.asarray(v) for k, v in inputs.items()}
    if "nc" not in _CACHE:
        _CACHE["nc"] = build_program()[0]
    nc = _CACHE["nc"]
    in_maps = prepare_inputs(inp)
    res = run_bass_kernel_spmd(nc, in_maps, core_ids=list(range(8)))
    y_prompt = np.empty((32, U, D), np.float32)
    y_sample = np.empty((2, 8192, D), np.float32)
    for c in range(8):
        yo = np.asarray(res.results[c]["yout"], np.float32)
        q, s = c % 4, c // 4
        y_sample[s, q * U:(q + 1) * U] = yo[0]
        for i in range(4):
            y_prompt[c * 4 + i] = yo[1 + i]
    return (y_prompt, y_sample)
```

```python
import math
from contextlib import ExitStack

import numpy as np
import concourse.bass as bass
import concourse.mybir as mybir
from concourse.bass_utils import run_bass_kernel_spmd

F32 = mybir.dt.float32
BF16 = mybir.dt.bfloat16
AF = mybir.ActivationFunctionType
ALU = mybir.AluOpType

D = 1024
DEPTH = 2
U = 2048
NT = 16
IN_COLS = 7168
OFF_CONV = 1536
OFF_DIFF = 2560
OFF_GATE = 4096
FFN_H = 2816
NHC = 22
EPS = 1e-6
NEG = -30000.0
SCALE = 0.125
NPC = 177
PC_GPRE, PC_GFFN, PC_BGATE, PC_CW, PC_CB, PC_LNG, PC_LNB, PC_SUB = 0, 8, 16, 40, 164, 168, 172, 176
NAB_GRP = {"b0": (0, list(range(-2, 4))), "b1": (6, list(range(-2, 3))), "int": (11, list(range(-2, 3))),
           "b14": (16, list(range(-2, 3))), "b15": (21, list(range(-3, 3)))}
CH = 8000
KQ = 8


def _grp_of_block(i):
    return {0: "b0", 1: "b1", 14: "b14", 15: "b15"}.get(i, "int")


class _Op:
    __slots__ = ("eng", "fn", "deps", "kind", "sig", "needs_sig")

    def __init__(self, eng, fn, deps, kind):
        self.eng, self.fn, self.deps, self.kind = eng, fn, deps, kind
        self.sig = None
        self.needs_sig = kind != "c"


class Prog:
    def __init__(self):
        self.marks = []
        self.ops = []
        self.last_w = {}
        self.readers = {}
        self.reg_cur = {}
        self.reg_prev = {}

    def mark(self, label):
        cnt = {}
        for op in self.ops:
            cnt[op.eng] = cnt.get(op.eng, 0) + 1
        self.marks.append((label, cnt))

    def new_epoch(self, R):
        cur = self.reg_cur.pop(R, ({}, []))
        self.reg_prev[R] = list(cur[0].values()) + cur[1]

    def add(self, eng, fn, reads=(), writes=(), kind="c", regions=()):
        i = len(self.ops)
        deps = set()
        for R in regions:
            deps.update(self.reg_prev.get(R, ()))
            cur = self.reg_cur.setdefault(R, ({}, []))
            if kind == "d":
                cur[1].append(i)
            else:
                cur[0][eng] = i
        lw, rd = self.last_w, self.readers
        for k in reads:
            w = lw.get(k)
            if w is not None:
                deps.add(w)
        for k in writes:
            w = lw.get(k)
            if w is not None:
                deps.add(w)
            r = rd.get(k)
            if r:
                deps.update(r[0].values())
                deps.update(r[1])
        for k in reads:
            r = rd.get(k)
            if r is None:
                r = rd[k] = ({}, [])
            if kind == "d":
                r[1].append(i)
            else:
                r[0][eng] = i
        for k in writes:
            lw[k] = i
            rd[k] = ({}, [])
        deps.discard(i)
        self.ops.append(_Op(eng, fn, deps, kind))
        return i

    def finalize(self):
        ops = self.ops
        for op in ops:
            nd = []
            for d in op.deps:
                a = ops[d]
                if a.eng == "tensor" and op.eng == "tensor" and a.kind == "c" and op.kind == "c":
                    continue
                a.needs_sig = True
                nd.append(d)
            op.deps = nd
        self.ccount = {}
        self.dcount = {}
        self.ncc = 0
        for op in ops:
            if op.kind == "d":
                n = self.dcount.get(op.eng, 0)
                self.dcount[op.eng] = n + 1
                op.sig = ("d", op.eng, n)
            elif op.kind == "cc":
                op.sig = ("cc", self.ncc)
                self.ncc += 1
            elif op.needs_sig:
                k = self.ccount.get(op.eng, 0) + 1
                self.ccount[op.eng] = k
                op.sig = ("c", op.eng, k)
        last = {}
        for i, op in enumerate(ops):
            if op.kind == "d":
                lst = last.setdefault(op.eng, [])
                if len(lst) >= KQ:
                    op.deps.append(lst[-KQ])
                lst.append(i)

    def alloc_sems(self, nc, es):
        self.csem = {e: [es.enter_context(nc.semaphore(f"c_{e}_{j}")) for j in range((k + CH - 1) // CH)]
                     for e, k in self.ccount.items()}
        self.dsem = {e: [es.enter_context(nc.semaphore(f"d_{e}_{j}")) for j in range(KQ)] for e in self.dcount}
        self.ccsem = [es.enter_context(nc.semaphore(f"cc_{j}")) for j in range(self.ncc)]

    def _resolve(self, sig):
        if sig[0] == "c":
            k = sig[2]
            return self.csem[sig[1]][(k - 1) // CH], (k - 1) % CH + 1, 1
        if sig[0] == "d":
            n = sig[2]
            return self.dsem[sig[1]][n % KQ], 16 * (n // KQ + 1), 16
        return self.ccsem[sig[1]], 1, None

    def emit(self, engname, e, final_wait=False):
        waited = {}
        ops = self.ops
        for op in ops:
            if op.eng != engname:
                continue
            for d in op.deps:
                sem, val, _ = self._resolve(ops[d].sig)
                key = id(sem)
                if waited.get(key, 0) < val:
                    e.wait_ge(sem, val)
                    waited[key] = val
            try:
                ins = op.fn(e)
            except Exception:
                import inspect
                cv = inspect.getclosurevars(op.fn)
                print("EMIT FAIL", engname, op.kind, {k: (str(v)[:200]) for k, v in cv.nonlocals.items()}, flush=True)
                raise
            if op.sig is not None:
                sem, val, inc = self._resolve(op.sig)
                if inc is None:
                    ins.then_inc(sem)
                else:
                    ins.then_inc(sem, inc)
        if final_wait:
            for q, n in self.dcount.items():
                for j in range(min(KQ, n)):
                    cnt = (n - 1 - j) // KQ + 1
                    e.wait_ge(self.dsem[q][j], 16 * cnt)


def _t5_bucket(rel):
    nb, max_exact = 16, 8
    ret = np.where(rel > 0, nb, 0)
    n = np.abs(rel)
    nf = np.maximum(n, 1).astype(np.float32)
    large = max_exact + (np.log(nf / np.float32(max_exact)) / np.float32(math.log(128 / max_exact))
                         * np.float32(nb - max_exact)).astype(np.int32)
    large = np.minimum(large, nb - 1)
    return ret + np.where(n < max_exact, n, large)


def _na_tile(rpb_h, R, gi, gj):
    if gj < 0 or gj >= R // 2 or gi < 0 or gi >= R // 2:
        return np.full((128, 128), NEG, np.float32)
    kp = np.arange(128)
    krow, kcol = 2 * gj + kp // 64, kp % 64
    qrow, qcol = 2 * gi + kp // 64, kp % 64
    rs = np.clip(qrow - 4, 0, R - 8)
    cs = np.clip(qcol - 8, 0, 48)
    valid = ((krow[:, None] >= rs[None, :]) & (krow[:, None] < rs[None, :] + 8)
             & (kcol[:, None] >= cs[None, :]) & (kcol[:, None] < cs[None, :] + 16))
    val = rpb_h[np.clip(krow[:, None] - qrow[None, :] + 7, 0, 14), np.clip(kcol[:, None] - qcol[None, :] + 15, 0, 30)]
    return np.where(valid, val, np.float32(NEG)).astype(np.float32)


def _nab_table(rpb, R, base_block):
    L = rpb.shape[0]
    out = np.empty((L, 8, 128, 27, 128), np.float32)
    loc = {"b0": 0, "b1": 1, "int": 5, "b14": 14, "b15": 15}
    for l in range(L):
        for h in range(8):
            for g, (off, jrels) in NAB_GRP.items():
                gi = base_block + loc[g]
                for s, jr in enumerate(jrels):
                    out[l, h, :, off + s, :] = _na_tile(rpb[l, h], R, gi, gi + jr)
    return out


def _pack_pcol(inp, l):
    pc = np.zeros((128, NPC), np.float32)
    pc[:, PC_GPRE:PC_GPRE + 8] = inp["ln_mix_pre"][l].reshape(8, 128).T
    pc[:, PC_GFFN:PC_GFFN + 8] = inp["ln_ffn_pre"][l].reshape(8, 128).T
    pc[:, PC_BGATE:PC_BGATE + 24] = inp["b_gate"][l].reshape(24, 128).T
    pc[:, PC_CW:PC_CW + 124] = inp["conv_dw_w"][l].reshape(31, 4, 128).transpose(2, 1, 0).reshape(128, 124)
    pc[:, PC_CB:PC_CB + 4] = inp["conv_dw_b"][l].reshape(4, 128).T
    pc[:, PC_LNG:PC_LNG + 4] = inp["conv_ln_g"][l].reshape(4, 128).T
    pc[:, PC_LNB:PC_LNB + 4] = inp["conv_ln_b"][l].reshape(4, 128).T
    pc[:, PC_SUB] = inp["diff_subln_g"][l]
    return pc


def build_program(n_prompt=4, depth=DEPTH, with_sample=True, taps=()):
    nc = bass.Bass("TRN2", target_bir_lowering=False)
    NU = 1 + n_prompt
    P = Prog()
    dt = nc.dram_tensor

    xin = dt("xin", [NU, U, D], F32, kind="ExternalInput").ap()
    w_in = dt("w_in", [DEPTH, D, IN_COLS], F32, kind="ExternalInput").ap()
    w_branch = dt("w_branch", [DEPTH, 3, 512, D], F32, kind="ExternalInput").ap()
    w_out = dt("w_out", [DEPTH, D, D], F32, kind="ExternalInput").ap()
    w_ffn_in = dt("w_ffn_in", [DEPTH, D, 2 * FFN_H], F32, kind="ExternalInput").ap()
    w_ffn_out = dt("w_ffn_out", [DEPTH, FFN_H, D], F32, kind="ExternalInput").ap()
    pcol_d = dt("pcol", [DEPTH, 128, NPC], F32, kind="ExternalInput").ap()
    prow_d = dt("prow", [DEPTH, 2, D], F32, kind="ExternalInput").ap()
    lam_d = dt("lam", [DEPTH, 256], F32, kind="ExternalInput").ap()
    t5t_d = dt("t5t", [4, 128, 6, 512], F32, kind="ExternalInput").ap()
    t5c_d = dt("t5c", [128, 4, 2], F32, kind="ExternalInput").ap()
    t5cs_d = dt("t5cs", [128, 4, 4], F32, kind="ExternalInput").ap()
    t5j_d = dt("t5j", [4, 128, 2, 512], F32, kind="ExternalInput").ap()
    nabp_d = dt("nabp", [DEPTH, 8, 128, 27, 128], F32, kind="ExternalInput").ap()
    nabs_d = dt("nabs", [DEPTH, 8, 128, 27, 128], F32, kind="ExternalInput").ap()
    cflag_d = dt("cflag", [128, 2], F32, kind="ExternalInput").ap()
    ident_d = dt("ident", [128, 128], F32, kind="ExternalInput").ap()
    yout = dt("yout", [NU, U, D], F32, kind="ExternalOutput").ap()
    tap_d = {}
    for name, shape in taps:
        tap_d[name] = dt("tap_" + name, list(shape), F32, kind="ExternalOutput").ap()

    xmid = dt("xmid", [NU, U, D], F32).ap()
    x1 = dt("x1", [NU, U, D], F32).ap()
    cf = [[dt(f"cf{l}_{s_}", [128, U], BF16).ap() for s_ in range(12)] for l in range(depth)]
    ct = [[dt(f"ct{l}_{j_}", [U, 128], BF16).ap() for j_ in range(8)] for l in range(depth)]
    tfm = [[dt(f"tfm{l}_{s_}", [4 * 128, U], BF16).ap() for s_ in range(12)] for l in range(depth)]
    ttm = [[dt(f"ttm{l}_{j_}", [4 * U, 128], BF16).ap() for j_ in range(8)] for l in range(depth)]

    es = ExitStack()
    with es:
        S = lambda name, shape, dtype: es.enter_context(nc.sbuf_tensor(name, shape, dtype))
        ident = S("ident_s", [128, 128], BF16)
        identf = S("identf", [128, 128], F32)
        ones_bf = S("ones_bf", [128, 128], BF16)
        ones_f = S("ones_f", [128, 128], F32)
        o512_f = S("o512_f", [128, 128], F32)
        epsc = S("epsc", [128, 1], F32)
        hT = S("hT", [128, 8, U], BF16)
        brT = S("brT", [128, 12 * U], BF16)
        arena = S("arena", [128, 22528], BF16)
        wbuf = [S(f"wbuf{i}", [128, 4608], BF16) for i in range(2)]
        xt = [S(f"xt{i}", [128, D], F32) for i in range(3)]
        hb = [S(f"hb{i}", [128, D], BF16) for i in range(2)]
        pT = [S(f"pT{i}", [128, 768], BF16) for i in range(4)]
        tmp = [S(f"tmp{i}", [128, 768], F32) for i in range(4)]
        ytmp = S("ytmp", [128, D], F32)
        lamb = ytmp[:, 0:256]
        biasbuf = S("biasbuf", [128, 4096], F32)
        gpost = S("gpost", [128, 2, D], F32)
        pcol = S("pcol_s", [128, NPC], F32)
        small = S("small", [128, 64], F32)
        t5c = S("t5c_s", [128, 4, 2], F32)
        t5cs = S("t5cs_s", [128, 4, 4], F32)
        cflag = S("cflag_s", [128, 2], F32)
        ps = es.enter_context(nc.psum_tensor("ps", [128, 4096], F32))

        rr = {"ps": 0, "x": 0, "hb": 0, "pT": 0, "tmp": 0, "wb": 0, "ev": 0, "col": 0, "psn": 8, "dfs": 0}

        def nxt(name, n):
            v = rr[name]
            rr[name] = (v + 1) % n
            return v

        def bank(b, n=512):
            return ps[:, b * 512:b * 512 + n]

        def nbank():
            return nxt("ps", rr["psn"])

        def nbank2():
            b = rr["ps"]
            if b % 2:
                b = (b + 1) % 8
            b = b % 8
            rr["ps"] = (b + 2) % 8
            return b

        def col():
            c = 8 + nxt("col", 48)
            return small[:, c:c + 1], ("small", c)

        def regs(*aps):
            r = set()
            for a in aps:
                if a is None or isinstance(a, (int, float)):
                    continue
                n = a.name
                if n == "arena":
                    r.add("RA")
                elif n == "brT":
                    r.add("RB")
            return tuple(r)

        def dma(q, out, in_, reads, writes):
            return P.add(q, lambda e, out=out, in_=in_: e.dma_start(out=out, in_=in_), reads, writes, kind="d", regions=regs(out, in_))

        def dma_dyn(out, in_fn, reads, writes):
            return P.add("sync", lambda e: e.dma_start(out=out, in_=in_fn(e)), reads, writes, kind="d", regions=regs(out))

        def mm(out, lhsT, rhs, start, stop, reads, writes):
            return P.add("tensor", lambda e: e.matmul(out, lhsT=lhsT, rhs=rhs, start=start, stop=stop), reads, writes,
                         regions=regs(lhsT, rhs))

        def tr(out, in_, reads, writes):
            return P.add("tensor", lambda e: e.transpose(out=out, in_=in_, identity=ident[:]), reads + [("ident",)], writes,
                         regions=regs(in_))

        def act(out, in_, func, reads, writes, **kw):
            return P.add("scalar", lambda e: e.activation(out=out, in_=in_, func=func, **kw), reads, writes, regions=regs(out, in_))

        def amul(out, in_, mul, reads, writes):
            return P.add("scalar", lambda e: e.mul(out=out, in_=in_, mul=mul), reads, writes, regions=regs(out, in_))

        def tt(out, in0, in1, op, reads, writes, eng="vector"):
            return P.add(eng, lambda e: e.tensor_tensor(out=out, in0=in0, in1=in1, op=op), reads, writes, regions=regs(out, in0, in1))

        def tsc(out, in0, s1, s2, op0, op1, reads, writes, eng="vector"):
            if op1 is None:
                return P.add(eng, lambda e: e.tensor_scalar(out=out, in0=in0, scalar1=s1, scalar2=None, op0=op0), reads, writes,
                             regions=regs(out, in0))
            return P.add(eng, lambda e: e.tensor_scalar(out=out, in0=in0, scalar1=s1, scalar2=s2, op0=op0, op1=op1), reads, writes,
                         regions=regs(out, in0))

        def stt(out, in0, scalar, in1, op0, op1, reads, writes, eng="vector"):
            return P.add(eng, lambda e: e.scalar_tensor_tensor(out=out, in0=in0, scalar=scalar, in1=in1, op0=op0, op1=op1), reads, writes,
                         regions=regs(out, in0, in1))

        def cpy(out, in_, reads, writes, eng="vector"):
            return P.add(eng, lambda e: e.tensor_copy(out=out, in_=in_), reads, writes, regions=regs(out, in_))

        def memset(ap, val, writes, eng="vector"):
            return P.add(eng, lambda e: e.memset(ap, val), [], writes, regions=regs(ap))

        def recip(out, in_, reads, writes):
            return P.add("vector", lambda e: e.reciprocal(out=out, in_=in_), reads, writes, regions=regs(out, in_))

        def fence(eng, old, new):
            raise RuntimeError("unused")

        def evac(out, in_, reads, writes):
            if nxt("ev", 2) == 0:
                return cpy(out, in_, reads, writes)
            return P.add("scalar", lambda e: e.copy(out=out, in_=in_), reads, writes, regions=regs(out, in_))

        def rsqrt_col(dst, dkey, src, skey, scale):
            act(dst, src, AF.Ln, [skey, ("epsc",)], [dkey], scale=scale, bias=epsc[:, 0:1])
            act(dst, dst, AF.Exp, [dkey], [dkey], scale=-0.5)

        def tap(name, idx, src, reads):
            if name in tap_d:
                dma("sync", tap_d[name][idx], src, reads, [("tap", name, str(idx))])

        def ar(off, n):
            return arena[:, off:off + n]

        AR_ALL = [("ar", i) for i in range(11)]
        BKALL = [("bb", i) for i in range(5)]
        BK_NA = {"A": [("bb", 0)], "B": [("bb", 1), ("bb", 2)], "C": [("bb", 3)]}
        BK_DG = [[("bb", 0), ("bb", 1)], [("bb", 2), ("bb", 3), ("bb", 4)]]

        def arkeys(off, n):
            return [("ar", i) for i in range(off // 2048, (off + n - 1) // 2048 + 1)]

        dma("sync", identf[:], ident_d[:, :], [], [("identf",)])
        cpy(ident[:], identf[:], [("identf",)], [("ident",)])
        memset(ones_bf[:], 1.0, [("ones_bf",)])
        memset(ones_f[:], 1.0 / 128, [("ones_f",)])
        memset(o512_f[:], 1.0 / 512, [("o512_f",)])
        memset(epsc[:], EPS, [("epsc",)])
        dma("sync", t5c[:], t5c_d[:, :, :], [], [("t5c",)])
        dma("sync", t5cs[:], t5cs_d[:, :, :], [], [("t5cs",)])
        dma("sync", cflag[:], cflag_d[:, :], [], [("cflag",)])

        _pid = {}

        def rank_of(e, ro):
            if id(e) not in _pid:
                _pid[id(e)] = e.partition_id() % 4
            return (_pid[id(e)] + ro) % 4

        def load_w(dst_ap, src_ap, wkeys):
            if len(dst_ap.shape) == 4:
                ax = 2 if dst_ap.shape[2] <= dst_ap.shape[1] else 1
                for i_ in range(dst_ap.shape[ax]):
                    if ax == 2:
                        dma("gpsimd", dst_ap[:, :, i_, :], src_ap[:, :, i_, :], [], wkeys)
                    else:
                        dma("gpsimd", dst_ap[:, i_, :, :], src_ap[:, i_, :, :], [], wkeys)
                return
            dma("gpsimd", dst_ap, src_ap, [], wkeys)

        def win_cols(l, c0, n):
            return w_in[l].rearrange("(kc p) n -> p kc n", p=128)[:, :, c0:c0 + n]

        def wslot():
            s = nxt("wb", 2)
            return s, [("wb", s)]

        def prenorm_tile(xtile, xkey, t, gcol0):
            ss, sk = col()
            hi = nxt("hb", 2)
            act(hb[hi][:], xtile, AF.Square, [xkey], [sk, ("hb", hi)], accum_out=ss)
            rs, rk = col()
            rsqrt_col(rs, rk, ss, sk, 1.0 / D)
            amul(hb[hi][:], xtile, rs, [xkey, rk], [("hb", hi)])
            b = nbank()
            pb = bank(b).bitcast(BF16).rearrange("p (a b) -> p a b", a=8)
            for k in range(8):
                tr(pb[:, k, :], hb[hi][:, k * 128:(k + 1) * 128], [("hb", hi)], [("ps", b)])
            tt(hT[:, :, t * 128:(t + 1) * 128], pb,
               pcol[:, gcol0:gcol0 + 8].unsqueeze(2).broadcast_to([128, 8, 128]), ALU.mult,
               [("ps", b), ("pcol",)], [("hT", t)])

        def phase_a(xsrc, u, l):
            for t in range(NT):
                xi = nxt("x", 3)
                dma("sync", xt[xi][:], xsrc[u, t * 128:(t + 1) * 128, :], [("xd", l, u, t)], [("xt", xi)])
                prenorm_tile(xt[xi][:], ("xt", xi), t, PC_GPRE)

        def proj_fm(dst_fn, wk_fn, nk, rhs_fn, rkeys_fn, dkeys_fn, post=None):
            for tg in range(4):
                b = nbank()
                for k in range(nk):
                    mm(bank(b), wk_fn(k), rhs_fn(k, tg), k == 0, k == nk - 1, rkeys_fn(tg), [("ps", b)])
                if post is None:
                    evac(dst_fn(tg), bank(b), [("ps", b)], dkeys_fn(tg))
                else:
                    post(tg, b)

        hT_rhs = lambda k, tg: hT[:, k, tg * 512:(tg + 1) * 512]
        hT_keys = lambda tg: [("hT", t) for t in range(tg * 4, tg * 4 + 4)]

        def sample_prepass(xsrc, l):
            phase_a(xsrc, 0, l)
            P.new_epoch("RA")
            for sec, c0 in ((0, OFF_DIFF + 512), (1, 512)):
                s, wk = wslot()
                wv = wbuf[s][:, 0:4096].rearrange("p (k n) -> p k n", k=8)
                load_w(wv, win_cols(l, c0, 512), wk)
                for c in range(4):
                    st = ar((c % 2) * 2048, 2048)
                    skeys = lambda tg, c=c: [("ar", c % 2, "s", tg)]
                    proj_fm(lambda tg, st=st: st[:, tg * 512:(tg + 1) * 512], lambda k, c=c: wv[:, k, c * 128:(c + 1) * 128], 8,
                            hT_rhs, lambda tg: hT_keys(tg) + wk, skeys)
                    dma("sync", cf[l][sec * 4 + c], st,
                        [("ar", c % 2, "s", tg) for tg in range(4)], [("cf", l, sec * 4 + c)])
            sa, wka = wslot()
            wa = wbuf[sa][:, 0:4096].rearrange("p (k n) -> p k n", k=8)
            load_w(wa, win_cols(l, OFF_CONV, 512), wka)
            sg_, wkg = wslot()
            wg = wbuf[sg_][:, 0:4096].rearrange("p (k n) -> p k n", k=8)
            load_w(wg, win_cols(l, OFF_CONV + 512, 512), wkg)
            for c in range(4):
                st = ar((c % 2) * 2048, 2048)
                for tg in range(4):
                    glu_group(lambda k: wa[:, k, c * 128:(c + 1) * 128], lambda k: wg[:, k, c * 128:(c + 1) * 128],
                              wka + wkg, tg, st[:, tg * 512:(tg + 1) * 512], [("ar", c % 2, "s", tg)])
                dma("sync", cf[l][8 + c], st,
                    [("ar", c % 2, "s", tg) for tg in range(4)], [("cf", l, 8 + c)])
            sv, wkv = wslot()
            wv1 = wbuf[sv][:, 0:4096].rearrange("p (k n) -> p k n", k=8)
            load_w(wv1, win_cols(l, OFF_DIFF + 1024, 512), wkv)
            sn, wkn = wslot()
            wv2 = wbuf[sn][:, 0:4096].rearrange("p (k n) -> p k n", k=8)
            load_w(wv2, win_cols(l, 1024, 512), wkn)
            stg = ar(4096, 16384).rearrange("p (t n) -> p t n", t=16)
            for t in range(NT):
                sk = [("ar", "v", t)]
                b = nbank()
                for k in range(8):
                    mm(bank(b), hT[:, k, t * 128:(t + 1) * 128], wv1[:, k, :], k == 0, k == 7, [("hT", t)] + wkv, [("ps", b)])
                evac(stg[:, t, 0:512], bank(b), [("ps", b)], sk)
                b = nbank()
                for k in range(8):
                    mm(bank(b), hT[:, k, t * 128:(t + 1) * 128], wv2[:, k, :], k == 0, k == 7, [("hT", t)] + wkn, [("ps", b)])
                evac(stg[:, t, 512:1024], bank(b), [("ps", b)], sk)
            for j_ in range(8):
                dma("sync", ct[l][j_].rearrange("(t p) n -> p t n", p=128), stg[:, :, j_ * 128:(j_ + 1) * 128],
                    [("ar", "v", t) for t in range(NT)], [("ct", l, j_)])
            rg = [[0, 1, 2, 3], [4, 5, 6, 7]]
            for s_ in range(12):
                P.add("gpsimd", lambda e, s_=s_: e.collective_compute("AllGather", ALU.bypass, replica_groups=rg,
                                                                      ins=[cf[l][s_].opt()], outs=[tfm[l][s_].opt()]),
                      [("cf", l, s_)], [("tfm", l, s_)], kind="cc")
            for j_ in range(8):
                P.add("gpsimd", lambda e, j_=j_: e.collective_compute("AllGather", ALU.bypass, replica_groups=rg,
                                                                      ins=[ct[l][j_].opt()], outs=[ttm[l][j_].opt()]),
                      [("ct", l, j_)], [("ttm", l, j_)], kind="cc")

        def tfv(l, s_):
            return tfm[l][s_].rearrange("(r m) n -> r m n", r=4)

        def ttv(l, j_):
            return ttm[l][j_].rearrange("(r t p) n -> r p t n", r=4, p=128)

        def glu_group(wa_fn, wg_fn, wkeys, tg, dst, dkeys):
            ba = nbank()
            for k in range(8):
                mm(bank(ba), wa_fn(k), hT_rhs(k, tg), k == 0, k == 7, hT_keys(tg) + wkeys, [("ps", ba)])
            bg = nbank()
            for k in range(8):
                mm(bank(bg), wg_fn(k), hT_rhs(k, tg), k == 0, k == 7, hT_keys(tg) + wkeys, [("ps", bg)])
            ti = nxt("tmp", 4)
            act(tmp[ti][:, 0:512], bank(bg), AF.Sigmoid, [("ps", bg)], [("tmp", ti)])
            tt(dst, bank(ba), tmp[ti][:, 0:512], ALU.mult, [("ps", ba), ("tmp", ti)], dkeys)

        def na_phase(u, l, is_sample):
            P.new_epoch("RA")
            P.new_epoch("RB")
            nab_src = nabs_d if is_sample else nabp_d
            for c in range(4):
                base = (c % 2) * 7232
                qT = ar(base, 2048)
                kT = ar(base + 2048, 2560)
                va = ar(base + 4608, 2600).rearrange("p (t h e) -> p t h e", t=20, h=2)
                otm = ar(14464 + (c % 2) * 2048, 2048).rearrange("p (i f) -> p i f", i=16)
                kq, kk, kv, ko = ("na", c % 2, "q"), ("na", c % 2, "k"), ("na", c % 2, "v"), ("na", c % 2, "o")
                s, wk = wslot()
                wv = wbuf[s][:, 0:3072].rearrange("p (k b n) -> p k b n", k=8, b=3)
                nb_w = 1 if is_sample else 3
                wsrc = w_in[l].rearrange("(kc p) (b n) -> p kc b n", p=128, n=512)[:, :, 0:nb_w, c * 128:(c + 1) * 128]
                load_w(wv[:, :, 0:nb_w, :], wsrc, wk)
                proj_fm(lambda tg: qT[:, tg * 512:(tg + 1) * 512], lambda k: wv[:, k, 0, :], 8, hT_rhs,
                        lambda tg: hT_keys(tg) + wk, lambda tg: [kq + (tg,)])
                if not is_sample:
                    proj_fm(lambda tg: kT[:, 256 + tg * 512:256 + (tg + 1) * 512], lambda k: wv[:, k, 1, :], 8, hT_rhs,
                            lambda tg: hT_keys(tg) + wk, lambda tg: [kk + (tg,)])
                    for t4 in range(4):
                        b = nbank()
                        for tl in range(4):
                            t = t4 * 4 + tl
                            for k in range(8):
                                mm(bank(b)[:, tl * 128:(tl + 1) * 128], hT[:, k, t * 128:(t + 1) * 128], wv[:, k, 2, :],
                                   k == 0, k == 7, [("hT", t)] + wk, [("ps", b)])
                        evac(va[:, 2 + t4 * 4:6 + t4 * 4, :, 0:64], bank(b).rearrange("p (t h e) -> p t h e", t=4, h=2),
                             [("ps", b)], [kv + (t4,)])
                    memset(va[:, :, :, 64:65], 1.0, [kv + ("ones",)])
                    kkeys_all = [kk + (tg,) for tg in range(4)]
                    vkeys_all = [kv + (t4,) for t4 in range(4)] + [kv + ("ones",)]
                else:
                    fk, fv = ("tfm", l, 4 + c), ("ttm", l, 4 + c)
                    dma_dyn(kT[:, 256:2304], lambda e, c=c: tfv(l, 4 + c)[rank_of(e, 0)][:, :], [fk], [kk + (0,)])
                    dma_dyn(kT[:, 0:256], lambda e, c=c: tfv(l, 4 + c)[rank_of(e, 3)][:, 1792:2048], [fk], [kk + (1,)])
                    dma_dyn(kT[:, 2304:2560], lambda e, c=c: tfv(l, 4 + c)[rank_of(e, 1)][:, 0:256], [fk], [kk + (2,)])
                    for hh_ in range(2):
                        hs = slice(hh_ * 64, (hh_ + 1) * 64)
                        dma_dyn(va[:, 2:18, hh_, 0:64], lambda e, c=c, hs=hs: ttv(l, 4 + c)[rank_of(e, 0)][:, :, hs], [fv], [kv + (0, hh_)])
                        dma_dyn(va[:, 0:2, hh_, 0:64], lambda e, c=c, hs=hs: ttv(l, 4 + c)[rank_of(e, 3)][:, 14:16, hs], [fv], [kv + (1, hh_)])
                        dma_dyn(va[:, 18:20, hh_, 0:64], lambda e, c=c, hs=hs: ttv(l, 4 + c)[rank_of(e, 1)][:, 0:2, hs], [fv], [kv + (2, hh_)])
                    memset(va[:, :, :, 64:65], 1.0, [kv + ("ones",)])
                    kkeys_all = [kk + (i,) for i in range(3)]
                    vkeys_all = [kv + (i, hh_) for i in range(3) for hh_ in range(2)] + [kv + ("ones",)]
                for hh in range(2):
                    h = 2 * c + hh
                    pb_ = hh * 64
                    nabv = biasbuf[:, 0:3456]

                    def load_nab(hd, grp):
                        lo_, hi_ = {"A": (0, 11), "B": (11, 16), "C": (16, 27)}[grp]
                        dma("sync", biasbuf[:, lo_ * 128:hi_ * 128].rearrange("p (s n) -> p s n", s=hi_ - lo_),
                            nab_src[l, hd][:, lo_:hi_, :], [], BK_NA[grp])

                    if h == 0:
                        load_nab(0, "A")
                        load_nab(0, "B")
                    load_nab(h, "C")
                    def na_S(i):
                        goff, jrels = NAB_GRP[_grp_of_block(i)]
                        slots = [(s_, jr) for s_, jr in enumerate(jrels) if is_sample or 0 <= i + jr <= 15]
                        ns = len(slots)
                        s0 = slots[0][0]
                        b = nbank2()
                        for n_, (s_, jr) in enumerate(slots):
                            co = 256 + (i + jr) * 128
                            mm(ps[:, b * 512 + n_ * 128: b * 512 + (n_ + 1) * 128], kT[pb_:pb_ + 64, co:co + 128],
                               qT[pb_:pb_ + 64, i * 128:(i + 1) * 128], True, True,
                               kkeys_all + [kq + (i // 4,)], [("ps", b + n_ // 4)])
                        pkeys = [("ps", b)] + ([("ps", b + 1)] if ns > 4 else [])
                        ti = nxt("tmp", 4)
                        stt(tmp[ti][:, 0:ns * 128], ps[:, b * 512:b * 512 + ns * 128], SCALE,
                            nabv[:, (goff + s0) * 128:(goff + s0 + ns) * 128], ALU.mult, ALU.add,
                            pkeys + BK_NA[{"b0": "A", "b1": "A", "int": "B", "b14": "C", "b15": "C"}[_grp_of_block(i)]], [("tmp", ti)])
                        pi = nxt("pT", 4)
                        act(pT[pi][:, 0:ns * 128], tmp[ti][:, 0:ns * 128], AF.Exp, [("tmp", ti)], [("pT", pi)])
                        if h < 7 and i == 2:
                            load_nab(h + 1, "A")
                        if h < 7 and i == 14:
                            load_nab(h + 1, "B")
                        return (i, slots, pi)

                    def na_PV(ctx):
                        i, slots, pi = ctx
                        ns = len(slots)
                        bo = nbank()
                        for n_, (s_, jr) in enumerate(slots):
                            mm(bank(bo, 65), pT[pi][:, n_ * 128:(n_ + 1) * 128], va[:, 2 + i + jr, hh, :], n_ == 0, n_ == ns - 1,
                               [("pT", pi)] + vkeys_all, [("ps", bo)])
                        rc, rk = col()
                        recip(rc, bank(bo)[:, 64:65], [("ps", bo)], [rk])
                        tsc(otm[:, i, hh * 64:(hh + 1) * 64], bank(bo)[:, 0:64], rc, None, ALU.mult, None,
                            [("ps", bo), rk], [ko + (i, hh)])

                    ctxs = {}
                    for j_ in range(-1, 16):
                        if j_ + 1 < 16:
                            ctxs[j_ + 1] = na_S(j_ + 1)
                        if j_ >= 0:
                            na_PV(ctxs.pop(j_))
                for i4 in range(4):
                    b = nbank()
                    pbv = bank(b).bitcast(BF16)
                    for il in range(4):
                        i = i4 * 4 + il
                        tr(pbv[:, il * 128:(il + 1) * 128], otm[:, i, :], [ko + (i, 0), ko + (i, 1)], [("ps", b)])
                    evac(brT[:, c * U + i4 * 512: c * U + (i4 + 1) * 512], pbv[:, 0:512], [("ps", b)], [("brT", c, i4)])

        def conv_phase(u, l, is_sample):
            P.new_epoch("RA")
            acc4 = arena[:, 0:16384].bitcast(F32).rearrange("p (c n) -> p c n", c=4)
            for c in range(4):
                g = ar(16384 + (c % 2) * 2080, 2078)
                gk = ("cv", "g", c % 2)
                if not is_sample:
                    s, wk = wslot()
                    wv = wbuf[s][:, 0:2048].rearrange("p (k b n) -> p k b n", k=8, b=2)
                    wsrc = w_in[l][:, OFF_CONV:OFF_CONV + 1024].rearrange("(kc p) (b n) -> p kc b n", p=128, n=512)[:, :, :, c * 128:(c + 1) * 128]
                    load_w(wv, wsrc, wk)
                    memset(g[:, 0:15], 0.0, [gk + ("h0",)])
                    memset(g[:, 2063:2078], 0.0, [gk + ("h1",)])
                    for tg in range(4):
                        glu_group(lambda k: wv[:, k, 0, :], lambda k: wv[:, k, 1, :], wk, tg,
                                  g[:, 15 + tg * 512: 15 + (tg + 1) * 512], [gk + (tg,)])
                    gkeys = [gk + (tg,) for tg in range(4)] + [gk + ("h0",), gk + ("h1",)]
                else:
                    fk = ("tfm", l, 8 + c)
                    dma_dyn(g[:, 15:2063], lambda e, c=c: tfv(l, 8 + c)[rank_of(e, 0)][:, :], [fk], [gk + (0,)])
                    dma_dyn(g[:, 0:15], lambda e, c=c: tfv(l, 8 + c)[rank_of(e, 3)][:, 2033:2048], [fk], [gk + ("h0",)])
                    dma_dyn(g[:, 2063:2078], lambda e, c=c: tfv(l, 8 + c)[rank_of(e, 1)][:, 0:15], [fk], [gk + ("h1",)])
                    tsc(g[:, 0:15], g[:, 0:15], cflag[:, 0:1], None, ALU.mult, None, [gk + ("h0",), ("cflag",)], [gk + ("h0",)])
                    tsc(g[:, 2063:2078], g[:, 2063:2078], cflag[:, 1:2], None, ALU.mult, None, [gk + ("h1",), ("cflag",)], [gk + ("h1",)])
                    gkeys = [gk + (0,), gk + ("h0",), gk + ("h1",)]
                ak = ("cv", "acc", c)
                wc = PC_CW + c * 31
                dg = biasbuf[:, :].bitcast(BF16)[:, (c % 2) * 3968:(c % 2 + 1) * 3968].rearrange("p (j n) -> p j n", j=31)
                dk = ("dg", c % 2)
                tt(dg, identf[:].unsqueeze(1).broadcast_to([128, 31, 128]),
                   pcol[:, wc:wc + 31].unsqueeze(2).broadcast_to([128, 31, 128]), ALU.mult,
                   [("identf",), ("pcol",)], [dk] + BK_DG[c % 2])
                for tg in range(4):
                    b = nbank()
                    for j in range(31):
                        mm(bank(b), dg[:, j, :], g[:, j + tg * 512: j + tg * 512 + 512], j == 0, j == 30, gkeys + [dk] + BK_DG[c % 2], [("ps", b)])
                    tsc(acc4[:, c, tg * 512:(tg + 1) * 512], bank(b), pcol[:, PC_CB + c:PC_CB + c + 1], None, ALU.add, None,
                        [("ps", b), ("pcol",)], [ak + (tg,)])
            for tg in range(4):
                akeys = [("cv", "acc", c, tg) for c in range(4)]
                sl = slice(tg * 512, (tg + 1) * 512)
                bm = nbank()
                for c in range(4):
                    mm(bank(bm), o512_f[:], acc4[:, c, sl], c == 0, c == 3, akeys + [("o512_f",)], [("ps", bm)])
                bs = nbank()
                for c in range(4):
                    ti = nxt("tmp", 4)
                    act(tmp[ti][:, 0:512], acc4[:, c, sl], AF.Square, akeys, [("tmp", ti)])
                    mm(bank(bs), o512_f[:], tmp[ti][:, 0:512], c == 0, c == 3, [("tmp", ti), ("o512_f",)], [("ps", bs)])
                tm = nxt("tmp", 4)
                P.add("scalar", lambda e, tm=tm, bm=bm: e.copy(out=tmp[tm][:, 0:512], in_=bank(bm)), [("ps", bm)], [("tmp", tm)])
                tv = nxt("tmp", 4)
                tt(tmp[tv][:, 0:512], tmp[tm][:, 0:512], tmp[tm][:, 0:512], ALU.mult, [("tmp", tm)], [("tmp", tv)])
                tt(tmp[tv][:, 0:512], bank(bs), tmp[tv][:, 0:512], ALU.subtract, [("ps", bs), ("tmp", tv)], [("tmp", tv)])
                act(tmp[tv][:, 0:512], tmp[tv][:, 0:512], AF.Ln, [("tmp", tv), ("epsc",)], [("tmp", tv)], bias=epsc[:, 0:1])
                act(tmp[tv][:, 0:512], tmp[tv][:, 0:512], AF.Exp, [("tmp", tv)], [("tmp", tv)], scale=-0.5)
                for c in range(4):
                    ty = nxt("tmp", 4)
                    while ty in (tm, tv):
                        ty = nxt("tmp", 4)
                    tt(tmp[ty][:, 0:512], acc4[:, c, sl], tmp[tm][:, 0:512], ALU.subtract, akeys + [("tmp", tm)], [("tmp", ty)])
                    tt(tmp[ty][:, 0:512], tmp[ty][:, 0:512], tmp[tv][:, 0:512], ALU.mult, [("tmp", ty), ("tmp", tv)], [("tmp", ty)])
                    act(brT[:, (4 + c) * U + tg * 512:(4 + c) * U + (tg + 1) * 512], tmp[ty][:, 0:512], AF.Silu,
                        [("tmp", ty), ("pcol",)], [("brT", 4 + c, tg)],
                        scale=pcol[:, PC_LNG + c:PC_LNG + c + 1], bias=pcol[:, PC_LNB + c:PC_LNB + c + 1])

        def diff_phase(u, l, is_sample):
            P.new_epoch("RA")
            rr["psn"] = 4
            rr["ps"] = 0
            nkc = 64 if is_sample else 16
            for h in range(4):
                if is_sample:
                    qT, kT = ar(0, 2048), ar(2048, 8192)
                    vh = ar(10240, 8192).rearrange("p (t e) -> p t e", t=64)
                    par = 0
                else:
                    par = h % 2
                    base = par * 6144
                    qT, kT = ar(base, 2048), ar(base + 2048, 2048)
                    vh = ar(base + 4096, 2048).rearrange("p (t e) -> p t e", t=16)
                kq, kk, kv = ("df", par, "q"), ("df", par, "k"), ("df", par, "v")
                s, wk = wslot()
                wv = wbuf[s][:, 0:3072].rearrange("p (k b n) -> p k b n", k=8, b=3)
                nb_w = 1 if is_sample else 3
                wsrc = w_in[l][:, OFF_DIFF:OFF_DIFF + 1536].rearrange("(kc p) (b n) -> p kc b n", p=128, n=512)[:, :, 0:nb_w, h * 128:(h + 1) * 128]
                load_w(wv[:, :, 0:nb_w, :], wsrc, wk)
                proj_fm(lambda tg: qT[:, tg * 512:(tg + 1) * 512], lambda k: wv[:, k, 0, :], 8, hT_rhs,
                        lambda tg: hT_keys(tg) + wk, lambda tg: [kq + (tg,)])
                if not is_sample:
                    proj_fm(lambda tg: kT[:, tg * 512:(tg + 1) * 512], lambda k: wv[:, k, 1, :], 8, hT_rhs,
                            lambda tg: hT_keys(tg) + wk, lambda tg: [kk + (tg,)])
                    for t4 in range(4):
                        b = nbank()
                        for tl in range(4):
                            t = t4 * 4 + tl
                            for k in range(8):
                                mm(bank(b)[:, tl * 128:(tl + 1) * 128], hT[:, k, t * 128:(t + 1) * 128], wv[:, k, 2, :],
                                   k == 0, k == 7, [("hT", t)] + wk, [("ps", b)])
                        evac(vh[:, t4 * 4:t4 * 4 + 4, :], bank(b).rearrange("p (t e) -> p t e", t=4), [("ps", b)], [kv + (t4,)])
                    kkeys_all = [kk + (tg,) for tg in range(4)]
                    vkeys_all = [kv + (t4,) for t4 in range(4)]
                else:
                    for ro in range(4):
                        dma_dyn(kT[:, ro * 2048:(ro + 1) * 2048], lambda e, ro=ro, h=h: tfv(l, h)[rank_of(e, ro)][:, :],
                                [("tfm", l, h)], [kk + (ro,)])
                        dma_dyn(vh[:, ro * 16:(ro + 1) * 16, :], lambda e, ro=ro, h=h: ttv(l, h)[rank_of(e, ro)],
                                [("ttm", l, h)], [kv + (ro,)])
                    kkeys_all = [kk + (ro,) for ro in range(4)]
                    vkeys_all = [kv + (ro,) for ro in range(4)]
                dma("sync", biasbuf[:, 0:3072].rearrange("p (m n) -> p m n", m=6), t5t_d[h], [], BKALL)
                if is_sample:
                    dma("sync", biasbuf[:, 3072:4096].rearrange("p (m n) -> p m n", m=2), t5j_d[h], [], BKALL)
                def df_class(qg, kc):
                    ro, kcl = kc // 16, kc % 16
                    special = None
                    cb = None
                    cbk = None
                    if ro == 0:
                        d = kcl * 128 - qg * 512
                        if -128 <= d <= 512:
                            special = (d + 128) // 128
                        else:
                            cb = t5c[:, h, 0:1] if d < 0 else t5c[:, h, 1:2]
                            cbk = ("t5c",)
                    elif ro == 1 and kcl == 0 and qg == 3:
                        special = 6
                    elif ro == 3 and kcl == 15 and qg == 0:
                        special = 7
                    else:
                        cb = t5cs[:, h, ro:ro + 1]
                        cbk = ("t5cs",)
                    return special, cb, cbk

                def df_S(qg, kc, c):
                    special, cb, cbk = df_class(qg, kc)
                    b = nxt("dfs", 4)
                    mm(bank(b), kT[c * 64:(c + 1) * 64, kc * 128:(kc + 1) * 128], qT[c * 64:(c + 1) * 64, qg * 512:(qg + 1) * 512],
                       True, True, kkeys_all + [kq + (qg,)], [("ps", b)])
                    pi = nxt("pT", 4)
                    if special is not None:
                        ti = nxt("tmp", 4)
                        stt(tmp[ti][:, 0:512], bank(b), SCALE, biasbuf[:, special * 512:(special + 1) * 512], ALU.mult, ALU.add,
                            [("ps", b)] + BKALL, [("tmp", ti)])
                        act(pT[pi][:, 0:512], tmp[ti][:, 0:512], AF.Exp, [("tmp", ti)], [("pT", pi)])
                    else:
                        act(pT[pi][:, 0:512], bank(b), AF.Exp, [("ps", b), cbk], [("pT", pi)], scale=SCALE, bias=cb)
                    return pi

                def df_PV(qg, kc, c, pi):
                    mm(bank(4 + c), vh[:, kc, :], pT[pi][:, 0:512], kc == 0, kc == nkc - 1, vkeys_all + [("pT", pi)], [("ps", 4 + c)])
                    mm(bank(6 + c), ones_bf[:], pT[pi][:, 0:512], kc == 0, kc == nkc - 1, [("ones_bf",), ("pT", pi)], [("ps", 6 + c)])

                def df_post(qg):
                    tA, tB, tC, tD = 0, 1, 2, 3
                    cpy(tmp[tA][:, 0:512], bank(4), [("ps", 4)], [("tmp", tA)])
                    P.add("scalar", lambda e: e.copy(out=tmp[tC][:, 0:512], in_=bank(6)), [("ps", 6)], [("tmp", tC)])
                    cpy(tmp[tB][:, 0:512], bank(5), [("ps", 5)], [("tmp", tB)])
                    P.add("scalar", lambda e: e.copy(out=tmp[tD][:, 0:512], in_=bank(7)), [("ps", 7)], [("tmp", tD)])
                    A_, B_, C_, D_ = tmp[tA][:, 0:512], tmp[tB][:, 0:512], tmp[tC][:, 0:512], tmp[tD][:, 0:512]
                    kA, kB, kC, kD = ("tmp", tA), ("tmp", tB), ("tmp", tC), ("tmp", tD)
                    dst = brT[:, (8 + h) * U + qg * 512:(8 + h) * U + (qg + 1) * 512]
                    bb = [None]

                    def s1():
                        act(C_, C_, AF.Ln, [kC], [kC])
                        act(D_, D_, AF.Ln, [kD], [kD])

                    def s2():
                        act(C_, C_, AF.Exp, [kC], [kC], scale=-1.0)
                        act(D_, D_, AF.Exp, [kD], [kD], scale=-1.0)

                    def s3():
                        tt(A_, A_, C_, ALU.mult, [kA, kC], [kA])
                        tt(B_, B_, D_, ALU.mult, [kB, kD], [kB])

                    def s4():
                        stt(A_, B_, small[:, 2:3], A_, ALU.mult, ALU.add, [kA, kB, ("lamc",)], [kA])

                    def s5():
                        act(C_, A_, AF.Square, [kA], [kC])

                    def s6():
                        bb[0] = nxt("dfs", 4)
                        mm(bank(bb[0]), ones_f[:], C_, True, True, [kC, ("ones_f",)], [("ps", bb[0])])

                    def s7():
                        act(C_, bank(bb[0]), AF.Ln, [("ps", bb[0]), ("epsc",)], [kC], bias=epsc[:, 0:1])

                    def s8():
                        act(C_, C_, AF.Exp, [kC], [kC], scale=-0.5)

                    def s9():
                        stt(dst, A_, small[:, 3:4], C_, ALU.mult, ALU.mult, [kA, kC, ("lamc",)], [("brT", 8 + h, qg)])

                    return [s1, s2, s3, s4, s5, s6, s7, s8, s9]

                steps = []
                for qg in range(4):
                    far = [kc for kc in range(nkc) if df_class(qg, kc)[0] is None]
                    spc = [kc for kc in range(nkc) if df_class(qg, kc)[0] is not None]
                    order = far + spc
                    for pos, kc in enumerate(order):
                        steps.append((qg, kc, pos == 0, pos == nkc - 1))
                DL = 1
                pend = {}
                post_q = []
                for j_ in range(-DL, len(steps)):
                    if j_ + DL < len(steps):
                        qg2, kc2, _, _ = steps[j_ + DL]
                        pend[j_ + DL] = (df_S(qg2, kc2, 0), df_S(qg2, kc2, 1))
                    if j_ >= 0:
                        qg, kc, first, last = steps[j_]
                        p0, p1 = pend.pop(j_)
                        mm(bank(4), vh[:, kc, :], pT[p0][:, 0:512], first, last, vkeys_all + [("pT", p0)], [("ps", 4)])
                        mm(bank(5), vh[:, kc, :], pT[p1][:, 0:512], first, last, vkeys_all + [("pT", p1)], [("ps", 5)])
                        mm(bank(6), ones_bf[:], pT[p0][:, 0:512], first, last, [("ones_bf",), ("pT", p0)], [("ps", 6)])
                        mm(bank(7), ones_bf[:], pT[p1][:, 0:512], first, last, [("ones_bf",), ("pT", p1)], [("ps", 7)])
                        if post_q:
                            post_q.pop(0)()
                        if last:
                            while post_q:
                                post_q.pop(0)()
                            post_q = df_post(qg)
                while post_q:
                    post_q.pop(0)()
            rr["ps"] = 0
            rr["psn"] = 8

        def merge_phase(u, l, xsrc, xdst_mid):
            P.new_epoch("RA")
            mT = arena[:, 0:16384].rearrange("p (k n) -> p k n", k=8)
            for dc in range(8):
                s, wk = wslot()
                wg = wbuf[s][:, 0:3072].rearrange("p (k b n) -> p k b n", k=8, b=3)
                wbr = wbuf[s][:, 3072:4608].rearrange("p (b k n) -> p b k n", b=3, k=4)
                load_w(wg, w_in[l][:, OFF_GATE:].rearrange("(kc p) (b n) -> p kc b n", p=128, n=1024)[:, :, :, dc * 128:(dc + 1) * 128], wk)
                load_w(wbr, w_branch[l].rearrange("b (k p) n -> p b k n", p=128)[:, :, :, dc * 128:(dc + 1) * 128], wk)
                for tg in range(4):
                    tacc = nxt("tmp", 4)
                    for b_ in range(3):
                        bg = nbank()
                        for k in range(8):
                            mm(bank(bg), wg[:, k, b_, :], hT_rhs(k, tg), k == 0, k == 7, hT_keys(tg) + wk, [("ps", bg)])
                        bp = nbank()
                        for k in range(4):
                            mm(bank(bp), wbr[:, b_, k, :], brT[:, (b_ * 4 + k) * U + tg * 512:(b_ * 4 + k) * U + (tg + 1) * 512],
                               k == 0, k == 3, [("brT", b_ * 4 + k, tg)] + wk, [("ps", bp)])
                        tg_ = nxt("tmp", 4)
                        while tg_ == tacc:
                            tg_ = nxt("tmp", 4)
                        bc = PC_BGATE + b_ * 8 + dc
                        act(tmp[tg_][:, 0:512], bank(bg), AF.Sigmoid, [("ps", bg), ("pcol",)], [("tmp", tg_)], bias=pcol[:, bc:bc + 1])
                        if b_ == 0:
                            tt(tmp[tacc][:, 0:512], bank(bp), tmp[tg_][:, 0:512], ALU.mult, [("ps", bp), ("tmp", tg_)], [("tmp", tacc)])
                        else:
                            tt(tmp[tg_][:, 0:512], bank(bp), tmp[tg_][:, 0:512], ALU.mult, [("ps", bp), ("tmp", tg_)], [("tmp", tg_)])
                            if b_ == 1:
                                tt(tmp[tacc][:, 0:512], tmp[tacc][:, 0:512], tmp[tg_][:, 0:512], ALU.add,
                                   [("tmp", tacc), ("tmp", tg_)], [("tmp", tacc)])
                            else:
                                tt(mT[:, dc, tg * 512:(tg + 1) * 512], tmp[tacc][:, 0:512], tmp[tg_][:, 0:512], ALU.add,
                                   [("tmp", tacc), ("tmp", tg_)], [("mT", dc, tg)])
            wo = biasbuf[:, :].bitcast(BF16).rearrange("p (k n) -> p k n", k=8)
            load_w(wo, w_out[l].rearrange("(k p) n -> p k n", p=128), BKALL)
            LA = 2
            held = {}
            for t in range(NT + LA):
                if t < NT:
                    xi = nxt("x", 3)
                    dma("sync", xt[xi][:], xsrc[u, t * 128:(t + 1) * 128, :], [("xd", l, u, t)], [("xt", xi)])
                    b = nbank2()
                    for nh in range(2):
                        for k in range(8):
                            mm(bank(b + nh), mT[:, k, t * 128:(t + 1) * 128], wo[:, k, nh * 512:(nh + 1) * 512], k == 0, k == 7,
                               [("mT", k, t // 4)] + BKALL, [("ps", b + nh)])
                if t - LA >= 0:
                    xj = held.pop(t - LA)
                    prenorm_tile(xt[xj][:], ("xt", xj), t - LA, PC_GFFN)
                if t < NT:
                    resid_tail(ps[:, b * 512:(b + 2) * 512], [("ps", b), ("ps", b + 1)], xi, 0, l)
                    dma("sync", xdst_mid[u, t * 128:(t + 1) * 128, :], xt[xi][:], [("xt", xi)], [("xm", l, u, t)])
                    held[t] = xi

        def resid_tail(y_ps, ykeys, xi, which, l):
            ss, sk = col()
            tj = nxt("tmp", 4)
            act(tmp[tj][:, 0:512].bitcast(BF16), y_ps, AF.Square, ykeys, [sk, ("tmp", tj)], accum_out=ss)
            rs, rk = col()
            rsqrt_col(rs, rk, ss, sk, 1.0 / D)
            stt(ytmp[:], y_ps, rs, gpost[:, which, :], ALU.mult, ALU.mult, ykeys + [rk, ("gpost",)], [("ytmp",)])
            tt(xt[xi][:], xt[xi][:], ytmp[:], ALU.add, [("xt", xi), ("ytmp",)], [("xt", xi)])

        def ffn_phase(u, l, xsrc_mid, xdst, next_a=None):
            P.new_epoch("RB")
            wfo = brT[:, 0:22528].rearrange("p (k n) -> p k n", k=NHC)
            for k0 in range(0, NHC, 4):
                k1 = min(NHC, k0 + 4)
                load_w(wfo[:, k0:k1, :], w_ffn_out[l].rearrange("(k p) n -> p k n", p=128)[:, k0:k1, :], [("wfo", k0)])
            wfo_keys = [("wfo", k0) for k0 in range(0, NHC, 4)] + [("wfo",)]
            aT = arena[:, 0:22528].rearrange("p (k n) -> p k n", k=NHC)
            for half in range(2):
                if half == 0:
                    P.new_epoch("RA")
                for hc in range(NHC):
                    s, wk = wslot()
                    wv = wbuf[s][:, 0:2048].rearrange("p (k b n) -> p k b n", k=8, b=2)
                    load_w(wv, w_ffn_in[l].rearrange("(kc p) (b n) -> p kc b n", p=128, n=FFN_H)[:, :, :, hc * 128:(hc + 1) * 128], wk)
                    for t2 in range(2):
                        tg = half * 2 + t2
                        bg = nbank()
                        for k in range(8):
                            mm(bank(bg), wv[:, k, 0, :], hT_rhs(k, tg), k == 0, k == 7, hT_keys(tg) + wk, [("ps", bg)])
                        bu = nbank()
                        for k in range(8):
                            mm(bank(bu), wv[:, k, 1, :], hT_rhs(k, tg), k == 0, k == 7, hT_keys(tg) + wk, [("ps", bu)])
                        ti = nxt("tmp", 4)
                        act(tmp[ti][:, 0:512], bank(bg), AF.Silu, [("ps", bg)], [("tmp", ti)])
                        tt(aT[:, hc, t2 * 512:(t2 + 1) * 512], bank(bu), tmp[ti][:, 0:512], ALU.mult,
                           [("ps", bu), ("tmp", ti)], [("aT", hc, t2)])
                if half == 1 and next_a is not None:
                    next_a()
                for tl in range(8):
                    t = half * 8 + tl
                    xi = nxt("x", 3)
                    dma("sync", xt[xi][:], xsrc_mid[u, t * 128:(t + 1) * 128, :], [("xm", l, u, t)], [("xt", xi)])
                    b = nbank2()
                    for nh in range(2):
                        for hc in range(NHC):
                            mm(bank(b + nh), aT[:, hc, tl * 128:(tl + 1) * 128], wfo[:, hc, nh * 512:(nh + 1) * 512],
                               hc == 0, hc == NHC - 1, [("aT", hc, tl // 4)] + wfo_keys, [("ps", b + nh)])
                    resid_tail(ps[:, b * 512:(b + 2) * 512], [("ps", b), ("ps", b + 1)], xi, 1, l)
                    dma("sync", xdst[u, t * 128:(t + 1) * 128, :], xt[xi][:], [("xt", xi)], [("xd", l + 1, u, t)])

        def layer_setup(l):
            lam_init = 0.8 - 0.6 * math.exp(-0.3 * l)
            dma("sync", pcol[:], pcol_d[l], [], [("pcol",)])
            dma("sync", gpost[:, 0, :], prow_d[l, 0:1, :].broadcast_to([128, D]), [], [("gpost",)])
            dma("sync", gpost[:, 1, :], prow_d[l, 1:2, :].broadcast_to([128, D]), [], [("gpost",)])
            dma("sync", lamb, lam_d[l:l + 1, :].broadcast_to([128, 256]), [], [("ytmp",)])
            P.add("vector", lambda e: e.tensor_tensor(out=lamb[:, 0:64], in0=lamb[:, 0:64], in1=lamb[:, 64:128], op=ALU.mult), [("ytmp",)], [("ytmp",)])
            P.add("vector", lambda e: e.tensor_tensor(out=lamb[:, 128:192], in0=lamb[:, 128:192], in1=lamb[:, 192:256], op=ALU.mult), [("ytmp",)], [("ytmp",)])
            P.add("vector", lambda e: e.reduce_sum(out=small[:, 0:1], in_=lamb[:, 0:64], axis=mybir.AxisListType.X), [("ytmp",)], [("lamc",)])
            P.add("vector", lambda e: e.reduce_sum(out=small[:, 1:2], in_=lamb[:, 128:192], axis=mybir.AxisListType.X), [("ytmp",), ("lamc",)], [("lamc",)])
            act(small[:, 0:2], small[:, 0:2], AF.Exp, [("lamc",)], [("lamc",)])
            stt(small[:, 2:3], small[:, 0:1], -1.0, small[:, 1:2], ALU.mult, ALU.add, [("lamc",)], [("lamc",)])
            tsc(small[:, 2:3], small[:, 2:3], -lam_init, None, ALU.add, None, [("lamc",)], [("lamc",)])
            tsc(small[:, 3:4], pcol[:, PC_SUB:PC_SUB + 1], 1.0 - lam_init, None, ALU.mult, None, [("pcol",), ("lamc",)], [("lamc",)])

        for l in range(depth):
            xsrc = xin if l == 0 else x1
            xdst = yout if l == depth - 1 else x1
            layer_setup(l)
            if with_sample:
                P.mark(f"L{l} PRE")
                sample_prepass(xsrc, l)
            units = list(range(1, NU)) + ([0] if with_sample else [])
            for ui, u in enumerate(units):
                smp = (u == 0)
                if ui == 0:
                    P.mark(f"L{l}U{u} A")
                    phase_a(xsrc, u, l)
                P.mark(f"L{l}U{u} NA")
                na_phase(u, l, smp)
                P.mark(f"L{l}U{u} CV")
                conv_phase(u, l, smp)
                P.mark(f"L{l}U{u} DF")
                diff_phase(u, l, smp)
                P.mark(f"L{l}U{u} MG")
                merge_phase(u, l, xsrc, xmid)
                P.mark(f"L{l}U{u} FF")
                nxt_a = None
                if ui + 1 < len(units):
                    nu_ = units[ui + 1]
                    nxt_a = (lambda nu_=nu_: phase_a(xsrc, nu_, l))
                ffn_phase(u, l, xmid, xdst, nxt_a)
        P.mark("END")

        P.finalize()
        P.alloc_sems(nc, es)
        block = es.enter_context(nc.Block())

        @block.sync
        def _(e):
            P.emit("sync", e, final_wait=True)

        @block.gpsimd
        def _(e):
            P.emit("gpsimd", e)

        @block.tensor
        def _(e):
            P.emit("tensor", e)

        @block.vector
        def _(e):
            P.emit("vector", e)

        @block.scalar
        def _(e):
            P.emit("scalar", e)

    return nc, P


def prepare_inputs(inp, n_prompt=4, cores=8):
    t5 = np.asarray(inp["t5_bias"], np.float32)
    p = np.arange(128)[:, None]
    j = np.arange(512)[None, :]
    t5t = np.empty((4, 128, 6, 512), np.float32)
    for m in range(6):
        bk = _t5_bucket(((m - 1) * 128 + p - j).astype(np.int32))
        for h in range(4):
            t5t[h, :, m, :] = t5[bk, h]
    t5c = np.empty((128, 4, 2), np.float32)
    t5c[:, :, 0] = t5[15][None, :]
    t5c[:, :, 1] = t5[31][None, :]
    rpb = np.asarray(inp["na_rpb"], np.float32)
    nabp = _nab_table(rpb, 32, 0)
    shared = {
        "w_in": np.ascontiguousarray(inp["w_in"], np.float32),
        "w_branch": np.ascontiguousarray(inp["w_branch"], np.float32),
        "w_out": np.ascontiguousarray(inp["w_out"], np.float32),
        "w_ffn_in": np.ascontiguousarray(inp["w_ffn_in"], np.float32),
        "w_ffn_out": np.ascontiguousarray(inp["w_ffn_out"], np.float32),
        "pcol": np.stack([_pack_pcol(inp, l) for l in range(DEPTH)]),
        "prow": np.stack([np.stack([inp["ln_mix_post"][l], inp["ln_ffn_post"][l]]) for l in range(DEPTH)]).astype(np.float32),
        "lam": np.asarray(inp["diff_lambda"], np.float32).reshape(DEPTH, 256),
        "t5t": t5t, "t5c": t5c, "nabp": nabp,
        "ident": np.eye(128, dtype=np.float32),
    }
    nabs_q = [_nab_table(rpb, 128, 16 * q) for q in range(4)]
    in_maps = []
    xp = np.asarray(inp["x_prompt"], np.float32)
    xs = np.asarray(inp["x_sample"], np.float32)
    for c in range(cores):
        q, s = c % 4, c // 4
        m = dict(shared)
        xin = np.empty((1 + n_prompt, U, D), np.float32)
        xin[0] = xs[s, q * U:(q + 1) * U]
        for i in range(n_prompt):
            xin[1 + i] = xp[c * n_prompt + i]
        m["xin"] = xin
        t5cs = np.empty((128, 4, 4), np.float32)
        for ro in range(4):
            r = (q + ro) % 4
            t5cs[:, :, ro] = (t5[31] if r > q else t5[15])[None, :]
        m["t5cs"] = t5cs
        t5j = np.empty((4, 128, 2, 512), np.float32)
        r1 = ((q + 1) % 4 - q) * U + 0 * 128 + p - (3 * 512 + j)
        r3 = ((q + 3) % 4 - q) * U + 15 * 128 + p - j
        b1, b3 = _t5_bucket(r1.astype(np.int32)), _t5_bucket(r3.astype(np.int32))
        for h in range(4):
            t5j[h, :, 0, :] = t5[b1, h]
            t5j[h, :, 1, :] = t5[b3, h]
        m["t5j"] = t5j
        m["nabs"] = nabs_q[q]
        cf = np.zeros((128, 2), np.float32)
        cf[:, 0] = 1.0 if q > 0 else 0.0
        cf[:, 1] = 1.0 if q < 3 else 0.0
        m["cflag"] = cf
        in_maps.append(m)
    return in_maps


_CACHE = {}


def kernel(**inputs):
    inp = {k: np.asarray(v) for k, v in inputs.items()}
    if "nc" not in _CACHE:
        _CACHE["nc"] = build_program()[0]
    nc = _CACHE["nc"]
    in_maps = prepare_inputs(inp)
    res = run_bass_kernel_spmd(nc, in_maps, core_ids=list(range(8)))
    y_prompt = np.empty((32, U, D), np.float32)
    y_sample = np.empty((2, 8192, D), np.float32)
    for c in range(8):
        yo = np.asarray(res.results[c]["yout"], np.float32)
        q, s = c % 4, c // 4
        y_sample[s, q * U:(q + 1) * U] = yo[0]
        for i in range(4):
            y_prompt[c * 4 + i] = yo[1 + i]
    return (y_prompt, y_sample)
```
